# Optimizing a Trainium2 kernel written in Bass

```python
import jax, jax.numpy as jnp
from jax import lax
import numpy as np


D_MODEL = 1024
BATCH = 32
SEQ = 256
DEPTH = 1
DEC_BATCH = 4
DEC_SEQ = 2048
PAST_LEN = 512

GRID_W = 64
BLOCK = 128
WINDOW = 128
A_HEADS = 8
A_KV_HEADS = 2
A_GROUP = A_HEADS // A_KV_HEADS
A_HEAD_DIM = 64
A_Q_W = A_HEADS * A_HEAD_DIM
A_KV_W = A_KV_HEADS * A_HEAD_DIM
B_HEADS = 8
B_NOPE_DIM = 64
B_ROPE_DIM = 32
B_V_DIM = 64
B_QK_DIM = B_NOPE_DIM + B_ROPE_DIM
B_OUT_W = B_HEADS * B_V_DIM
Q_LORA = 256
KV_LORA = 256
D_FF = 2816
N_MOD = 9
ROPE_THETA = 10000.0
EPS = 1e-6
NEG = -1e30
A_SCALE = A_HEAD_DIM ** -0.5
B_SCALE = B_QK_DIM ** -0.5
IN_SIZES = (A_Q_W, A_KV_W, A_KV_W, Q_LORA, KV_LORA, B_ROPE_DIM, D_MODEL, D_MODEL)
IN_WIDTH = A_Q_W + 2 * A_KV_W + Q_LORA + KV_LORA + B_ROPE_DIM + 2 * D_MODEL

kernel_name = 'hybrid_dit_prefix_windowed_gqa_mla_macaron'


def rms_norm(x, g):
    xf = x.astype(jnp.float32)
    y = xf * lax.rsqrt(jnp.mean(xf * xf, axis=-1, keepdims=True) + EPS)
    return (y * g.astype(jnp.float32)).astype(x.dtype)


def modulate(h, shift, scale):
    return h * (1 + scale[:, None, :]) + shift[:, None, :]


def swiglu(h, w_gu, w_down):
    a, u = jnp.split(h @ w_gu, 2, axis=-1)
    return (jax.nn.silu(a) * u) @ w_down


def ada_mods(cvec, ada_w, ada_b):
    m = jax.nn.silu(cvec) @ ada_w + ada_b
    return [m[:, i * D_MODEL:(i + 1) * D_MODEL] for i in range(N_MOD)]


def split_in(z):
    offs = []
    acc = 0
    for s in IN_SIZES[:-1]:
        acc += s
        offs.append(acc)
    return jnp.split(z, offs, axis=-1)


def axial_rope(n, d_rot):
    rows = n // GRID_W
    t_row = jnp.repeat(jnp.arange(rows, dtype=jnp.float32), GRID_W)
    t_col = jnp.tile(jnp.arange(GRID_W, dtype=jnp.float32), rows)
    d_half = d_rot // 2
    inv = 1.0 / (ROPE_THETA ** (jnp.arange(0, d_half, 2, dtype=jnp.float32) / d_half))
    ar = t_row[:, None] * inv[None, :]
    ac = t_col[:, None] * inv[None, :]
    ang = jnp.concatenate([ar, ar, ac, ac], axis=-1)
    return jnp.cos(ang), jnp.sin(ang)


def apply_rope(x, cos, sin):
    shp = (x.shape[1],) + (1,) * (x.ndim - 3) + (x.shape[-1],)
    c = cos.reshape(shp)
    s = sin.reshape(shp)
    xf = x.astype(jnp.float32)
    x1, x2, x3, x4 = jnp.split(xf, 4, axis=-1)
    rot = jnp.concatenate([-x2, x1, -x4, x3], axis=-1)
    return (xf * c + rot * s).astype(x.dtype)


def block_sweep_attention(q, k, v, scale, sink):
    bq, n = q.shape[0], q.shape[1]
    nb = n // BLOCK
    qb = q.astype(jnp.float32).reshape((bq, nb, BLOCK) + q.shape[2:]).swapaxes(0, 1)
    kf = k.astype(jnp.float32)
    vf = v.astype(jnp.float32)

    def one_block(qblk):
        s = jnp.einsum('bqkgd,bskd->bkgqs', qblk, kf) * scale
        if sink is not None:
            col = jnp.broadcast_to(sink.astype(jnp.float32)[None, :, :, None, None], s.shape[:-1] + (1,))
            s = jnp.concatenate([s, col], axis=-1)
        p = jax.nn.softmax(s, axis=-1)
        if sink is not None:
            p = p[..., :-1]
        return jnp.einsum('bkgqs,bskd->bqkgd', p, vf)

    o = lax.map(one_block, qb)
    return o.swapaxes(0, 1).reshape(bq, n, -1).astype(q.dtype)


def windowed_attention(q, k, v, k_ctx, v_ctx, sink, scale):
    b, n = q.shape[0], q.shape[1]
    nb = n // BLOCK
    pad = ((0, 0), (BLOCK, BLOCK), (0, 0), (0, 0))
    kp = jnp.pad(k.astype(jnp.float32), pad).reshape(b, nb + 2, BLOCK, A_KV_HEADS, -1)
    vp = jnp.pad(v.astype(jnp.float32), pad).reshape(b, nb + 2, BLOCK, A_KV_HEADS, -1)
    kband = jnp.concatenate([kp[:, :-2], kp[:, 1:-1], kp[:, 2:]], axis=2)
    vband = jnp.concatenate([vp[:, :-2], vp[:, 1:-1], vp[:, 2:]], axis=2)
    qb = q.astype(jnp.float32).reshape(b, nb, BLOCK, A_KV_HEADS, A_GROUP, -1)
    s_band = jnp.einsum('bnqkgd,bnskd->bnkgqs', qb, kband) * scale
    qi = jnp.arange(BLOCK)
    kj = jnp.arange(3 * BLOCK)
    blk = jnp.arange(nb)
    rel = kj[None, :] - BLOCK - qi[:, None]
    kpos = blk[:, None] * BLOCK - BLOCK + kj[None, :]
    valid = (jnp.abs(rel) <= WINDOW)[None] & ((kpos >= 0) & (kpos < n))[:, None, :]
    s_band = jnp.where(valid[None, :, None, None], s_band, NEG)
    kc = k_ctx.astype(jnp.float32)
    vc = v_ctx.astype(jnp.float32)
    s_ctx = jnp.einsum('bnqkgd,bckd->bnkgqc', qb, kc) * scale
    col = jnp.broadcast_to(sink.astype(jnp.float32)[None, None, :, :, None, None], s_band.shape[:-1] + (1,))
    p = jax.nn.softmax(jnp.concatenate([s_band, s_ctx, col], axis=-1), axis=-1)
    lb = 3 * BLOCK
    lc = k_ctx.shape[1]
    o = (jnp.einsum('bnkgqs,bnskd->bnqkgd', p[..., :lb], vband)
         + jnp.einsum('bnkgqc,bckd->bnqkgd', p[..., lb:lb + lc], vc))
    return o.reshape(b, n, -1).astype(q.dtype)


def mla_queries(q_lat, q_lat_norm, w_uq):
    b, n = q_lat.shape[0], q_lat.shape[1]
    return (rms_norm(q_lat, q_lat_norm) @ w_uq).reshape(b, n, B_HEADS, B_QK_DIM)


def mla_keys(ckv_n, krope, w_ukv):
    b, l = ckv_n.shape[0], ckv_n.shape[1]
    kv = (ckv_n @ w_ukv).reshape(b, l, B_HEADS, B_NOPE_DIM + B_V_DIM)
    k_nope, v = kv[..., :B_NOPE_DIM], kv[..., B_NOPE_DIM:]
    k = jnp.concatenate([k_nope, jnp.broadcast_to(krope[:, :, None, :], (b, l, B_HEADS, B_ROPE_DIM))], axis=-1)
    return k, v


def merge_branches(o_a, o_b, gate_a, gate_b, w_o_a, w_o_b, w_out):
    m = jax.nn.sigmoid(gate_a) * (o_a @ w_o_a) + jax.nn.sigmoid(gate_b) * (o_b @ w_o_b)
    return m @ w_out


def setup_inputs(seed: int = 0) -> dict:
    key = jax.random.key(seed)
    ks = jax.random.split(key, 32)

    def nrm(k, shape, scale=1.0):
        return jax.random.normal(k, shape, jnp.float32) * scale

    def gain(k, shape):
        return 1.0 + nrm(k, shape, 0.05)

    return {
        'x_prompt': nrm(ks[0], (BATCH, SEQ, D_MODEL)),
        'x_sample': nrm(ks[1], (DEC_BATCH, DEC_SEQ, D_MODEL)),
        'cache_attn_k': nrm(ks[2], (DEC_BATCH, DEPTH, PAST_LEN, A_KV_HEADS, A_HEAD_DIM)),
        'cache_attn_v': nrm(ks[3], (DEC_BATCH, DEPTH, PAST_LEN, A_KV_HEADS, A_HEAD_DIM)),
        'cache_mla_ckv': nrm(ks[4], (DEC_BATCH, DEPTH, PAST_LEN, KV_LORA)),
        'cache_mla_krope': nrm(ks[5], (DEC_BATCH, DEPTH, PAST_LEN, B_ROPE_DIM)),
        'c': nrm(ks[6], (DEC_BATCH, D_MODEL)),
        'c_ctx': nrm(ks[7], (D_MODEL,)),
        'ada_w': nrm(ks[8], (DEPTH, D_MODEL, N_MOD * D_MODEL), 0.5 * D_MODEL ** -0.5),
        'ada_b': nrm(ks[9], (DEPTH, N_MOD * D_MODEL), 0.01),
        'ffn1_norm': gain(ks[10], (DEPTH, D_MODEL)),
        'ffn1_w_gu': nrm(ks[11], (DEPTH, D_MODEL, 2 * D_FF), D_MODEL ** -0.5),
        'ffn1_w_down': nrm(ks[12], (DEPTH, D_FF, D_MODEL), D_FF ** -0.5),
        'mix_norm': gain(ks[13], (DEPTH, D_MODEL)),
        'w_in': nrm(ks[14], (DEPTH, D_MODEL, IN_WIDTH), D_MODEL ** -0.5),
        'attn_sink': nrm(ks[15], (DEPTH, A_HEADS), 0.5),
        'q_lat_norm': gain(ks[16], (DEPTH, Q_LORA)),
        'kv_lat_norm': gain(ks[17], (DEPTH, KV_LORA)),
        'w_uq': nrm(ks[18], (DEPTH, Q_LORA, B_HEADS * B_QK_DIM), Q_LORA ** -0.5),
        'w_ukv': nrm(ks[19], (DEPTH, KV_LORA, B_HEADS * (B_NOPE_DIM + B_V_DIM)), KV_LORA ** -0.5),
        'w_o_a': nrm(ks[20], (DEPTH, A_Q_W, D_MODEL), A_Q_W ** -0.5),
        'w_o_b': nrm(ks[21], (DEPTH, B_OUT_W, D_MODEL), B_OUT_W ** -0.5),
        'w_out': nrm(ks[22], (DEPTH, D_MODEL, D_MODEL), D_MODEL ** -0.5),
        'ffn2_norm': gain(ks[23], (DEPTH, D_MODEL)),
        'ffn2_w_gu': nrm(ks[24], (DEPTH, D_MODEL, 2 * D_FF), D_MODEL ** -0.5),
        'ffn2_w_down': nrm(ks[25], (DEPTH, D_FF, D_MODEL), D_FF ** -0.5),
        'final_norm': gain(ks[26], (D_MODEL,)),
    }


def reference(x_prompt, x_sample, cache_attn_k, cache_attn_v, cache_mla_ckv, cache_mla_krope, c, c_ctx,
              ada_w, ada_b, ffn1_norm, ffn1_w_gu, ffn1_w_down, mix_norm, w_in, attn_sink,
              q_lat_norm, kv_lat_norm, w_uq, w_ukv, w_o_a, w_o_b, w_out,
              ffn2_norm, ffn2_w_gu, ffn2_w_down, final_norm):
    xp = x_prompt
    bp, sp = x_prompt.shape[0], x_prompt.shape[1]
    ks_a, vs_a, cs_b, rs_b = [], [], [], []
    for l in range(DEPTH):
        sh1, sc1, g1, sh2, sc2, g2, sh3, sc3, g3 = ada_mods(c_ctx[None, :], ada_w[l], ada_b[l])
        xp = xp + 0.5 * g1[:, None, :] * swiglu(modulate(rms_norm(xp, ffn1_norm[l]), sh1, sc1), ffn1_w_gu[l], ffn1_w_down[l])
        h = modulate(rms_norm(xp, mix_norm[l]), sh2, sc2)
        qa, ka, va, q_lat, ckv, krope, gate_a, gate_b = split_in(h @ w_in[l])
        qa = qa.reshape(bp, sp, A_KV_HEADS, A_GROUP, A_HEAD_DIM)
        ka = ka.reshape(bp, sp, A_KV_HEADS, A_HEAD_DIM)
        va = va.reshape(bp, sp, A_KV_HEADS, A_HEAD_DIM)
        o_a = block_sweep_attention(qa, ka, va, A_SCALE, attn_sink[l].reshape(A_KV_HEADS, A_GROUP))
        q_b = mla_queries(q_lat, q_lat_norm[l], w_uq[l])
        ckv_n = rms_norm(ckv, kv_lat_norm[l])
        k_b, v_b = mla_keys(ckv_n, krope, w_ukv[l])
        o_b = block_sweep_attention(q_b[:, :, :, None, :], k_b, v_b, B_SCALE, None)
        xp = xp + g2[:, None, :] * merge_branches(o_a, o_b, gate_a, gate_b, w_o_a[l], w_o_b[l], w_out[l])
        xp = xp + 0.5 * g3[:, None, :] * swiglu(modulate(rms_norm(xp, ffn2_norm[l]), sh3, sc3), ffn2_w_gu[l], ffn2_w_down[l])
        ks_a.append(ka)
        vs_a.append(va)
        cs_b.append(ckv_n)
        rs_b.append(krope)
    y_prompt = rms_norm(xp, final_norm)
    new_attn_k = jnp.stack(ks_a, axis=1)
    new_attn_v = jnp.stack(vs_a, axis=1)
    new_mla_ckv = jnp.stack(cs_b, axis=1)
    new_mla_krope = jnp.stack(rs_b, axis=1)

    xs = x_sample
    bs, ns = x_sample.shape[0], x_sample.shape[1]
    cos_a, sin_a = axial_rope(ns, A_HEAD_DIM)
    cos_b, sin_b = axial_rope(ns, B_ROPE_DIM)
    for l in range(DEPTH):
        sh1, sc1, g1, sh2, sc2, g2, sh3, sc3, g3 = ada_mods(c, ada_w[l], ada_b[l])
        xs = xs + 0.5 * g1[:, None, :] * swiglu(modulate(rms_norm(xs, ffn1_norm[l]), sh1, sc1), ffn1_w_gu[l], ffn1_w_down[l])
        h = modulate(rms_norm(xs, mix_norm[l]), sh2, sc2)
        qa, ka, va, q_lat, ckv, krope, gate_a, gate_b = split_in(h @ w_in[l])
        qa = apply_rope(qa.reshape(bs, ns, A_KV_HEADS, A_GROUP, A_HEAD_DIM), cos_a, sin_a)
        ka = apply_rope(ka.reshape(bs, ns, A_KV_HEADS, A_HEAD_DIM), cos_a, sin_a)
        va = va.reshape(bs, ns, A_KV_HEADS, A_HEAD_DIM)
        o_a = windowed_attention(qa, ka, va, cache_attn_k[:, l], cache_attn_v[:, l],
                                 attn_sink[l].reshape(A_KV_HEADS, A_GROUP), A_SCALE)
        q_b = mla_queries(q_lat, q_lat_norm[l], w_uq[l])
        q_b = jnp.concatenate([q_b[..., :B_NOPE_DIM], apply_rope(q_b[..., B_NOPE_DIM:], cos_b, sin_b)], axis=-1)
        ckv_n = rms_norm(ckv, kv_lat_norm[l])
        k_lat, v_lat = mla_keys(ckv_n, apply_rope(krope, cos_b, sin_b), w_ukv[l])
        k_ctx, v_ctx = mla_keys(cache_mla_ckv[:, l], cache_mla_krope[:, l], w_ukv[l])
        o_b = block_sweep_attention(q_b[:, :, :, None, :], jnp.concatenate([k_ctx, k_lat], axis=1),
                                    jnp.concatenate([v_ctx, v_lat], axis=1), B_SCALE, None)
        xs = xs + g2[:, None, :] * merge_branches(o_a, o_b, gate_a, gate_b, w_o_a[l], w_o_b[l], w_out[l])
        xs = xs + 0.5 * g3[:, None, :] * swiglu(modulate(rms_norm(xs, ffn2_norm[l]), sh3, sc3), ffn2_w_gu[l], ffn2_w_down[l])
    y_sample = rms_norm(xs, final_norm)

    return (y_prompt, y_sample, new_attn_k, new_attn_v, new_mla_ckv, new_mla_krope)
```

```python
import numpy as np
from contextlib import ExitStack
import concourse.bass as bass
import concourse.mybir as mybir
from concourse.bass_utils import run_bass_kernel_spmd

F32 = mybir.dt.float32
BF16 = mybir.dt.bfloat16
AF = mybir.ActivationFunctionType
ALU = mybir.AluOpType

D = 1024
DFF = 2816
NCH = 22
NBLK = 2
CPB = NCH // NBLK
INW = 3360
EPS = 1e-6
A_SCALE = 64 ** -0.5
B_SCALE = 96 ** -0.5
NKEY = 2560
K_CTX, K_OWN, K_OTH = 0, 512, 1536

ENGS = ("pe", "act", "dve", "pool", "sp")


class Buf:
    __slots__ = ("name", "w", "r", "rd")

    def __init__(self, name=""):
        self.name = name
        self.w = None
        self.r = {}
        self.rd = []


class Op:
    __slots__ = ("eng", "fn", "deps", "needs_inc", "val", "sem", "is_dma", "key", "inc")

    def __init__(self, eng, fn, is_dma=False):
        self.eng = eng
        self.fn = fn
        self.deps = []
        self.needs_inc = False
        self.val = None
        self.sem = None
        self.is_dma = is_dma
        self.key = None


class Sched:
    def __init__(self):
        self.ops = {e: [] for e in ENGS}
        self.dma_cnt = {}

    def _add(self, o, reads, writes):
        deps = {}

        def add(d):
            if d is None:
                return
            if (not d.is_dma) and (not o.is_dma) and d.eng == "pe" and o.eng == "pe":
                return
            deps[id(d)] = d
        for b in reads:
            add(b.w)
        for b in writes:
            add(b.w)
            for d in b.r.values():
                add(d)
            for d in b.rd:
                add(d)
        o.deps = list(deps.values())
        for d in o.deps:
            d.needs_inc = True
        for b in reads:
            if o.is_dma:
                b.rd.append(o)
            else:
                b.r[o.eng] = o
        for b in writes:
            b.w = o
            b.r = {}
            b.rd = []
        self.ops[o.eng].append(o)
        return o

    def op(self, eng, fn, reads=(), writes=()):
        return self._add(Op(eng, fn), reads, writes)

    def dma(self, eng, fn, key, reads=(), writes=(), inc=16):
        o = Op(eng, fn, is_dma=True)
        o.key = key
        o.inc = inc
        c = self.dma_cnt.get(key, 0) + 1
        self.dma_cnt[key] = c
        o.val = inc * c
        o.needs_inc = True
        return self._add(o, reads, writes)

    def wait_all(self, eng, ops):
        o = Op(eng, None)
        o.deps = list(ops)
        for d in o.deps:
            d.needs_inc = True
        self.ops[eng].append(o)
        return o

    def emit(self, nc, stack):
        esem = {e: stack.enter_context(nc.semaphore("s_" + e)) for e in ENGS}
        dsem = {}
        for i, k in enumerate(self.dma_cnt):
            dsem[k] = stack.enter_context(nc.semaphore("d%d" % i))
        for e in ENGS:
            c = 0
            for o in self.ops[e]:
                if o.is_dma:
                    o.sem = dsem[o.key]
                elif o.needs_inc:
                    c += 1
                    o.val = c
                    o.sem = esem[e]

        def run(e, eng):
            known = {}
            for o in self.ops[e]:
                need = {}
                for d in o.deps:
                    k = id(d.sem)
                    if k not in need or need[k][1] < d.val:
                        need[k] = (d.sem, d.val)
                for k, (sem, val) in need.items():
                    if known.get(k, 0) >= val:
                        continue
                    eng.wait_ge(sem, val)
                    known[k] = val
                if o.fn is None:
                    continue
                ins = o.fn(eng)
                if o.is_dma:
                    if o.inc == 16:
                        ins.then_inc(o.sem, 16)
                    else:
                        ins.then_inc(o.sem)
                elif o.needs_inc:
                    ins.then_inc(o.sem, 1)

        with nc.Block() as block:
            @block.tensor
            def _(eng):
                run("pe", eng)

            @block.scalar
            def _(eng):
                run("act", eng)

            @block.vector
            def _(eng):
                run("dve", eng)

            @block.gpsimd
            def _(eng):
                run("pool", eng)

            @block.sync
            def _(eng):
                run("sp", eng)


def _rope_tables(n, d_rot):
    rows = n // 64
    t_row = np.repeat(np.arange(rows, dtype=np.float32), 64)
    t_col = np.tile(np.arange(64, dtype=np.float32), rows)
    d_half = d_rot // 2
    inv = (1.0 / (np.float32(10000.0) ** (np.arange(0, d_half, 2, dtype=np.float32) / np.float32(d_half)))).astype(np.float32)
    ar = t_row[:, None] * inv[None, :]
    ac = t_col[:, None] * inv[None, :]
    ang = np.concatenate([ar, ar, ac, ac], axis=-1).astype(np.float32)
    return np.cos(ang).astype(np.float32), np.sin(ang).astype(np.float32)


def _rot_T(d_rot):
    s = d_rot // 4
    R = np.zeros((d_rot, d_rot), np.float32)
    for i in range(s):
        R[i, i + s] = -1.0
        R[i + s, i] = 1.0
        R[i + 2 * s, i + 3 * s] = -1.0
        R[i + 3 * s, i + 2 * s] = 1.0
    return np.ascontiguousarray(R.T)


_CONST = {}


def _consts():
    if _CONST:
        return _CONST
    cosA, sinA = _rope_tables(2048, 64)
    cosB, sinB = _rope_tables(2048, 32)
    ropeA = np.zeros((2, 2, 128, 1024), np.float32)
    ropeB = np.zeros((2, 2, 96, 1024), np.float32)
    for hh in range(2):
        sl = slice(hh * 1024, (hh + 1) * 1024)
        ropeA[hh, 0] = np.concatenate([cosA[sl].T, cosA[sl].T], 0)
        ropeA[hh, 1] = np.concatenate([sinA[sl].T, sinA[sl].T], 0)
        ropeB[hh, 0, :64] = 1.0
        ropeB[hh, 0, 64:] = cosB[sl].T
        ropeB[hh, 1, 64:] = sinB[sl].T
    RA = np.zeros((128, 128), np.float32)
    r64 = _rot_T(64)
    RA[:64, :64] = r64
    RA[64:, 64:] = r64
    RB = np.zeros((96, 96), np.float32)
    RB[64:, 64:] = _rot_T(32)
    _CONST.update(ropeA=ropeA, ropeB=ropeB, RA=RA, RB=RB, ident=np.eye(128, dtype=np.float32))
    return _CONST


DBG = {}


def build_nc(dbg=()):
    nc = bass.Bass("TRN2", target_bir_lowering=False)
    S = Sched()

    def din(name, shape):
        return nc.dram_tensor(name, list(shape), F32, kind="ExternalInput").ap()

    def dout(name, shape):
        return nc.dram_tensor(name, list(shape), F32, kind="ExternalOutput").ap()

    xg = din("xg", [3, 1024, D])
    cvfm = din("cvfm", [128, 16])
    ck = din("ck", [512, 128])
    cv = din("cv", [512, 128])
    cckv = din("cckv", [512, 256])
    ckr = din("ckr", [512, 32])
    ada_w = din("ada_w", [D, 9 * D])
    ada_bc = din("ada_bc", [128, 72])
    ncol = din("ncol", [128, 24])
    fnorm = din("fnorm", [1, D])
    w_gu = [din("w_gu1", [D, 2 * DFF]), din("w_gu2", [D, 2 * DFF])]
    w_dn = [din("w_d1", [DFF, D]), din("w_d2", [DFF, D])]
    w_in = din("w_in", [D, INW])
    sink65 = din("sink65", [65, 8])
    lncol = din("lncol", [128, 4])
    kvln = din("kvln", [1, 256])
    w_uq = din("w_uq", [256, 768])
    w_ukv = din("w_ukv", [256, 1024])
    w_oa = din("w_oa", [512, D])
    w_ob = din("w_ob", [512, D])
    w_out = din("w_out", [D, D])
    identd = din("ident", [128, 128])
    ropeA = din("ropeA", [2, 2, 128, 1024])
    ropeB = din("ropeB", [2, 2, 96, 1024])
    RAd = din("RA", [128, 128])
    RBd = din("RB", [96, 96])
    flagsd = din("flags", [128, 2])

    y = dout("y", [2, 1024, D])
    nk = dout("nk", [1024, 128])
    nv = dout("nv", [1024, 128])
    nckv = dout("nckv", [1024, 256])
    nkr = dout("nkr", [1024, 32])
    dbg_out = {}
    for name, shape in dbg:
        dbg_out[name] = dout(name, shape)

    XW = 4096
    snd = nc.dram_tensor("kv_snd", [128, XW], BF16)
    rcv = nc.dram_tensor("kv_rcv", [256, XW], BF16)
    kr_snd = nc.dram_tensor("kr_snd", [32, 1024], BF16)
    kr_rcv = nc.dram_tensor("kr_rcv", [64, 1024], BF16)
    st = ExitStack()
    out_ops = []

    def sb(name, shape, dt=F32):
        return st.enter_context(nc.sbuf_tensor(name, list(shape), dt))

    x = sb("x", [128, 8, D]);                 bx = [Buf("x%d" % t) for t in range(8)]
    hT = sb("hT", [128, 8, 1024], BF16);      bhA = [Buf("hTa%d" % t) for t in range(8)]; bhD = [Buf("hTd%d" % t) for t in range(8)]

    def BH(t0, t1):
        return bhA[t0:t1] + bhD[t0:t1]
    ar1 = sb("ar1", [128, 22528], BF16)
    actT = ar1[:, 0:CPB * 1024].rearrange("p (j t) -> p j t", j=CPB)
    wd = ar1[:, CPB * 1024:2 * CPB * 1024].rearrange("p (j d) -> p j d", j=CPB)
    oaT = ar1[0:65, 0:8192].rearrange("p (h t) -> p h t", h=8)
    obT = ar1[0:65, 8192:16384].rearrange("p (h t) -> p h t", h=8)
    oaT2 = ar1[:, 0:8192].rearrange("p (h t) -> p h t", h=8)
    obT2 = ar1[:, 8192:16384].rearrange("p (h t) -> p h t", h=8)
    b_oa2 = [Buf("oa2_%d" % i) for i in range(4)]
    b_ob2 = [Buf("ob2_%d" % i) for i in range(4)]
    KbT = [ar1[0:96, 16384 + i * NKEY:16384 + (i + 1) * NKEY] for i in range(2)]
    b_act = [[Buf("act%d_%d" % (j, tt)) for tt in range(2)] for j in range(CPB)]
    b_wd = [Buf("wd%d" % j) for j in range(CPB)]
    b_oa = [Buf("oa%d" % h) for h in range(8)]
    b_ob = [Buf("ob%d" % h) for h in range(8)]
    b_kb = [Buf("kb%d" % i) for i in range(2)]
    b_kbr = [Buf("kbr%d" % i) for i in range(2)]
    NSLOT = 4
    wsl = [sb("wsl%d" % i, [128, 8, 256], BF16) for i in range(NSLOT)]
    b_ws = [Buf("ws%d" % i) for i in range(NSLOT)]
    ar2 = sb("ar2", [128, 9472], BF16)
    QaT = ar2[:, 0:4096].rearrange("p (g t) -> p g t", g=4)
    KaT = ar2[:, 4096:4096 + NKEY]
    Va = ar2[:, 6656:6656 + 20 * 132].rearrange("p (n k e) -> p n k e", n=20, k=2)
    mT = ar2[:, 0:8192].rearrange("p (c t) -> p c t", c=8)
    b_qa = [Buf("qa%d" % g) for g in range(4)]
    b_ka = {k: Buf("ka_" + k) for k in ("ctx", "own", "oth")}
    b_va = {k: Buf("va_" + k) for k in ("ctx", "own", "oth")}
    b_va1 = Buf("va_ones")
    b_m = [[Buf("m%d_%d" % (c, tt)) for tt in range(2)] for c in range(8)]
    CkvT = sb("CkvT", [128, 2, NKEY], BF16);  b_ckv = {k: Buf("ckv_" + k) for k in ("ctx", "own", "oth")}
    KrT = sb("KrT", [96, NKEY], BF16);        b_kr = {k: Buf("kr_" + k) for k in ("ctx", "own", "oth")}
    qlT = sb("qlT", [128, 2, 1024], BF16);    b_ql = [Buf("ql%d" % t) for t in range(8)]
    QbT = [sb("QbT%d" % i, [96, 1024], BF16) for i in range(2)]
    b_qb = [Buf("qb%d" % i) for i in range(2)]
    Vb = [sb("Vb%d" % i, [128, 20, 66], BF16) for i in range(2)]
    b_vb = [Buf("vb%d" % i) for i in range(2)]
    b_vb1 = [Buf("vb1_%d" % i) for i in range(2)]
    NPT = 4
    PT = [sb("PT%d" % i, [128, 512], BF16) for i in range(NPT)]
    b_pt = [Buf("pt%d" % i) for i in range(NPT)]
    ropA = sb("ropA", [128, 2, 1024]);        b_ropA = Buf("ropA")
    ropB = sb("ropB", [96, 2, 1024]);         b_ropB = Buf("ropB")
    Gbc = sb("Gbc", [128, D]);                b_G = Buf("Gbc")
    fgbc = sb("fgbc", [128, D]);              b_fg = Buf("fg")
    kvlnbc = sb("kvlnbc", [128, 256]);        b_kvln = Buf("kvlnbc")
    ident32 = sb("ident32", [128, 128]);      b_id32 = Buf("id32")
    identb = sb("identb", [128, 128], BF16);  b_idb = Buf("idb")
    RAt = sb("RAt", [128, 128]);              b_RA = Buf("RA")
    RBt = sb("RBt", [96, 96]);                b_RB = Buf("RB")
    ones32 = sb("ones32", [128, 128]);        b_ones = Buf("ones32")
    onesb = sb("onesb", [128, 64], BF16);     b_onesb = Buf("onesb")
    esinkb = sb("esinkb", [64, 8]);           b_esinkb = Buf("esinkb")
    diag = [sb("diag%d" % i, [128, 128]) for i in range(2)]
    b_diag = [Buf("diag%d" % i) for i in range(2)]
    flags = sb("flags_sb", [128, 2]);         b_flags = Buf("flags")
    esink = sb("esink", [65, 8]);             b_esink = Buf("esink")
    cvs = sb("cvs", [128, 16]);               b_cvs = Buf("cvs")
    cvb = sb("cvb", [128, 8, 2], BF16);       b_cvb = Buf("cvb")
    adab = sb("adab", [128, 72]);             b_adab = Buf("adab")
    ncols = sb("ncols", [128, 24]);           b_ncols = Buf("ncols")
    lncols = sb("lncols", [128, 4]);          b_lncols = Buf("lncols")
    modc = sb("modc", [128, 72, 2]);          b_modc = Buf("modc")
    Acol = sb("Acol", [128, 3, 8, 2]);        b_Acol = Buf("Acol")
    ss = sb("ss", [128, 8]);                  b_ss = Buf("ss")
    rstd = sb("rstd", [128, 8]);              b_rstd = Buf("rstd")
    ss2 = sb("ss2", [128, 16]);               b_ss2 = Buf("ss2")
    b_ss2c = [Buf("ss2_%d" % i) for i in range(16)]
    b_rs2c = [Buf("rs2_%d" % i) for i in range(16)]
    rs2 = sb("rs2", [128, 16]);               b_rs2 = Buf("rs2")
    epsc = sb("epsc", [128, 1]);              b_eps = Buf("eps")
    junk = sb("junk", [128, D], BF16);        b_junk = Buf("junk")
    xn = [sb("xn%d" % i, [128, D], BF16) for i in range(2)]
    b_xn = [Buf("xn%d" % i) for i in range(2)]
    NSC = 3
    sc32 = [sb("sc32_%d" % i, [128, 512]) for i in range(NSC)]
    b_sc = [Buf("sc32_%d" % i) for i in range(NSC)]
    rden2 = sb("rden", [65, 2, 512]);         b_rden2 = [Buf("rden0"), Buf("rden1")]
    ctm = sb("ctm", [128, 4, 352], BF16);     b_ctm = Buf("ctm")
    fdum = sb("fdum", [128, 8]);              b_fd = Buf("fdum")

    PSB = [st.enter_context(nc.psum_tensor("psb%d" % i, [128, 512], F32)) for i in range(8)]
    b_ps = [Buf("ps%d" % i) for i in range(8)]
    ps_ctr = [0]

    ps_res = set()

    def ps(reserve=False):
        while True:
            i = ps_ctr[0] % 8
            ps_ctr[0] += 1
            if i not in ps_res:
                break
        if reserve:
            ps_res.add(i)
        return PSB[i], b_ps[i]

    def ps_release(bank):
        for i, b in enumerate(PSB):
            if b is bank:
                ps_res.discard(i)

    rr = {}

    def rot(name, n):
        i = rr.get(name, 0)
        rr[name] = i + 1
        return i % n

    def fence(bufs):
        S.op("pool", lambda e: e.memset(fdum[:, 0:1], 0.0), writes=list(bufs) + [b_fd])

    def dbg_dump(name, src_ap, bufs):
        if name in dbg_out:
            o = S.dma("sp", lambda e: e.dma_start(out=dbg_out[name], in_=src_ap), key="dbg_" + name, reads=bufs)
            out_ops.append(o)

    def ld(dst, src, buf, key, q="sp"):
        return S.dma(q, lambda e: e.dma_start(out=dst, in_=src), key=key, writes=[buf])

    ld(ident32[:], identd[:, :], b_id32, "c0")
    ld(identb[:], identd[:, :], b_idb, "c1", q="pool")
    ld(RAt[:], RAd[:, :], b_RA, "c2")
    ld(RBt[:], RBd[:, :], b_RB, "c3")
    ld(flags[:], flagsd[:, :], b_flags, "c4")
    ld(esink[:], sink65[:, :], b_esink, "c5")
    ld(cvs[:], cvfm[:, :], b_cvs, "c6")
    ld(adab[:], ada_bc[:, :], b_adab, "c7")
    ld(ncols[:], ncol[:, :], b_ncols, "c8")
    ld(lncols[:], lncol[:, :], b_lncols, "c9")
    ld(fgbc[:], fnorm.partition_broadcast(128), b_fg, "c10")
    ld(kvlnbc[:], kvln.partition_broadcast(128), b_kvln, "c11")
    S.op("pool", lambda e: e.memset(ones32[:], 1.0), writes=[b_ones])
    S.op("pool", lambda e: e.memset(onesb[:], 1.0), writes=[b_onesb])
    ld(esinkb[:], sink65[64:65, :].partition_broadcast(64), b_esinkb, "c12")
    S.op("act", lambda e: e.activation(out=esinkb[:], in_=esinkb[:], func=AF.Exp), reads=[b_esinkb], writes=[b_esinkb])
    S.op("pool", lambda e: e.memset(epsc[:], EPS), writes=[b_eps])
    S.op("act", lambda e: e.activation(out=esink[64:65, :], in_=esink[64:65, :], func=AF.Exp), reads=[b_esink], writes=[b_esink])

    S.op("act", lambda e: e.activation(out=cvb[:].rearrange("p k m -> p (k m)"), in_=cvs[:], func=AF.Silu), reads=[b_cvs], writes=[b_cvb])
    pm, bpm = ps(reserve=True)
    b_modc_n = [Buf("modc%d" % n) for n in range(3)]
    b_Acol_n = [Buf("Acol%d" % n) for n in range(3)]
    ada_state = {"cc": 0}

    def ada_finish(n):
        S.op("dve", lambda e: e.tensor_tensor(out=modc[:, 24 * n:24 * n + 24, :], in0=pm[:, 48 * n:48 * n + 48].rearrange("p (c m) -> p c m", m=2),
                                              in1=adab[:, 24 * n:24 * n + 24].unsqueeze(2).to_broadcast([128, 24, 2]), op=ALU.add),
             reads=[bpm, b_adab], writes=[b_modc_n[n]])
        S.op("dve", (lambda e: e.tensor_scalar(out=Acol[:, n], in0=modc[:, (3 * n + 1) * 8:(3 * n + 2) * 8, :], scalar1=1.0, scalar2=None, op0=ALU.add)),
             reads=[b_modc_n[n]], writes=[b_Acol_n[n]])
        S.op("dve", (lambda e: e.tensor_tensor(out=Acol[:, n], in0=Acol[:, n], in1=ncols[:, n * 8:(n + 1) * 8].unsqueeze(2).to_broadcast([128, 8, 2]), op=ALU.mult)),
             reads=[b_Acol_n[n], b_ncols], writes=[b_Acol_n[n]])
        if n == 2:
            ps_release(pm)

    def ada_chunk():
        cc = ada_state["cc"]
        if cc >= 36:
            return
        ada_state["cc"] += 1
        s = rot("ws", NSLOT)
        S.dma("pool", (lambda e: e.dma_start(out=wsl[s][:], in_=ada_w[:, cc * 256:(cc + 1) * 256].rearrange("(kc p) n -> p kc n", p=128))),
              key="ws%d" % s, writes=[b_ws[s]])

        def mm_ada(e):
            for sub in range(2):
                ch = cc * 2 + sub
                for kc in range(8):
                    ins = e.matmul(pm[:, ch * 2:ch * 2 + 2], lhsT=wsl[s][:, kc, sub * 128:(sub + 1) * 128], rhs=cvb[:, kc, :], start=(kc == 0), stop=(kc == 7))
            return ins
        S.op("pe", mm_ada, reads=[b_ws[s], b_cvb], writes=[bpm])
        if cc % 12 == 11:
            ada_finish(cc // 12)

    for _ in range(12):
        ada_chunk()
    dbg_dump("d_modc", modc[:].rearrange("p c m -> p (c m)"), b_modc_n)

    def A_ap(n, kc, m):
        return Acol[:, n, kc, m:m + 1]

    def B_ap(n, kc, m):
        return modc[:, 3 * n * 8 + kc, m:m + 1]

    def load_x(gi):
        for t in range(8):
            S.dma("sp", (lambda e, t=t: e.dma_start(out=x[:, t, :], in_=xg[gi, t * 128:(t + 1) * 128, :])), key="x%d" % t, writes=[bx[t]])

    def gate_steps(n, m, half_scale):
        sc = 0.5 if half_scale else 1.0
        for kc in range(8):
            di = rot("diag", 2)
            S.op("dve", (lambda e, di=di, kc=kc: e.tensor_scalar(out=diag[di][:], in0=ident32[:], scalar1=modc[:, (3 * n + 2) * 8 + kc, m:m + 1], scalar2=sc, op0=ALU.mult, op1=ALU.mult)),
                 reads=[b_id32, b_modc_n[n]], writes=[b_diag[di]])
            pg, bpg = ps()
            S.op("pe", (lambda e, di=di, pg=pg: e.matmul(pg[:, 0:128], lhsT=ones32[:], rhs=diag[di][:], start=True, stop=True)),
                 reads=[b_ones, b_diag[di]], writes=[bpg])
            S.op("act", (lambda e, pg=pg, kc=kc: e.activation(out=Gbc[:, kc * 128:(kc + 1) * 128], in_=pg[:, 0:128], func=AF.Copy)),
                 reads=[bpg], writes=[b_G])
            yield

    def norm_to_hT(n, m, gate=None):
        gs = gate_steps(*gate) if gate is not None else iter(())
        for t in range(8):
            S.op("act", (lambda e, t=t: e.activation(out=junk[:], in_=x[:, t, :], func=AF.Square, accum_out=ss[:, t:t + 1])),
                 reads=[bx[t]], writes=[b_junk, b_ss])
        S.op("act", lambda e: e.activation(out=rstd[:], in_=ss[:], func=AF.Sqrt, scale=1.0 / D, bias=epsc[:]), reads=[b_ss, b_eps], writes=[b_rstd])
        S.op("dve", lambda e: e.reciprocal(out=rstd[:], in_=rstd[:]), reads=[b_rstd], writes=[b_rstd])

        def cp(t):
            xi = rot("xn", 2)
            S.op("act", (lambda e: e.activation(out=xn[xi][:], in_=x[:, t, :], func=AF.Copy, scale=rstd[:, t:t + 1])),
                 reads=[bx[t], b_rstd], writes=[b_xn[xi]])
            return xi
        nxt = cp(0)
        for t in range(8):
            xi = nxt
            if t + 1 < 8:
                nxt = cp(t + 1)
            pt_, bpt = ps()
            ptb = pt_[:].bitcast(BF16)

            def tr(e, xi=xi, ptb=ptb):
                for kc in range(8):
                    ins = e.transpose(out=ptb[:, kc * 128:(kc + 1) * 128], in_=xn[xi][:, kc * 128:(kc + 1) * 128], identity=identb[:])
                return ins
            S.op("pe", tr, reads=[b_xn[xi], b_idb], writes=[bpt])

            def ev_d(e, t=t, ptb=ptb):
                for kc in range(8):
                    ins = e.tensor_scalar(out=hT[:, kc, t * 128:(t + 1) * 128], in0=ptb[:, kc * 128:(kc + 1) * 128],
                                          scalar1=A_ap(n, kc, m), scalar2=B_ap(n, kc, m), op0=ALU.mult, op1=ALU.add)
                return ins
            S.op("dve", ev_d, reads=[bpt, b_Acol_n[n], b_modc_n[n]], writes=[bhA[t], bhD[t]])
            next(gs, None)
        for _ in gs:
            pass

    def resid_add(t, c0, w, pd, bpd):
        si = rot("sc", NSC)
        S.op("dve", (lambda e: e.tensor_tensor(out=sc32[si][:, 0:w], in0=pd[:, 0:w], in1=Gbc[:, c0:c0 + w], op=ALU.mult)),
             reads=[bpd, b_G], writes=[b_sc[si]])
        S.op("dve", (lambda e: e.tensor_tensor(out=x[:, t, c0:c0 + w], in0=x[:, t, c0:c0 + w], in1=sc32[si][:, 0:w], op=ALU.add)),
             reads=[b_sc[si], bx[t]], writes=[bx[t]])

    def ffn(fi, n, m, hook=None):
        norm_to_hT(n, m, gate=(n, m, True))
        wg, wdn = w_gu[fi], w_dn[fi]
        for blk in range(NBLK):
            pend_wd = []
            for jj in range(CPB):
                j = blk * CPB + jj
                s = rot("ws", NSLOT)
                S.dma("pool", (lambda e, s=s, j=j: e.dma_start(out=wsl[s][:, :, 0:128], in_=wg[:, j * 128:(j + 1) * 128].rearrange("(kc p) n -> p kc n", p=128))),
                      key="ws%d" % s, writes=[b_ws[s]])
                S.dma("pool", (lambda e, s=s, j=j: e.dma_start(out=wsl[s][:, :, 128:256], in_=wg[:, DFF + j * 128:DFF + (j + 1) * 128].rearrange("(kc p) n -> p kc n", p=128))),
                      key="ws%d" % s, writes=[b_ws[s]])
                if jj >= 2:
                    jw = jj - 2
                    S.dma("pool", (lambda e, jw=jw, blk=blk: e.dma_start(out=wd[:, jw, :], in_=wdn[(blk * CPB + jw) * 128:(blk * CPB + jw + 1) * 128, :])),
                          key="wd%d" % jw, writes=[b_wd[jw]])
                for tt in range(2):
                    pa, bpa = ps()
                    pu, bpu = ps()

                    def mm_gu(e, s=s, tt=tt, pa=pa, pu=pu):
                        for kc in range(8):
                            e.matmul(pa[:], lhsT=wsl[s][:, kc, 0:128], rhs=hT[:, kc, tt * 512:(tt + 1) * 512], start=(kc == 0), stop=(kc == 7))
                        for kc in range(8):
                            ins = e.matmul(pu[:], lhsT=wsl[s][:, kc, 128:256], rhs=hT[:, kc, tt * 512:(tt + 1) * 512], start=(kc == 0), stop=(kc == 7))
                        return ins
                    S.op("pe", mm_gu, reads=[b_ws[s]] + BH(tt * 4, (tt + 1) * 4), writes=[bpa, bpu])
                    si = rot("sc", NSC)
                    S.op("act", (lambda e, si=si, pa=pa: e.activation(out=sc32[si][:], in_=pa[:], func=AF.Silu)), reads=[bpa], writes=[b_sc[si]])
                    S.op("dve", (lambda e, si=si, pu=pu, jj=jj, tt=tt: e.tensor_tensor(out=actT[:, jj, tt * 512:(tt + 1) * 512], in0=sc32[si][:], in1=pu[:], op=ALU.mult)),
                         reads=[b_sc[si], bpu], writes=[b_act[jj][tt]])
            for jw in range(CPB - 2, CPB):
                S.dma("pool", (lambda e, jw=jw, blk=blk: e.dma_start(out=wd[:, jw, :], in_=wdn[(blk * CPB + jw) * 128:(blk * CPB + jw + 1) * 128, :])),
                      key="wd%d" % jw, writes=[b_wd[jw]])
            for t in range(8):
                if hook is not None:
                    hook()
                for half in range(2):
                    pd, bpd = ps()

                    def mm_d(e, t=t, half=half, pd=pd):
                        for jj in range(CPB):
                            ins = e.matmul(pd[:], lhsT=actT[:, jj, t * 128:(t + 1) * 128], rhs=wd[:, jj, half * 512:(half + 1) * 512], start=(jj == 0), stop=(jj == CPB - 1))
                        return ins
                    S.op("pe", mm_d, reads=[b_act[jj][t // 4] for jj in range(CPB)] + b_wd, writes=[bpd])
                    resid_add(t, half * 512, 512, pd, bpd)

    def final_out(gi_out):
        for t in range(8):
            S.op("act", (lambda e, t=t: e.activation(out=junk[:], in_=x[:, t, :], func=AF.Square, accum_out=ss[:, t:t + 1])),
                 reads=[bx[t]], writes=[b_junk, b_ss])
        S.op("act", lambda e: e.activation(out=rstd[:], in_=ss[:], func=AF.Sqrt, scale=1.0 / D, bias=epsc[:]), reads=[b_ss, b_eps], writes=[b_rstd])
        S.op("dve", lambda e: e.reciprocal(out=rstd[:], in_=rstd[:]), reads=[b_rstd], writes=[b_rstd])
        for t in range(8):
            S.op("dve", (lambda e, t=t: e.scalar_tensor_tensor(out=x[:, t, :], in0=x[:, t, :], scalar=rstd[:, t:t + 1], in1=fgbc[:], op0=ALU.mult, op1=ALU.mult)),
                 reads=[bx[t], b_rstd, b_fg], writes=[bx[t]])
            o = S.dma("sp", (lambda e, t=t: e.dma_start(out=y[gi_out, t * 128:(t + 1) * 128, :], in_=x[:, t, :])), key="yo%d" % t, reads=[bx[t]])
            out_ops.append(o)

    PO = DBG.get("po", 7)
    KOFF = {"ctx": K_CTX, "own": K_OWN, "oth": K_OTH}
    VT0 = {"ctx": 0, "own": 4, "oth": 12}

    def wslot():
        s = rot("ws", NSLOT)
        return s

    def rope_evac(p1, bp1, rows, scale, ridx, tok0, out_ap, out_bufs, rope):
        if not rope:
            S.op("act", (lambda e: e.activation(out=out_ap, in_=p1[0:rows, :], func=AF.Copy, scale=scale)), reads=[bp1], writes=out_bufs)
            return
        tab, btab, Rt, bR = (ropA, b_ropA, RAt, b_RA) if rows == 128 else (ropB, b_ropB, RBt, b_RB)
        sa = rot("sc", NSC)
        S.op("act", (lambda e: e.activation(out=sc32[sa][0:rows, :], in_=p1[0:rows, :], func=AF.Copy, scale=scale)), reads=[bp1], writes=[b_sc[sa]])
        p2, bp2 = ps()
        S.op("pe", (lambda e: e.matmul(p2[0:rows, :], lhsT=Rt[0:rows, 0:rows], rhs=sc32[sa][0:rows, :], start=True, stop=True)), reads=[b_sc[sa], bR], writes=[bp2])
        sb_ = rot("sc", NSC)
        S.op("dve", (lambda e: e.tensor_tensor(out=sc32[sb_][0:rows, :], in0=p2[0:rows, :], in1=tab[0:rows, 1, tok0:tok0 + 512], op=ALU.mult)),
             reads=[bp2, btab], writes=[b_sc[sb_]])
        S.op("dve", (lambda e: e.tensor_tensor(out=sc32[sa][0:rows, :], in0=sc32[sa][0:rows, :], in1=tab[0:rows, 0, tok0:tok0 + 512], op=ALU.mult)),
             reads=[b_sc[sa], btab], writes=[b_sc[sa]])
        S.op("dve", (lambda e: e.tensor_tensor(out=out_ap, in0=sc32[sa][0:rows, :], in1=sc32[sb_][0:rows, :], op=ALU.add)),
             reads=[b_sc[sa], b_sc[sb_]], writes=out_bufs)

    def fm_proj(s, bslot, lhs_cols, M, tt):
        p1, bp1 = ps()
        c0 = lhs_cols

        def mm(e):
            for kc in range(8):
                ins = e.matmul(p1[0:M, :], lhsT=wsl[s][:, kc, c0:c0 + M], rhs=hT[:, kc, tt * 512:(tt + 1) * 512], start=(kc == 0), stop=(kc == 7))
            return ins
        S.op("pe", mm, reads=[bslot] + BH(tt * 4, (tt + 1) * 4), writes=[bp1])
        return p1, bp1

    def tm_proj(s, bslot, c0, N, t):
        p1, bp1 = ps()

        def mm(e):
            for kc in range(8):
                ins = e.matmul(p1[:, 0:N], lhsT=hT[:, kc, t * 128:(t + 1) * 128], rhs=wsl[s][:, kc, c0:c0 + N], start=(kc == 0), stop=(kc == 7))
            return ins
        S.op("pe", mm, reads=[bslot] + BH(t, t + 1), writes=[bp1])
        return p1, bp1

    def load_w_cols(src, c0, n, dcol=0, s=None):
        if s is None:
            s = wslot()
        S.dma("pool", (lambda e: e.dma_start(out=wsl[s][:, :, dcol:dcol + n], in_=src[:, c0:c0 + n].rearrange("(kc p) n -> p kc n", p=128))),
              key="ws%d" % s, writes=[b_ws[s]])
        return s

    def lat_norm_T(p1, bp1, c0, lncol0, t, out_fn, out_bufs, tm_out=None):
        ci = rot("ss2", 16)
        ssc, rsc, bssc, brsc = ss2[:, ci:ci + 1], rs2[:, ci:ci + 1], b_ss2c[ci], b_rs2c[ci]
        S.op("act", (lambda e: e.activation(out=junk[:, 0:256], in_=p1[:, c0:c0 + 256], func=AF.Square, accum_out=ssc)),
             reads=[bp1], writes=[b_junk, bssc])
        S.op("act", lambda e: e.activation(out=rsc, in_=ssc, func=AF.Sqrt, scale=1.0 / 256, bias=epsc[:]), reads=[bssc, b_eps], writes=[brsc])
        S.op("dve", lambda e: e.reciprocal(out=rsc, in_=rsc), reads=[brsc], writes=[brsc])
        xi = rot("xn", 2)
        S.op("act", (lambda e: e.activation(out=xn[xi][:, 0:256], in_=p1[:, c0:c0 + 256], func=AF.Copy, scale=rsc)), reads=[bp1, brsc], writes=[b_xn[xi]])
        if tm_out is not None:
            tm_out(rsc, brsc, b_xn[xi])

        def stage_b():
            p2, bp2 = ps()
            p2b = p2[:].bitcast(BF16)

            def tr(e):
                for k2 in range(2):
                    ins = e.transpose(out=p2b[:, k2 * 128:(k2 + 1) * 128], in_=xn[xi][:, k2 * 128:(k2 + 1) * 128], identity=identb[:])
                return ins
            S.op("pe", tr, reads=[b_xn[xi], b_idb], writes=[bp2])

            def ev(e):
                for k2 in range(2):
                    ins = e.tensor_scalar(out=out_fn(k2), in0=p2b[:, k2 * 128:(k2 + 1) * 128], scalar1=lncols[:, lncol0 + k2:lncol0 + k2 + 1], scalar2=None, op0=ALU.mult)
                return ins
            S.op("dve", ev, reads=[bp2, b_lncols], writes=out_bufs)
        return stage_b

    def mix_proj(kind, reg, ridx, rope, full, prompt_out):
        k0 = KOFF[reg]
        vt0 = VT0[reg]
        if full:
            for half in range(2):
                s = wslot()
                for gl in range(2):
                    g = 2 * half + gl
                    for kv in range(2):
                        hc = (kv * 4 + g) * 64
                        S.dma("pool", (lambda e, s=s, gl=gl, kv=kv, hc=hc: e.dma_start(out=wsl[s][:, :, gl * 128 + kv * 64:gl * 128 + (kv + 1) * 64],
                                                                                   in_=w_in[:, hc:hc + 64].rearrange("(kc p) d -> p kc d", p=128))),
                              key="ws%d" % s, writes=[b_ws[s]])
                for gl in range(2):
                    g = 2 * half + gl
                    for tt in range(2):
                        p1, bp1 = fm_proj(s, b_ws[s], gl * 128, 128, tt)
                        rope_evac(p1, bp1, 128, A_SCALE, ridx, tt * 512, QaT[:, g, tt * 512:(tt + 1) * 512], [b_qa[g]], rope)
        s = load_w_cols(w_in, 512, 256)
        for tt in range(2):
            p1, bp1 = fm_proj(s, b_ws[s], 0, 128, tt)
            rope_evac(p1, bp1, 128, 1.0, ridx, tt * 512, KaT[:, k0 + tt * 512:k0 + (tt + 1) * 512], [b_ka[reg]], rope)
        for t in range(8):
            p1, bp1 = tm_proj(s, b_ws[s], 0, 256, t)
            S.op("act", (lambda e, p1=p1, t=t: e.activation(out=Va[:, vt0 + t, :, 0:64], in_=p1[:, 128:256].rearrange("p (k d) -> p k d", k=2), func=AF.Copy)),
                 reads=[bp1], writes=[b_va[reg]])
            if prompt_out and (PO & 1):
                si = rot("sc", NSC)
                S.op("dve", (lambda e, p1=p1, si=si: e.tensor_copy(out=sc32[si][:, 0:256], in_=p1[:, 0:256])), reads=[bp1, b_va[reg]], writes=[b_sc[si]])
                if PO & 8:
                    continue
                if PO & 16:
                    o = S.dma("sp", (lambda e, si=si, t=t: e.dma_start(out=y[1, t * 128:(t + 1) * 128, 0:256], in_=sc32[si][:, 0:256])), key="osc%d" % si, reads=[b_sc[si]])
                    out_ops.append(o)
                    continue
                o = S.dma("sp", (lambda e, si=si, t=t: e.dma_start(out=nk[t * 128:(t + 1) * 128, :], in_=sc32[si][:, 0:128])), key="osc%d" % si, reads=[b_sc[si]])
                out_ops.append(o)
                o = S.dma("sp", (lambda e, si=si, t=t: e.dma_start(out=nv[t * 128:(t + 1) * 128, :], in_=sc32[si][:, 128:256])), key="osc%d" % si, reads=[b_sc[si]])
                out_ops.append(o)
        if full:
            s = load_w_cols(w_in, 768, 256)
            pend = None
            for t in range(8):
                p1, bp1 = tm_proj(s, b_ws[s], 0, 256, t)
                stb = lat_norm_T(p1, bp1, 0, 0, t, (lambda k2, t=t: qlT[:, k2, t * 128:(t + 1) * 128]), [b_ql[t]])
                if pend is not None:
                    pend()
                pend = stb
            pend()
        s = load_w_cols(w_in, 1024, 256)
        pend = None
        for t in range(8):
            p1, bp1 = tm_proj(s, b_ws[s], 0, 256, t)
            tm_out = None
            if prompt_out and (PO & 2):
                def tm_out(rs_ap, brs, bxn, p1=p1, bp1=bp1, t=t):
                    si = rot("sc", NSC)
                    S.op("dve", (lambda e: e.scalar_tensor_tensor(out=sc32[si][:, 0:256], in0=p1[:, 0:256], scalar=rs_ap, in1=kvlnbc[:], op0=ALU.mult, op1=ALU.mult)),
                         reads=[bp1, brs, b_kvln, bxn], writes=[b_sc[si]])
                    o = S.dma("sp", (lambda e: e.dma_start(out=nckv[t * 128:(t + 1) * 128, :], in_=sc32[si][:, 0:256])), key="osc%d" % si, reads=[b_sc[si]])
                    out_ops.append(o)
            stb = lat_norm_T(p1, bp1, 0, 2, t, (lambda k2, t=t: CkvT[:, k2, k0 + t * 128:k0 + (t + 1) * 128]), [b_ckv[reg]], tm_out=tm_out)
            if pend is not None:
                pend()
            pend = stb
        pend()
        s = load_w_cols(w_in, 1216, 96)
        for tt in range(2):
            p1, bp1 = fm_proj(s, b_ws[s], 0, 96, tt)
            if rope:
                sa = rot("sc", NSC)
                rope_evac(p1, bp1, 96, 1.0, ridx, tt * 512, sc32[sa][0:96, :], [b_sc[sa]], True)
                S.op("act", (lambda e, sa=sa, tt=tt: e.activation(out=KrT[64:96, k0 + tt * 512:k0 + (tt + 1) * 512], in_=sc32[sa][64:96, :], func=AF.Copy)),
                     reads=[b_sc[sa]], writes=[b_kr[reg]])
            else:
                S.op("act", (lambda e, p1=p1, tt=tt: e.activation(out=KrT[64:96, k0 + tt * 512:k0 + (tt + 1) * 512], in_=p1[64:96, :], func=AF.Copy)),
                     reads=[bp1], writes=[b_kr[reg]])
        if prompt_out and (PO & 4):
            for t in range(8):
                p1, bp1 = tm_proj(s, b_ws[s], 64, 32, t)
                si = rot("sc", NSC)
                S.op("dve", (lambda e, p1=p1, si=si: e.tensor_copy(out=sc32[si][:, 0:32], in_=p1[:, 0:32])), reads=[bp1], writes=[b_sc[si]])
                o = S.dma("sp", (lambda e, si=si, t=t: e.dma_start(out=nkr[t * 128:(t + 1) * 128, :], in_=sc32[si][:, 0:32])), key="osc%d" % si, reads=[b_sc[si]])
                out_ops.append(o)

    def load_ctx():
        S.op("pool", lambda e: e.memset(ctm[:], 0.0), writes=[b_ctm])
        S.dma("pool", lambda e: e.dma_start(out=ctm[:, :, 0:128], in_=ck.rearrange("(n p) c -> p n c", p=128)), key="ctm", writes=[b_ctm])
        S.dma("pool", lambda e: e.dma_start(out=ctm[:, :, 320:352], in_=ckr.rearrange("(n p) c -> p n c", p=128)), key="ctm", writes=[b_ctm])
        for kv in range(2):
            S.dma("pool", (lambda e, kv=kv: e.dma_start(out=Va[:, 0:4, kv, 0:64], in_=cv.rearrange("(n p) (k d) -> p n k d", p=128, k=2)[:, :, kv, :])), key="vactx", writes=[b_va["ctx"]])
        for n in range(4):
            p1, bp1 = ps()
            p1b = p1[:].bitcast(BF16)
            S.op("pe", (lambda e, n=n, p1b=p1b: e.transpose(out=p1b[:, 0:128], in_=ctm[:, n, 0:128], identity=identb[:])), reads=[b_ctm, b_idb], writes=[bp1])
            S.op("act", (lambda e, n=n, p1b=p1b: e.activation(out=KaT[:, K_CTX + n * 128:K_CTX + (n + 1) * 128], in_=p1b[:, 0:128], func=AF.Copy)), reads=[bp1], writes=[b_ka["ctx"]])
            p2, bp2 = ps()
            p2b = p2[:].bitcast(BF16)
            S.op("pe", (lambda e, n=n, p2b=p2b: e.transpose(out=p2b[0:96, 0:128], in_=ctm[:, n, 256:352], identity=identb[:])), reads=[b_ctm, b_idb], writes=[bp2])
            S.op("act", (lambda e, n=n, p2b=p2b: e.activation(out=KrT[64:96, K_CTX + n * 128:K_CTX + (n + 1) * 128], in_=p2b[64:96, 0:128], func=AF.Copy)), reads=[bp2], writes=[b_kr["ctx"]])
        S.dma("pool", lambda e: e.dma_start(out=ctm[:, :, 0:256], in_=cckv.rearrange("(n p) c -> p n c", p=128)), key="ctm", writes=[b_ctm])
        for n in range(4):
            p1, bp1 = ps()
            p1b = p1[:].bitcast(BF16)

            def tr(e, n=n, p1b=p1b):
                for k2 in range(2):
                    ins = e.transpose(out=p1b[:, k2 * 128:(k2 + 1) * 128], in_=ctm[:, n, k2 * 128:(k2 + 1) * 128], identity=identb[:])
                return ins
            S.op("pe", tr, reads=[b_ctm, b_idb], writes=[bp1])
            S.op("act", (lambda e, n=n, p1b=p1b: e.activation(out=CkvT[:, :, K_CTX + n * 128:K_CTX + (n + 1) * 128], in_=p1b[:, 0:256].rearrange("p (k t) -> p k t", k=2), func=AF.Copy)),
                 reads=[bp1], writes=[b_ckv["ctx"]])

    LOOK = 3

    class Unit:
        pass

    def run_attention_stream(front):
        from collections import deque
        backq = deque()
        normq = []

        def emit_st(u, k):
            k_ap, k_bufs, v_ap, v_bufs, mask, flag = u.tiles[k]
            N = u.N
            pS, bpS = ps()
            S.op("pe", (lambda e: e.matmul(pS[:, 0:N], lhsT=k_ap, rhs=u.q_ap, start=True, stop=True)), reads=list(k_bufs) + list(u.q_bufs), writes=[bpS])
            pi = rot("pt", NPT)
            S.op("act", (lambda e: e.activation(out=PT[pi][:, 0:N], in_=pS[:, 0:N], func=AF.Exp)), reads=[bpS], writes=[b_pt[pi]])
            if mask == "prev":
                S.op("pool", (lambda e: e.affine_select(out=PT[pi][:].rearrange("p (g q) -> p g q", g=4), in_=PT[pi][:].rearrange("p (g q) -> p g q", g=4),
                                                      pattern=[[0, 4], [-1, 128]], compare_op=ALU.is_ge, fill=0.0, base=0, channel_multiplier=1)),
                     reads=[b_pt[pi]], writes=[b_pt[pi]])
            elif mask == "next":
                S.op("pool", (lambda e: e.affine_select(out=PT[pi][:].rearrange("p (g q) -> p g q", g=4), in_=PT[pi][:].rearrange("p (g q) -> p g q", g=4),
                                                      pattern=[[0, 4], [1, 128]], compare_op=ALU.is_ge, fill=0.0, base=0, channel_multiplier=-1)),
                     reads=[b_pt[pi]], writes=[b_pt[pi]])
            if flag is not None:
                S.op("pool", (lambda e: e.tensor_scalar(out=PT[pi][:, 0:N], in0=PT[pi][:, 0:N], scalar1=flags[:, flag:flag + 1], scalar2=None, op0=ALU.mult)),
                     reads=[b_pt[pi], b_flags], writes=[b_pt[pi]])
            return pi

        def emit_pv(u, k, pi):
            k_ap, k_bufs, v_ap, v_bufs, mask, flag = u.tiles[k]
            N = u.N
            nt = len(u.tiles)
            if k == 0:
                u.pO, u.bpO = ps(reserve=True)
            pO = u.pO
            S.op("pe", (lambda e: e.matmul(pO[0:66, 0:N], lhsT=v_ap, rhs=PT[pi][:, 0:N], start=(k == 0), stop=(k == nt - 1))),
                 reads=[b_pt[pi]] + list(v_bufs), writes=[u.bpO])
            if u.use_pd:
                if k == 0:
                    u.pD, u.bpD = ps(reserve=True)
                pD = u.pD
                S.op("pe", (lambda e: e.matmul(pD[0:64, 0:N], lhsT=onesb[:], rhs=PT[pi][:, 0:N], start=(k == 0), stop=(k == nt - 1))),
                     reads=[b_pt[pi], b_onesb], writes=[u.bpD])

        def norm_pd(u):
            pO, bpO, pD, bpD, N = u.pO, u.bpO, u.pD, u.bpD, u.N
            si = rot("sc", NSC)
            if u.sink_cols is None:
                S.op("act", (lambda e: e.activation(out=sc32[si][0:64, 0:N], in_=pD[0:64, 0:N], func=AF.Ln)), reads=[bpD], writes=[b_sc[si]])
            else:
                def lnsink(e):
                    for g in range(4):
                        ins = e.activation(out=sc32[si][0:64, g * 128:(g + 1) * 128], in_=pD[0:64, g * 128:(g + 1) * 128], func=AF.Ln,
                                           bias=esinkb[:, u.sink_cols + g:u.sink_cols + g + 1])
                    return ins
                S.op("act", lnsink, reads=[bpD, b_esinkb], writes=[b_sc[si]])
            S.op("act", (lambda e: e.activation(out=sc32[si][0:64, 0:N], in_=sc32[si][0:64, 0:N], func=AF.Exp, scale=-1.0)), reads=[b_sc[si]], writes=[b_sc[si]])
            S.op("dve", (lambda e: e.tensor_tensor(out=u.out_ap, in0=u.view(pO[0:64, 0:N]), in1=u.view(sc32[si][0:64, 0:N]), op=ALU.mult)),
                 reads=[bpO, b_sc[si]], writes=u.out_bufs)
            ps_release(pO)
            ps_release(pD)

        def norm_a(u):
            pO, bpO, N = u.pO, u.bpO, u.N
            ri = rot("rden", 2)
            u.ri = ri
            rden, b_rden = rden2[:, ri, :], b_rden2[ri]
            if u.sink_cols is None:
                S.op("dve", (lambda e: e.tensor_copy(out=rden[64:65, 0:N], in_=pO[64:65, 0:N])), reads=[bpO], writes=[b_rden])
            else:
                def addsink(e):
                    for g in range(4):
                        ins = e.tensor_scalar(out=rden[64:65, g * 128:(g + 1) * 128], in0=pO[64:65, g * 128:(g + 1) * 128],
                                              scalar1=esink[64:65, u.sink_cols + g:u.sink_cols + g + 1], scalar2=None, op0=ALU.add)
                    return ins
                S.op("dve", addsink, reads=[bpO, b_esink], writes=[b_rden])

        def norm_b(u):
            pO, bpO, N = u.pO, u.bpO, u.N
            rden, b_rden = rden2[:, u.ri, :], b_rden2[u.ri]
            pB, bpB = ps()
            S.op("pe", (lambda e: e.matmul(pB[0:64, 0:N], lhsT=ones32[64:65, 0:64], rhs=rden[64:65, 0:N], start=True, stop=True)), reads=[b_ones, b_rden], writes=[bpB])
            si = rot("sc", NSC)
            S.op("act", (lambda e: e.activation(out=sc32[si][0:64, 0:N], in_=pB[0:64, 0:N], func=AF.Ln)), reads=[bpB], writes=[b_sc[si]])
            S.op("act", (lambda e: e.activation(out=sc32[si][0:64, 0:N], in_=sc32[si][0:64, 0:N], func=AF.Exp, scale=-1.0)), reads=[b_sc[si]], writes=[b_sc[si]])
            S.op("dve", (lambda e: e.tensor_tensor(out=u.out_ap, in0=u.view(pO[0:64, 0:N]), in1=u.view(sc32[si][0:64, 0:N]), op=ALU.mult)),
                 reads=[bpO, b_sc[si]], writes=u.out_bufs)
            ps_release(pO)

        def do_back():
            u, k, pi = backq.popleft()
            emit_pv(u, k, pi)
            for ent in list(normq):
                ent[1] -= 1
                if ent[1] <= 0:
                    norm_b(ent[0])
                    normq.remove(ent)
            if k == len(u.tiles) - 1 and u.use_pd:
                norm_pd(u)
            elif k == len(u.tiles) - 1:
                while len(normq) > 1:
                    norm_b(normq[0][0])
                    normq.pop(0)
                norm_a(u)
                normq.append([u, 3])

        for ent in front:
            if ent[0] == "call":
                ent[1]()
                continue
            _, u, k = ent
            pi = emit_st(u, k)
            backq.append((u, k, pi))
            if len(backq) > LOOK:
                do_back()
        while backq:
            do_back()
        for ent in normq:
            norm_b(ent[0])

    def attention(kind):
        sample = (kind == "S")
        v4 = lambda ap: ap.rearrange("p (g q) -> p g q", g=4)
        ident_v = lambda ap: ap
        front = []

        def add_unit(q_ap, q_bufs, N, tiles, sink_cols, out_ap, out_bufs, view):
            u = Unit()
            u.q_ap, u.q_bufs, u.N, u.tiles, u.sink_cols, u.out_ap, u.out_bufs, u.view = q_ap, q_bufs, N, tiles, sink_cols, out_ap, out_bufs, view
            u.use_pd = (not sample) or (sink_cols is not None)
            for k in range(len(tiles)):
                front.append(("st", u, k))
            return u

        s_uq = wslot()
        uq = wsl[s_uq][:].rearrange("p k c -> p (k c)")[:, 0:1536].rearrange("p (k c) -> p k c", k=2)
        S.dma("pool", lambda e: e.dma_start(out=uq, in_=w_uq.rearrange("(k p) c -> p k c", p=128)), key="ws%d" % s_uq, writes=[b_ws[s_uq]])
        s_ukv = wslot()
        ukv = wsl[s_ukv][:].rearrange("p k c -> p (k c)").rearrange("p (k c) -> p k c", k=2)
        S.dma("pool", lambda e: e.dma_start(out=ukv, in_=w_ukv.rearrange("(k p) c -> p k c", p=128)), key="ws%d" % s_ukv, writes=[b_ws[s_ukv]])
        if sample:
            regs = [("ctx", 4), ("own", 8), ("oth", 8)]
            nkt = 20
            kbase = 0
        else:
            regs = [("own", 8)]
            nkt = 8
            kbase = K_OWN
        all_kr = [b_kr[r] for r, _ in regs]
        all_ckv = [b_ckv[r] for r, _ in regs]
        vt_base = 0 if sample else 4

        def prep_rope_rows():
            for i in range(2):
                S.op("act", (lambda e, i=i: e.activation(out=KbT[i][64:96, kbase:kbase + nkt * 128], in_=KrT[64:96, kbase:kbase + nkt * 128], func=AF.Copy)),
                     reads=all_kr, writes=[b_kbr[i]])

        def prep(h):
            kb = h % 2
            for c in range(nkt // 4):
                p1, bp1 = ps()
                col0 = kbase + c * 512

                def mmk(e, p1=p1, col0=col0):
                    for k2 in range(2):
                        ins = e.matmul(p1[0:64, :], lhsT=ukv[:, k2, h * 128:h * 128 + 64], rhs=CkvT[:, k2, col0:col0 + 512], start=(k2 == 0), stop=(k2 == 1))
                    return ins
                S.op("pe", mmk, reads=[b_ws[s_ukv]] + all_ckv, writes=[bp1])
                if c % 2 == 0:
                    S.op("dve", (lambda e, p1=p1, col0=col0: e.tensor_copy(out=KbT[kb][0:64, col0:col0 + 512], in_=p1[0:64, :])), reads=[bp1], writes=[b_kb[kb]])
                else:
                    S.op("pool" if False else "dve", (lambda e, p1=p1, col0=col0: e.tensor_copy(out=KbT[kb][0:64, col0:col0 + 512], in_=p1[0:64, :])), reads=[bp1], writes=[b_kb[kb]])
            for c0 in range(0, nkt, 8):
                nn = min(8, nkt - c0)
                p1, bp1 = ps()

                def mmv(e, p1=p1, c0=c0, nn=nn):
                    for i in range(nn):
                        col0 = kbase + (c0 + i) * 128
                        for k2 in range(2):
                            ins = e.matmul(p1[:, i * 64:(i + 1) * 64], lhsT=CkvT[:, k2, col0:col0 + 128], rhs=ukv[:, k2, h * 128 + 64:h * 128 + 128], start=(k2 == 0), stop=(k2 == 1))
                    return ins
                S.op("pe", mmv, reads=[b_ws[s_ukv]] + all_ckv, writes=[bp1])
                S.op("dve", (lambda e, p1=p1, c0=c0, nn=nn: e.tensor_copy(out=Vb[kb][:, vt_base + c0:vt_base + c0 + nn, 0:64], in_=p1[:, 0:nn * 64].rearrange("p (n d) -> p n d", d=64))),
                     reads=[bp1], writes=[b_vb[kb]])
            for tt in range(2):
                p1, bp1 = ps()

                def mmq(e, p1=p1, tt=tt):
                    for k2 in range(2):
                        ins = e.matmul(p1[0:96, :], lhsT=uq[:, k2, h * 96:(h + 1) * 96], rhs=qlT[:, k2, tt * 512:(tt + 1) * 512], start=(k2 == 0), stop=(k2 == 1))
                    return ins
                S.op("pe", mmq, reads=[b_ws[s_uq]] + b_ql[tt * 4:(tt + 1) * 4], writes=[bp1])
                rope_evac(p1, bp1, 96, B_SCALE, 0, tt * 512, QbT[kb][0:96, tt * 512:(tt + 1) * 512], [b_qb[kb]], sample)

        a_units = 0
        if sample:
            for j in (1, 2, 3, 4, 5, 6, 0, 7):
                for kv in range(2):
                    ks = slice(kv * 64, (kv + 1) * 64)
                    tiles = []
                    for n in range(4):
                        tiles.append((KaT[ks, K_CTX + n * 128:K_CTX + (n + 1) * 128], [b_ka["ctx"]], Va[:, n, kv, 0:66], [b_va["ctx"], b_va1], None, None))
                    tiles.append((KaT[ks, K_OWN + j * 128:K_OWN + (j + 1) * 128], [b_ka["own"]], Va[:, 4 + j, kv, 0:66], [b_va["own"], b_va1], None, None))
                    if j > 0:
                        tiles.append((KaT[ks, K_OWN + (j - 1) * 128:K_OWN + j * 128], [b_ka["own"]], Va[:, 4 + j - 1, kv, 0:66], [b_va["own"], b_va1], "prev", None))
                    else:
                        tiles.append((KaT[ks, K_OTH + 7 * 128:K_OTH + 8 * 128], [b_ka["oth"]], Va[:, 12 + 7, kv, 0:66], [b_va["oth"], b_va1], "prev", 0))
                    if j < 7:
                        tiles.append((KaT[ks, K_OWN + (j + 1) * 128:K_OWN + (j + 2) * 128], [b_ka["own"]], Va[:, 4 + j + 1, kv, 0:66], [b_va["own"], b_va1], "next", None))
                    else:
                        tiles.append((KaT[ks, K_OTH:K_OTH + 128], [b_ka["oth"]], Va[:, 12, kv, 0:66], [b_va["oth"], b_va1], "next", 1))
                    add_unit(QaT[ks, :, j * 128:(j + 1) * 128], b_qa, 512, tiles, kv * 4,
                             oaT[0:64, kv * 4:(kv + 1) * 4, j * 128:(j + 1) * 128], b_oa[kv * 4:(kv + 1) * 4], v4)
                    a_units += 1
                    if a_units == 8:
                        front.append(("call", prep_rope_rows))
                        front.append(("call", (lambda: prep(0))))
        else:
            for sq in range(4):
                for qh in range(2):
                    for kv in range(2):
                        ks = slice(kv * 64, (kv + 1) * 64)
                        q0 = sq * 256 + qh * 128
                        tiles = []
                        for kt in range(2):
                            kk = sq * 2 + kt
                            tiles.append((KaT[ks, K_OWN + kk * 128:K_OWN + (kk + 1) * 128], [b_ka["own"]], Va[:, 4 + kk, kv, 0:66], [b_va["own"], b_va1], None, None))
                        add_unit(QaT[ks, :, q0:q0 + 128], b_qa, 512, tiles, kv * 4,
                                 oaT[0:64, kv * 4:(kv + 1) * 4, q0:q0 + 128], b_oa[kv * 4:(kv + 1) * 4], v4)
                        a_units += 1
                        if a_units == DBG.get("pcall", 8):
                            front.append(("call", prep_rope_rows))
                            front.append(("call", (lambda: prep(0))))
        for h in range(8):
            kb = h % 2
            start_idx = len(front)
            if sample:
                for tt in range(2):
                    tiles = []
                    for n in range(20):
                        tiles.append((KbT[kb][0:96, n * 128:(n + 1) * 128], [b_kb[kb], b_kbr[kb]], Vb[kb][:, n, 0:66], [b_vb[kb], b_vb1[kb]], None, None))
                    add_unit(QbT[kb][0:96, tt * 512:(tt + 1) * 512], [b_qb[kb]], 512, tiles, None, obT[0:64, h, tt * 512:(tt + 1) * 512], [b_ob[h]], ident_v)
            else:
                for sq in range(4):
                    tiles = []
                    for kt in range(2):
                        kk = sq * 2 + kt
                        tiles.append((KbT[kb][0:96, K_OWN + kk * 128:K_OWN + (kk + 1) * 128], [b_kb[kb], b_kbr[kb]], Vb[kb][:, 4 + kk, 0:66], [b_vb[kb], b_vb1[kb]], None, None))
                    add_unit(QbT[kb][0:96, sq * 256:(sq + 1) * 256], [b_qb[kb]], 256, tiles, None, obT[0:64, h, sq * 256:(sq + 1) * 256], [b_ob[h]], ident_v)
            if h + 1 < 8:
                front.insert(start_idx + LOOK + 2, ("call", (lambda h=h: prep(h + 1))))
        run_attention_stream(front)

    def merge(m):
        for i in range(4):
            S.dma("sp", (lambda e, i=i: e.dma_start(out=oaT2[64:128, 2 * i, :], in_=oaT2[0:64, 2 * i + 1, :])), key="pa%d" % i, reads=[b_oa[2 * i + 1]], writes=[b_oa2[i]])
            S.dma("sp", (lambda e, i=i: e.dma_start(out=obT2[64:128, 2 * i, :], in_=obT2[0:64, 2 * i + 1, :])), key="pb%d" % i, reads=[b_ob[2 * i + 1]], writes=[b_ob2[i]])

        def load_c(c):
            sg = wslot()
            S.dma("pool", (lambda e: e.dma_start(out=wsl[sg][:, :, 0:128], in_=w_in[:, 1312 + c * 128:1312 + (c + 1) * 128].rearrange("(kc p) n -> p kc n", p=128))),
                  key="ws%d" % sg, writes=[b_ws[sg]])
            S.dma("pool", (lambda e: e.dma_start(out=wsl[sg][:, :, 128:256], in_=w_in[:, 2336 + c * 128:2336 + (c + 1) * 128].rearrange("(kc p) n -> p kc n", p=128))),
                  key="ws%d" % sg, writes=[b_ws[sg]])
            so = wslot()
            S.dma("pool", (lambda e: e.dma_start(out=wsl[so][:, 0:4, 0:128], in_=w_oa.rearrange("(hp p) n -> p hp n", p=128)[:, :, c * 128:(c + 1) * 128])),
                  key="ws%d" % so, writes=[b_ws[so]])
            S.dma("pool", (lambda e: e.dma_start(out=wsl[so][:, 0:4, 128:256], in_=w_ob.rearrange("(hp p) n -> p hp n", p=128)[:, :, c * 128:(c + 1) * 128])),
                  key="ws%d" % so, writes=[b_ws[so]])
            return sg, so
        nxt = load_c(0)
        for c in range(8):
            sg, so = nxt
            if c + 1 < 8:
                nxt = load_c(c + 1)
            for tt in range(2):
                tsl = slice(tt * 512, (tt + 1) * 512)
                res = []
                for br in range(2):
                    pg, bpg = ps()

                    def mmg(e, pg=pg, br=br, sg=sg, tsl=tsl):
                        for kc in range(8):
                            ins = e.matmul(pg[:], lhsT=wsl[sg][:, kc, br * 128:(br + 1) * 128], rhs=hT[:, kc, tsl], start=(kc == 0), stop=(kc == 7))
                        return ins
                    S.op("pe", mmg, reads=[b_ws[sg]] + BH(tt * 4, (tt + 1) * 4), writes=[bpg])
                    pp, bpp = ps()
                    oT, bo = (oaT2, b_oa + b_oa2) if br == 0 else (obT2, b_ob + b_ob2)

                    def mmo(e, pp=pp, br=br, so=so, oT=oT, tsl=tsl):
                        for hp in range(4):
                            ins = e.matmul(pp[:], lhsT=wsl[so][:, hp, br * 128:(br + 1) * 128], rhs=oT[:, 2 * hp, tsl], start=(hp == 0), stop=(hp == 3))
                        return ins
                    S.op("pe", mmo, reads=[b_ws[so]] + bo, writes=[bpp])
                    si = rot("sc", NSC)
                    S.op("act", (lambda e, pg=pg, si=si: e.activation(out=sc32[si][:], in_=pg[:], func=AF.Sigmoid)), reads=[bpg], writes=[b_sc[si]])
                    S.op("dve", (lambda e, pp=pp, si=si: e.tensor_tensor(out=sc32[si][:], in0=sc32[si][:], in1=pp[:], op=ALU.mult)), reads=[b_sc[si], bpp], writes=[b_sc[si]])
                    res.append(si)
                S.op("dve", (lambda e, res=res, c=c, tsl=tsl: e.tensor_tensor(out=mT[:, c, tsl], in0=sc32[res[0]][:], in1=sc32[res[1]][:], op=ALU.add)),
                     reads=[b_sc[res[0]], b_sc[res[1]]], writes=[b_m[c][tt]])
        nxt = load_w_cols(w_out, 0, 256)
        for cq in range(4):
            s = nxt
            if cq + 1 < 4:
                nxt = load_w_cols(w_out, (cq + 1) * 256, 256)
            for t in range(8):
                pd, bpd = ps()

                def mmw(e, pd=pd, s=s, t=t):
                    for kc in range(8):
                        ins = e.matmul(pd[:, 0:256], lhsT=mT[:, kc, t * 128:(t + 1) * 128], rhs=wsl[s][:, kc, 0:256], start=(kc == 0), stop=(kc == 7))
                    return ins
                S.op("pe", mmw, reads=[b_ws[s]] + [b_m[kc][t // 4] for kc in range(8)], writes=[bpd])
                resid_add(t, cq * 256, 256, pd, bpd)

    def exchange_kv():
        snda = snd.ap()
        rcva = rcv.ap()
        b_snd = [Buf("snd%d" % i) for i in range(5)]
        b_rcv = Buf("rcv")
        S.dma("sp", lambda e: e.dma_start(out=snda[:, 0:1024], in_=KaT[:, K_OWN:K_OWN + 1024]), key="xs0", reads=[b_ka["own"]], writes=[b_snd[0]])
        for kv in range(2):
            S.dma("sp", (lambda e, kv=kv: e.dma_start(out=snda[:, 1024:2048].rearrange("p (n k d) -> p n k d", k=2, d=64)[:, :, kv, :], in_=Va[:, 4:12, kv, 0:64])),
                  key="xs%d" % (1 + kv), reads=[b_va["own"]], writes=[b_snd[1 + kv]])
        S.dma("sp", lambda e: e.dma_start(out=snda[:, 2048:4096].rearrange("p (k t) -> p k t", k=2), in_=CkvT[:, :, K_OWN:K_OWN + 1024]), key="xs3", reads=[b_ckv["own"]], writes=[b_snd[3]])
        S.dma("sp", lambda e: e.dma_start(out=kr_snd.ap(), in_=KrT[64:96, K_OWN:K_OWN + 1024]), key="xs4", reads=[b_kr["own"]], writes=[b_snd[4]])
        b_krr = Buf("kr_rcv")
        S.dma("pool", lambda e: e.collective_compute("AllGather", ALU.bypass, replica_groups=[[0, 1], [2, 3], [4, 5], [6, 7]],
                                                     ins=[kr_snd.ap().opt()], outs=[kr_rcv.ap().opt()]),
              key="ag2", reads=[b_snd[4]], writes=[b_krr], inc=1)
        S.dma("pool", lambda e: e.collective_compute("AllGather", ALU.bypass, replica_groups=[[0, 1], [2, 3], [4, 5], [6, 7]],
                                                     ins=[snd.ap().opt()], outs=[rcv.ap().opt()]),
              key="ag", reads=b_snd[0:4], writes=[b_rcv], inc=1)
        S.dma("sp", lambda e: e.dma_start(out=KaT[:, K_OTH + 896:K_OTH + 1024], in_=rcva[0:128, 896:1024]), key="xl0", reads=[b_rcv], writes=[b_ka["oth"]])
        S.dma("sp", lambda e: e.dma_start(out=KaT[:, K_OTH:K_OTH + 128], in_=rcva[128:256, 0:128]), key="xl0", reads=[b_rcv], writes=[b_ka["oth"]])
        S.dma("sp", lambda e: e.dma_start(out=Va[:, 19, :, 0:64], in_=rcva[0:128, 1920:2048].rearrange("p (k d) -> p k d", k=2)), key="xl1", reads=[b_rcv], writes=[b_va["oth"]])
        S.dma("sp", lambda e: e.dma_start(out=Va[:, 12, :, 0:64], in_=rcva[128:256, 1024:1152].rearrange("p (k d) -> p k d", k=2)), key="xl1", reads=[b_rcv], writes=[b_va["oth"]])
        S.dma("sp", lambda e: e.dma_start(out=CkvT[:, :, K_OWN:K_OWN + 1024], in_=rcva[0:128, 2048:4096].rearrange("p (k t) -> p k t", k=2)), key="xl2", reads=[b_rcv], writes=[b_ckv["own"]])
        S.dma("sp", lambda e: e.dma_start(out=CkvT[:, :, K_OTH:K_OTH + 1024], in_=rcva[128:256, 2048:4096].rearrange("p (k t) -> p k t", k=2)), key="xl3", reads=[b_rcv], writes=[b_ckv["oth"]])
        S.dma("sp", lambda e: e.dma_start(out=KrT[64:96, K_OWN:K_OWN + 1024], in_=kr_rcv.ap()[0:32, :]), key="xl4", reads=[b_krr], writes=[b_kr["own"]])
        S.dma("sp", lambda e: e.dma_start(out=KrT[64:96, K_OTH:K_OTH + 1024], in_=kr_rcv.ap()[32:64, :]), key="xl5", reads=[b_krr], writes=[b_kr["oth"]])

    ffn_bufs = [b for row in b_act for b in row] + b_wd
    att1_bufs = b_oa + b_ob + b_kb + b_kbr + b_oa2 + b_ob2
    att2_bufs = b_qa + list(b_ka.values()) + list(b_va.values()) + [b_va1]
    m_bufs = [b for row in b_m for b in row]

    def set_va_ones(t0, t1):
        S.op("pool", (lambda e: e.memset(Va[:, t0:t1, :, 64:66], 1.0)), writes=[b_va1])

    def program(stage):
        for i in range(2):
            S.op("pool", (lambda e, i=i: e.memset(Vb[i][:, :, 64:66], 1.0)), writes=[b_vb1[i]])
        load_x(1)
        S.dma("sp", lambda e: e.dma_start(out=ropA[:], in_=ropeA[0].rearrange("c p t -> p c t")), key="ropA", writes=[b_ropA])
        S.dma("sp", lambda e: e.dma_start(out=ropB[:], in_=ropeB[0].rearrange("c p t -> p c t")), key="ropB", writes=[b_ropB])
        ffn(0, 0, 0, hook=(lambda: (ada_chunk(), ada_chunk())))
        while ada_state["cc"] < 36:
            ada_chunk()
        set_va_ones(0, 20)
        load_ctx()
        norm_to_hT(1, 0, gate=(1, 0, False))
        mix_proj("S", "own", 0, True, True, False)
        exchange_kv()
        if stage == 2:
            final_out(0)
            return
        fence(ffn_bufs + att1_bufs)
        attention("S")
        if stage == 3:
            final_out(0)
            return
        fence(att2_bufs + m_bufs)
        merge(0)
        if stage == 4:
            final_out(0)
            return
        fence(att1_bufs + ffn_bufs)
        ffn(1, 2, 0)
        final_out(0)
        if stage == 5:
            return
        load_x(2)
        ffn(0, 0, 1)
        norm_to_hT(1, 1, gate=(1, 1, False))
        fence(m_bufs + att2_bufs)
        set_va_ones(4, 12)
        mix_proj("P", "own", 0, False, True, True)
        if stage == 6:
            final_out(1)
            return
        fence(ffn_bufs + att1_bufs)
        attention("P")
        if stage == 7:
            final_out(1)
            return
        fence(att2_bufs + m_bufs)
        merge(1)
        fence(att1_bufs + ffn_bufs)
        ffn(1, 2, 1)
        final_out(1)


    program(DBG.get("stage", 99))

    S.wait_all("sp", out_ops)
    S.emit(nc, st)
    st.close()
    return nc


_NC = {}


def _col(v, n):
    return np.ascontiguousarray(np.asarray(v, np.float32).reshape(n, 128).T)


def make_in_maps(inp):
    c = _consts()
    f = lambda a: np.ascontiguousarray(np.asarray(a, dtype=np.float32))
    x_prompt, x_sample = f(inp["x_prompt"]), f(inp["x_sample"])
    shared = dict(
        ada_w=f(inp["ada_w"][0]), ada_bc=_col(inp["ada_b"][0], 72),
        ncol=np.concatenate([_col(inp["ffn1_norm"][0], 8), _col(inp["mix_norm"][0], 8), _col(inp["ffn2_norm"][0], 8)], axis=1),
        fnorm=f(inp["final_norm"]).reshape(1, D),
        w_gu1=f(inp["ffn1_w_gu"][0]), w_d1=f(inp["ffn1_w_down"][0]), w_gu2=f(inp["ffn2_w_gu"][0]), w_d2=f(inp["ffn2_w_down"][0]),
        w_in=f(inp["w_in"][0]),
        lncol=np.concatenate([_col(inp["q_lat_norm"][0], 2), _col(inp["kv_lat_norm"][0], 2)], axis=1),
        kvln=f(inp["kv_lat_norm"][0]).reshape(1, 256),
        w_uq=f(inp["w_uq"][0]), w_ukv=f(inp["w_ukv"][0]), w_oa=f(inp["w_o_a"][0]), w_ob=f(inp["w_o_b"][0]), w_out=f(inp["w_out"][0]),
        ident=c["ident"], RA=c["RA"], RB=c["RB"],
    )
    sink65 = np.zeros((65, 8), np.float32)
    sink65[64] = f(inp["attn_sink"][0])
    shared["sink65"] = sink65
    maps = []
    for core in range(8):
        b, h = core // 2, core % 2
        xg = np.stack([x_sample[b, (1 - h) * 1024:(2 - h) * 1024], x_sample[b, h * 1024:(h + 1) * 1024],
                       x_prompt[4 * core:4 * core + 4].reshape(1024, D)], axis=0)
        cvec = np.stack([f(inp["c"])[b], f(inp["c_ctx"])], axis=0)
        cvfm = np.ascontiguousarray(cvec.reshape(2, 8, 128).transpose(2, 1, 0).reshape(128, 16))
        flags = np.zeros((128, 2), np.float32)
        flags[:, 0] = float(h)
        flags[:, 1] = float(1 - h)
        d = dict(shared)
        d.update(
            xg=np.ascontiguousarray(xg), cvfm=cvfm,
            ck=f(inp["cache_attn_k"][b, 0]).reshape(512, 128), cv=f(inp["cache_attn_v"][b, 0]).reshape(512, 128),
            cckv=f(inp["cache_mla_ckv"][b, 0]), ckr=f(inp["cache_mla_krope"][b, 0]),
            ropeA=np.ascontiguousarray(c["ropeA"][[h, 1 - h]]), ropeB=np.ascontiguousarray(c["ropeB"][[h, 1 - h]]),
            flags=flags,
        )
        maps.append(d)
    return maps


def kernel(**inputs):
    dbg = tuple(DBG.get("dbg", ()))
    key = (dbg, DBG.get("stage", 99), DBG.get("po", 7), DBG.get("pcall", 8))
    if key not in _NC:
        _NC[key] = build_nc(dbg)
    nc = _NC[key]
    maps = make_in_maps(inputs)
    res = run_bass_kernel_spmd(nc, maps, core_ids=list(range(8)))
    R = res.results
    DBG["results"] = R
    y_prompt = np.concatenate([R[c]["y"][1].reshape(4, 256, D) for c in range(8)], axis=0)
    y_sample = np.stack([np.concatenate([R[2 * b]["y"][0], R[2 * b + 1]["y"][0]], axis=0) for b in range(4)], axis=0)
    nk = np.concatenate([R[c]["nk"].reshape(4, 1, 256, 2, 64) for c in range(8)], axis=0)
    nv = np.concatenate([R[c]["nv"].reshape(4, 1, 256, 2, 64) for c in range(8)], axis=0)
    nckv = np.concatenate([R[c]["nckv"].reshape(4, 1, 256, 256) for c in range(8)], axis=0)
    nkr = np.concatenate([R[c]["nkr"].reshape(4, 1, 256, 32) for c in range(8)], axis=0)
    return (y_prompt.astype(np.float32), y_sample.astype(np.float32), nk.astype(np.float32), nv.astype(np.float32),
            nckv.astype(np.float32), nkr.astype(np.float32))
```

```python
import numpy as np
from contextlib import ExitStack
import concourse.bass as bass
import concourse.mybir as mybir
from concourse.bass_utils import run_bass_kernel_spmd

F32 = mybir.dt.float32
BF16 = mybir.dt.bfloat16
AF = mybir.ActivationFunctionType
ALU = mybir.AluOpType

D = 1024
DFF = 2816
NCH = 22
NBLK = 2
CPB = NCH // NBLK
INW = 3360
EPS = 1e-6
A_SCALE = 64 ** -0.5
B_SCALE = 96 ** -0.5
NKEY = 2560
K_CTX, K_OWN, K_OTH = 0, 512, 1536

ENGS = ("pe", "act", "dve", "pool", "sp")


class Buf:
    __slots__ = ("name", "w", "r", "rd")

    def __init__(self, name=""):
        self.name = name
        self.w = None
        self.r = {}
        self.rd = []


class Op:
    __slots__ = ("eng", "fn", "deps", "needs_inc", "val", "sem", "is_dma", "key", "inc")

    def __init__(self, eng, fn, is_dma=False):
        self.eng = eng
        self.fn = fn
        self.deps = []
        self.needs_inc = False
        self.val = None
        self.sem = None
        self.is_dma = is_dma
        self.key = None


class Sched:
    def __init__(self):
        self.ops = {e: [] for e in ENGS}
        self.dma_cnt = {}

    def _add(self, o, reads, writes):
        deps = {}

        def add(d):
            if d is None:
                return
            if (not d.is_dma) and (not o.is_dma) and d.eng == "pe" and o.eng == "pe":
                return
            deps[id(d)] = d
        for b in reads:
            add(b.w)
        for b in writes:
            add(b.w)
            for d in b.r.values():
                add(d)
            for d in b.rd:
                add(d)
        o.deps = list(deps.values())
        for d in o.deps:
            d.needs_inc = True
        for b in reads:
            if o.is_dma:
                b.rd.append(o)
            else:
                b.r[o.eng] = o
        for b in writes:
            b.w = o
            b.r = {}
            b.rd = []
        self.ops[o.eng].append(o)
        return o

    def op(self, eng, fn, reads=(), writes=()):
        return self._add(Op(eng, fn), reads, writes)

    def dma(self, eng, fn, key, reads=(), writes=(), inc=16):
        o = Op(eng, fn, is_dma=True)
        o.key = key
        o.inc = inc
        c = self.dma_cnt.get(key, 0) + 1
        self.dma_cnt[key] = c
        o.val = inc * c
        o.needs_inc = True
        return self._add(o, reads, writes)

    def wait_all(self, eng, ops):
        o = Op(eng, None)
        o.deps = list(ops)
        for d in o.deps:
            d.needs_inc = True
        self.ops[eng].append(o)
        return o

    def emit(self, nc, stack):
        esem = {e: stack.enter_context(nc.semaphore("s_" + e)) for e in ENGS}
        dsem = {}
        for i, k in enumerate(self.dma_cnt):
            dsem[k] = stack.enter_context(nc.semaphore("d%d" % i))
        for e in ENGS:
            c = 0
            for o in self.ops[e]:
                if o.is_dma:
                    o.sem = dsem[o.key]
                elif o.needs_inc:
                    c += 1
                    o.val = c
                    o.sem = esem[e]

        def run(e, eng):
            known = {}
            for o in self.ops[e]:
                need = {}
                for d in o.deps:
                    k = id(d.sem)
                    if k not in need or need[k][1] < d.val:
                        need[k] = (d.sem, d.val)
                for k, (sem, val) in need.items():
                    if known.get(k, 0) >= val:
                        continue
                    eng.wait_ge(sem, val)
                    known[k] = val
                if o.fn is None:
                    continue
                ins = o.fn(eng)
                if o.is_dma:
                    if o.inc == 16:
                        ins.then_inc(o.sem, 16)
                    else:
                        ins.then_inc(o.sem)
                elif o.needs_inc:
                    ins.then_inc(o.sem, 1)

        with nc.Block() as block:
            @block.tensor
            def _(eng):
                run("pe", eng)

            @block.scalar
            def _(eng):
                run("act", eng)

            @block.vector
            def _(eng):
                run("dve", eng)

            @block.gpsimd
            def _(eng):
                run("pool", eng)

            @block.sync
            def _(eng):
                run("sp", eng)


def _rope_tables(n, d_rot):
    rows = n // 64
    t_row = np.repeat(np.arange(rows, dtype=np.float32), 64)
    t_col = np.tile(np.arange(64, dtype=np.float32), rows)
    d_half = d_rot // 2
    inv = (1.0 / (np.float32(10000.0) ** (np.arange(0, d_half, 2, dtype=np.float32) / np.float32(d_half)))).astype(np.float32)
    ar = t_row[:, None] * inv[None, :]
    ac = t_col[:, None] * inv[None, :]
    ang = np.concatenate([ar, ar, ac, ac], axis=-1).astype(np.float32)
    return np.cos(ang).astype(np.float32), np.sin(ang).astype(np.float32)


def _rot_T(d_rot):
    s = d_rot // 4
    R = np.zeros((d_rot, d_rot), np.float32)
    for i in range(s):
        R[i, i + s] = -1.0
        R[i + s, i] = 1.0
        R[i + 2 * s, i + 3 * s] = -1.0
        R[i + 3 * s, i + 2 * s] = 1.0
    return np.ascontiguousarray(R.T)


_CONST = {}


def _consts():
    if _CONST:
        return _CONST
    cosA, sinA = _rope_tables(2048, 64)
    cosB, sinB = _rope_tables(2048, 32)
    ropeA = np.zeros((2, 2, 128, 1024), np.float32)
    ropeB = np.zeros((2, 2, 96, 1024), np.float32)
    for hh in range(2):
        sl = slice(hh * 1024, (hh + 1) * 1024)
        ropeA[hh, 0] = np.concatenate([cosA[sl].T, cosA[sl].T], 0)
        ropeA[hh, 1] = np.concatenate([sinA[sl].T, sinA[sl].T], 0)
        ropeB[hh, 0, :64] = 1.0
        ropeB[hh, 0, 64:] = cosB[sl].T
        ropeB[hh, 1, 64:] = sinB[sl].T
    RA = np.zeros((128, 128), np.float32)
    r64 = _rot_T(64)
    RA[:64, :64] = r64
    RA[64:, 64:] = r64
    RB = np.zeros((96, 96), np.float32)
    RB[64:, 64:] = _rot_T(32)
    _CONST.update(ropeA=ropeA, ropeB=ropeB, RA=RA, RB=RB, ident=np.eye(128, dtype=np.float32))
    return _CONST


DBG = {}


def build_nc(dbg=()):
    nc = bass.Bass("TRN2", target_bir_lowering=False)
    S = Sched()

    def din(name, shape):
        return nc.dram_tensor(name, list(shape), F32, kind="ExternalInput").ap()

    def dout(name, shape):
        return nc.dram_tensor(name, list(shape), F32, kind="ExternalOutput").ap()

    xg = din("xg", [3, 1024, D])
    cvfm = din("cvfm", [128, 16])
    ck = din("ck", [512, 128])
    cv = din("cv", [512, 128])
    cckv = din("cckv", [512, 256])
    ckr = din("ckr", [512, 32])
    ada_w = din("ada_w", [D, 9 * D])
    ada_bc = din("ada_bc", [128, 72])
    ncol = din("ncol", [128, 24])
    fnorm = din("fnorm", [1, D])
    w_gu = [din("w_gu1", [D, 2 * DFF]), din("w_gu2", [D, 2 * DFF])]
    w_dn = [din("w_d1", [DFF, D]), din("w_d2", [DFF, D])]
    w_in = din("w_in", [D, INW])
    sink65 = din("sink65", [65, 8])
    lncol = din("lncol", [128, 4])
    kvln = din("kvln", [1, 256])
    w_uq = din("w_uq", [256, 768])
    w_ukv = din("w_ukv", [256, 1024])
    w_oa = din("w_oa", [512, D])
    w_ob = din("w_ob", [512, D])
    w_out = din("w_out", [D, D])
    identd = din("ident", [128, 128])
    ropeA = din("ropeA", [2, 2, 128, 1024])
    ropeB = din("ropeB", [2, 2, 96, 1024])
    RAd = din("RA", [128, 128])
    RBd = din("RB", [96, 96])
    flagsd = din("flags", [128, 2])

    y = dout("y", [2, 1024, D])
    nk = dout("nk", [1024, 128])
    nv = dout("nv", [1024, 128])
    nckv = dout("nckv", [1024, 256])
    nkr = dout("nkr", [1024, 32])
    dbg_out = {}
    for name, shape in dbg:
        dbg_out[name] = dout(name, shape)

    XW = 4096
    snd = nc.dram_tensor("kv_snd", [128, XW], BF16)
    rcv = nc.dram_tensor("kv_rcv", [256, XW], BF16)
    kr_snd = nc.dram_tensor("kr_snd", [32, 1024], BF16)
    kr_rcv = nc.dram_tensor("kr_rcv", [64, 1024], BF16)
    st = ExitStack()
    out_ops = []

    def sb(name, shape, dt=F32):
        return st.enter_context(nc.sbuf_tensor(name, list(shape), dt))

    x = sb("x", [128, 8, D]);                 bx = [Buf("x%d" % t) for t in range(8)]
    hT = sb("hT", [128, 8, 1024], BF16);      bhA = [Buf("hTa%d" % t) for t in range(8)]; bhD = [Buf("hTd%d" % t) for t in range(8)]

    def BH(t0, t1):
        return bhA[t0:t1] + bhD[t0:t1]
    ar1 = sb("ar1", [128, 22528], BF16)
    actT = ar1[:, 0:CPB * 1024].rearrange("p (j t) -> p j t", j=CPB)
    wd = ar1[:, CPB * 1024:2 * CPB * 1024].rearrange("p (j d) -> p j d", j=CPB)
    oaT = ar1[0:65, 0:8192].rearrange("p (h t) -> p h t", h=8)
    obT = ar1[0:65, 8192:16384].rearrange("p (h t) -> p h t", h=8)
    oaT2 = ar1[:, 0:8192].rearrange("p (h t) -> p h t", h=8)
    obT2 = ar1[:, 8192:16384].rearrange("p (h t) -> p h t", h=8)
    b_oa2 = [Buf("oa2_%d" % i) for i in range(4)]
    b_ob2 = [Buf("ob2_%d" % i) for i in range(4)]
    KbT = [ar1[0:96, 16384 + i * NKEY:16384 + (i + 1) * NKEY] for i in range(2)]
    b_act = [[Buf("act%d_%d" % (j, tt)) for tt in range(2)] for j in range(CPB)]
    b_wd = [Buf("wd%d" % j) for j in range(CPB)]
    b_oa = [Buf("oa%d" % h) for h in range(8)]
    b_ob = [Buf("ob%d" % h) for h in range(8)]
    b_kb = [Buf("kb%d" % i) for i in range(2)]
    b_kbr = [Buf("kbr%d" % i) for i in range(2)]
    NSLOT = 4
    wsl = [sb("wsl%d" % i, [128, 8, 256], BF16) for i in range(NSLOT)]
    b_ws = [Buf("ws%d" % i) for i in range(NSLOT)]
    ar2 = sb("ar2", [128, 9472], BF16)
    QaT = ar2[:, 0:4096].rearrange("p (g t) -> p g t", g=4)
    KaT = ar2[:, 4096:4096 + NKEY]
    Va = ar2[:, 6656:6656 + 20 * 132].rearrange("p (n k e) -> p n k e", n=20, k=2)
    mT = ar2[:, 0:8192].rearrange("p (c t) -> p c t", c=8)
    b_qa = [Buf("qa%d" % g) for g in range(4)]
    b_ka = {k: Buf("ka_" + k) for k in ("ctx", "own", "oth")}
    b_va = {k: Buf("va_" + k) for k in ("ctx", "own", "oth")}
    b_va1 = Buf("va_ones")
    b_m = [[Buf("m%d_%d" % (c, tt)) for tt in range(2)] for c in range(8)]
    CkvT = sb("CkvT", [128, 2, NKEY], BF16);  b_ckv = {k: Buf("ckv_" + k) for k in ("ctx", "own", "oth")}
    KrT = sb("KrT", [96, NKEY], BF16);        b_kr = {k: Buf("kr_" + k) for k in ("ctx", "own", "oth")}
    qlT = sb("qlT", [128, 2, 1024], BF16);    b_ql = [Buf("ql%d" % t) for t in range(8)]
    QbT = [sb("QbT%d" % i, [96, 1024], BF16) for i in range(2)]
    b_qb = [Buf("qb%d" % i) for i in range(2)]
    Vb = [sb("Vb%d" % i, [128, 20, 66], BF16) for i in range(2)]
    b_vb = [Buf("vb%d" % i) for i in range(2)]
    b_vb1 = [Buf("vb1_%d" % i) for i in range(2)]
    NPT = 4
    PT = [sb("PT%d" % i, [128, 512], BF16) for i in range(NPT)]
    b_pt = [Buf("pt%d" % i) for i in range(NPT)]
    ropA = sb("ropA", [128, 2, 1024]);        b_ropA = Buf("ropA")
    ropB = sb("ropB", [96, 2, 1024]);         b_ropB = Buf("ropB")
    Gbc = sb("Gbc", [128, D]);                b_G = Buf("Gbc")
    fgbc = sb("fgbc", [128, D]);              b_fg = Buf("fg")
    kvlnbc = sb("kvlnbc", [128, 256]);        b_kvln = Buf("kvlnbc")
    ident32 = sb("ident32", [128, 128]);      b_id32 = Buf("id32")
    identb = sb("identb", [128, 128], BF16);  b_idb = Buf("idb")
    RAt = sb("RAt", [128, 128]);              b_RA = Buf("RA")
    RBt = sb("RBt", [96, 96]);                b_RB = Buf("RB")
    ones32 = sb("ones32", [128, 128]);        b_ones = Buf("ones32")
    onesb = sb("onesb", [128, 64], BF16);     b_onesb = Buf("onesb")
    esinkb = sb("esinkb", [64, 8]);           b_esinkb = Buf("esinkb")
    diag = [sb("diag%d" % i, [128, 128]) for i in range(2)]
    b_diag = [Buf("diag%d" % i) for i in range(2)]
    flags = sb("flags_sb", [128, 2]);         b_flags = Buf("flags")
    esink = sb("esink", [65, 8]);             b_esink = Buf("esink")
    cvs = sb("cvs", [128, 16]);               b_cvs = Buf("cvs")
    cvb = sb("cvb", [128, 8, 2], BF16);       b_cvb = Buf("cvb")
    adab = sb("adab", [128, 72]);             b_adab = Buf("adab")
    ncols = sb("ncols", [128, 24]);           b_ncols = Buf("ncols")
    lncols = sb("lncols", [128, 4]);          b_lncols = Buf("lncols")
    modc = sb("modc", [128, 72, 2]);          b_modc = Buf("modc")
    Acol = sb("Acol", [128, 3, 8, 2]);        b_Acol = Buf("Acol")
    ss = sb("ss", [128, 8]);                  b_ss = Buf("ss")
    rstd = sb("rstd", [128, 8]);              b_rstd = Buf("rstd")
    ss2 = sb("ss2", [128, 16]);               b_ss2 = Buf("ss2")
    b_ss2c = [Buf("ss2_%d" % i) for i in range(16)]
    b_rs2c = [Buf("rs2_%d" % i) for i in range(16)]
    rs2 = sb("rs2", [128, 16]);               b_rs2 = Buf("rs2")
    epsc = sb("epsc", [128, 1]);              b_eps = Buf("eps")
    junk = sb("junk", [128, D], BF16);        b_junk = Buf("junk")
    xn = [sb("xn%d" % i, [128, D], BF16) for i in range(2)]
    b_xn = [Buf("xn%d" % i) for i in range(2)]
    NSC = 3
    sc32 = [sb("sc32_%d" % i, [128, 512]) for i in range(NSC)]
    b_sc = [Buf("sc32_%d" % i) for i in range(NSC)]
    rden2 = sb("rden", [65, 2, 512]);         b_rden2 = [Buf("rden0"), Buf("rden1")]
    ctm = sb("ctm", [128, 4, 352], BF16);     b_ctm = Buf("ctm")
    fdum = sb("fdum", [128, 8]);              b_fd = Buf("fdum")

    PSB = [st.enter_context(nc.psum_tensor("psb%d" % i, [128, 512], F32)) for i in range(8)]
    b_ps = [Buf("ps%d" % i) for i in range(8)]
    ps_ctr = [0]

    ps_res = set()

    def ps(reserve=False):
        while True:
            i = ps_ctr[0] % 8
            ps_ctr[0] += 1
            if i not in ps_res:
                break
        if reserve:
            ps_res.add(i)
        return PSB[i], b_ps[i]

    def ps_release(bank):
        for i, b in enumerate(PSB):
            if b is bank:
                ps_res.discard(i)

    rr = {}

    def rot(name, n):
        i = rr.get(name, 0)
        rr[name] = i + 1
        return i % n

    def fence(bufs):
        S.op("pool", lambda e: e.memset(fdum[:, 0:1], 0.0), writes=list(bufs) + [b_fd])

    def dbg_dump(name, src_ap, bufs):
        if name in dbg_out:
            o = S.dma("sp", lambda e: e.dma_start(out=dbg_out[name], in_=src_ap), key="dbg_" + name, reads=bufs)
            out_ops.append(o)

    def ld(dst, src, buf, key, q="sp"):
        return S.dma(q, lambda e: e.dma_start(out=dst, in_=src), key=key, writes=[buf])

    ld(ident32[:], identd[:, :], b_id32, "c0")
    ld(identb[:], identd[:, :], b_idb, "c1", q="pool")
    ld(RAt[:], RAd[:, :], b_RA, "c2")
    ld(RBt[:], RBd[:, :], b_RB, "c3")
    ld(flags[:], flagsd[:, :], b_flags, "c4")
    ld(esink[:], sink65[:, :], b_esink, "c5")
    ld(cvs[:], cvfm[:, :], b_cvs, "c6")
    ld(adab[:], ada_bc[:, :], b_adab, "c7")
    ld(ncols[:], ncol[:, :], b_ncols, "c8")
    ld(lncols[:], lncol[:, :], b_lncols, "c9")
    ld(fgbc[:], fnorm.partition_broadcast(128), b_fg, "c10")
    ld(kvlnbc[:], kvln.partition_broadcast(128), b_kvln, "c11")
    S.op("pool", lambda e: e.memset(ones32[:], 1.0), writes=[b_ones])
    S.op("pool", lambda e: e.memset(onesb[:], 1.0), writes=[b_onesb])
    ld(esinkb[:], sink65[64:65, :].partition_broadcast(64), b_esinkb, "c12")
    S.op("act", lambda e: e.activation(out=esinkb[:], in_=esinkb[:], func=AF.Exp), reads=[b_esinkb], writes=[b_esinkb])
    S.op("pool", lambda e: e.memset(epsc[:], EPS), writes=[b_eps])
    S.op("act", lambda e: e.activation(out=esink[64:65, :], in_=esink[64:65, :], func=AF.Exp), reads=[b_esink], writes=[b_esink])

    S.op("act", lambda e: e.activation(out=cvb[:].rearrange("p k m -> p (k m)"), in_=cvs[:], func=AF.Silu), reads=[b_cvs], writes=[b_cvb])
    pm, bpm = ps(reserve=True)
    b_modc_n = [Buf("modc%d" % n) for n in range(3)]
    b_Acol_n = [Buf("Acol%d" % n) for n in range(3)]
    ada_state = {"cc": 0}

    def ada_finish(n):
        S.op("dve", lambda e: e.tensor_tensor(out=modc[:, 24 * n:24 * n + 24, :], in0=pm[:, 48 * n:48 * n + 48].rearrange("p (c m) -> p c m", m=2),
                                              in1=adab[:, 24 * n:24 * n + 24].unsqueeze(2).to_broadcast([128, 24, 2]), op=ALU.add),
             reads=[bpm, b_adab], writes=[b_modc_n[n]])
        S.op("dve", (lambda e: e.tensor_scalar(out=Acol[:, n], in0=modc[:, (3 * n + 1) * 8:(3 * n + 2) * 8, :], scalar1=1.0, scalar2=None, op0=ALU.add)),
             reads=[b_modc_n[n]], writes=[b_Acol_n[n]])
        S.op("dve", (lambda e: e.tensor_tensor(out=Acol[:, n], in0=Acol[:, n], in1=ncols[:, n * 8:(n + 1) * 8].unsqueeze(2).to_broadcast([128, 8, 2]), op=ALU.mult)),
             reads=[b_Acol_n[n], b_ncols], writes=[b_Acol_n[n]])
        if n == 2:
            ps_release(pm)

    def ada_chunk():
        cc = ada_state["cc"]
        if cc >= 36:
            return
        ada_state["cc"] += 1
        s = rot("ws", NSLOT)
        S.dma("pool", (lambda e: e.dma_start(out=wsl[s][:], in_=ada_w[:, cc * 256:(cc + 1) * 256].rearrange("(kc p) n -> p kc n", p=128))),
              key="ws%d" % s, writes=[b_ws[s]])

        def mm_ada(e):
            for sub in range(2):
                ch = cc * 2 + sub
                for kc in range(8):
                    ins = e.matmul(pm[:, ch * 2:ch * 2 + 2], lhsT=wsl[s][:, kc, sub * 128:(sub + 1) * 128], rhs=cvb[:, kc, :], start=(kc == 0), stop=(kc == 7))
            return ins
        S.op("pe", mm_ada, reads=[b_ws[s], b_cvb], writes=[bpm])
        if cc % 12 == 11:
            ada_finish(cc // 12)

    for _ in range(12):
        ada_chunk()
    dbg_dump("d_modc", modc[:].rearrange("p c m -> p (c m)"), b_modc_n)

    def A_ap(n, kc, m):
        return Acol[:, n, kc, m:m + 1]

    def B_ap(n, kc, m):
        return modc[:, 3 * n * 8 + kc, m:m + 1]

    def load_x(gi):
        for t in range(8):
            S.dma("sp", (lambda e, t=t: e.dma_start(out=x[:, t, :], in_=xg[gi, t * 128:(t + 1) * 128, :])), key="x%d" % t, writes=[bx[t]])

    def gate_steps(n, m, half_scale):
        sc = 0.5 if half_scale else 1.0
        for kc in range(8):
            di = rot("diag", 2)
            S.op("dve", (lambda e, di=di, kc=kc: e.tensor_scalar(out=diag[di][:], in0=ident32[:], scalar1=modc[:, (3 * n + 2) * 8 + kc, m:m + 1], scalar2=sc, op0=ALU.mult, op1=ALU.mult)),
                 reads=[b_id32, b_modc_n[n]], writes=[b_diag[di]])
            pg, bpg = ps()
            S.op("pe", (lambda e, di=di, pg=pg: e.matmul(pg[:, 0:128], lhsT=ones32[:], rhs=diag[di][:], start=True, stop=True)),
                 reads=[b_ones, b_diag[di]], writes=[bpg])
            S.op("act", (lambda e, pg=pg, kc=kc: e.activation(out=Gbc[:, kc * 128:(kc + 1) * 128], in_=pg[:, 0:128], func=AF.Copy)),
                 reads=[bpg], writes=[b_G])
            yield

    def norm_to_hT(n, m, gate=None):
        gs = gate_steps(*gate) if gate is not None else iter(())
        for t in range(8):
            S.op("act", (lambda e, t=t: e.activation(out=junk[:], in_=x[:, t, :], func=AF.Square, accum_out=ss[:, t:t + 1])),
                 reads=[bx[t]], writes=[b_junk, b_ss])
        S.op("act", lambda e: e.activation(out=rstd[:], in_=ss[:], func=AF.Sqrt, scale=1.0 / D, bias=epsc[:]), reads=[b_ss, b_eps], writes=[b_rstd])
        S.op("dve", lambda e: e.reciprocal(out=rstd[:], in_=rstd[:]), reads=[b_rstd], writes=[b_rstd])

        def cp(t):
            xi = rot("xn", 2)
            S.op("act", (lambda e: e.activation(out=xn[xi][:], in_=x[:, t, :], func=AF.Copy, scale=rstd[:, t:t + 1])),
                 reads=[bx[t], b_rstd], writes=[b_xn[xi]])
            return xi
        nxt = cp(0)
        for t in range(8):
            xi = nxt
            if t + 1 < 8:
                nxt = cp(t + 1)
            pt_, bpt = ps()
            ptb = pt_[:].bitcast(BF16)

            def tr(e, xi=xi, ptb=ptb):
                for kc in range(8):
                    ins = e.transpose(out=ptb[:, kc * 128:(kc + 1) * 128], in_=xn[xi][:, kc * 128:(kc + 1) * 128], identity=identb[:])
                return ins
            S.op("pe", tr, reads=[b_xn[xi], b_idb], writes=[bpt])

            def ev_d(e, t=t, ptb=ptb):
                for kc in range(8):
                    ins = e.tensor_scalar(out=hT[:, kc, t * 128:(t + 1) * 128], in0=ptb[:, kc * 128:(kc + 1) * 128],
                                          scalar1=A_ap(n, kc, m), scalar2=B_ap(n, kc, m), op0=ALU.mult, op1=ALU.add)
                return ins
            S.op("dve", ev_d, reads=[bpt, b_Acol_n[n], b_modc_n[n]], writes=[bhA[t], bhD[t]])
            next(gs, None)
        for _ in gs:
            pass

    def resid_add(t, c0, w, pd, bpd):
        si = rot("sc", NSC)
        S.op("dve", (lambda e: e.tensor_tensor(out=sc32[si][:, 0:w], in0=pd[:, 0:w], in1=Gbc[:, c0:c0 + w], op=ALU.mult)),
             reads=[bpd, b_G], writes=[b_sc[si]])
        S.op("dve", (lambda e: e.tensor_tensor(out=x[:, t, c0:c0 + w], in0=x[:, t, c0:c0 + w], in1=sc32[si][:, 0:w], op=ALU.add)),
             reads=[b_sc[si], bx[t]], writes=[bx[t]])

    def ffn(fi, n, m, hook=None):
        norm_to_hT(n, m, gate=(n, m, True))
        wg, wdn = w_gu[fi], w_dn[fi]
        for blk in range(NBLK):
            pend_wd = []
            for jj in range(CPB):
                j = blk * CPB + jj
                s = rot("ws", NSLOT)
                S.dma("pool", (lambda e, s=s, j=j: e.dma_start(out=wsl[s][:, :, 0:128], in_=wg[:, j * 128:(j + 1) * 128].rearrange("(kc p) n -> p kc n", p=128))),
                      key="ws%d" % s, writes=[b_ws[s]])
                S.dma("pool", (lambda e, s=s, j=j: e.dma_start(out=wsl[s][:, :, 128:256], in_=wg[:, DFF + j * 128:DFF + (j + 1) * 128].rearrange("(kc p) n -> p kc n", p=128))),
                      key="ws%d" % s, writes=[b_ws[s]])
                if jj >= 2:
                    jw = jj - 2
                    S.dma("pool", (lambda e, jw=jw, blk=blk: e.dma_start(out=wd[:, jw, :], in_=wdn[(blk * CPB + jw) * 128:(blk * CPB + jw + 1) * 128, :])),
                          key="wd%d" % jw, writes=[b_wd[jw]])
                for tt in range(2):
                    pa, bpa = ps()
                    pu, bpu = ps()

                    def mm_gu(e, s=s, tt=tt, pa=pa, pu=pu):
                        for kc in range(8):
                            e.matmul(pa[:], lhsT=wsl[s][:, kc, 0:128], rhs=hT[:, kc, tt * 512:(tt + 1) * 512], start=(kc == 0), stop=(kc == 7))
                        for kc in range(8):
                            ins = e.matmul(pu[:], lhsT=wsl[s][:, kc, 128:256], rhs=hT[:, kc, tt * 512:(tt + 1) * 512], start=(kc == 0), stop=(kc == 7))
                        return ins
                    S.op("pe", mm_gu, reads=[b_ws[s]] + BH(tt * 4, (tt + 1) * 4), writes=[bpa, bpu])
                    si = rot("sc", NSC)
                    S.op("act", (lambda e, si=si, pa=pa: e.activation(out=sc32[si][:], in_=pa[:], func=AF.Silu)), reads=[bpa], writes=[b_sc[si]])
                    S.op("dve", (lambda e, si=si, pu=pu, jj=jj, tt=tt: e.tensor_tensor(out=actT[:, jj, tt * 512:(tt + 1) * 512], in0=sc32[si][:], in1=pu[:], op=ALU.mult)),
                         reads=[b_sc[si], bpu], writes=[b_act[jj][tt]])
            for jw in range(CPB - 2, CPB):
                S.dma("pool", (lambda e, jw=jw, blk=blk: e.dma_start(out=wd[:, jw, :], in_=wdn[(blk * CPB + jw) * 128:(blk * CPB + jw + 1) * 128, :])),
                      key="wd%d" % jw, writes=[b_wd[jw]])
            for t in range(8):
                if hook is not None:
                    hook()
                for half in range(2):
                    pd, bpd = ps()

                    def mm_d(e, t=t, half=half, pd=pd):
                        for jj in range(CPB):
                            ins = e.matmul(pd[:], lhsT=actT[:, jj, t * 128:(t + 1) * 128], rhs=wd[:, jj, half * 512:(half + 1) * 512], start=(jj == 0), stop=(jj == CPB - 1))
                        return ins
                    S.op("pe", mm_d, reads=[b_act[jj][t // 4] for jj in range(CPB)] + b_wd, writes=[bpd])
                    resid_add(t, half * 512, 512, pd, bpd)

    def final_out(gi_out):
        for t in range(8):
            S.op("act", (lambda e, t=t: e.activation(out=junk[:], in_=x[:, t, :], func=AF.Square, accum_out=ss[:, t:t + 1])),
                 reads=[bx[t]], writes=[b_junk, b_ss])
        S.op("act", lambda e: e.activation(out=rstd[:], in_=ss[:], func=AF.Sqrt, scale=1.0 / D, bias=epsc[:]), reads=[b_ss, b_eps], writes=[b_rstd])
        S.op("dve", lambda e: e.reciprocal(out=rstd[:], in_=rstd[:]), reads=[b_rstd], writes=[b_rstd])
        for t in range(8):
            S.op("dve", (lambda e, t=t: e.scalar_tensor_tensor(out=x[:, t, :], in0=x[:, t, :], scalar=rstd[:, t:t + 1], in1=fgbc[:], op0=ALU.mult, op1=ALU.mult)),
                 reads=[bx[t], b_rstd, b_fg], writes=[bx[t]])
            o = S.dma("sp", (lambda e, t=t: e.dma_start(out=y[gi_out, t * 128:(t + 1) * 128, :], in_=x[:, t, :])), key="yo%d" % t, reads=[bx[t]])
            out_ops.append(o)

    PO = DBG.get("po", 7)
    KOFF = {"ctx": K_CTX, "own": K_OWN, "oth": K_OTH}
    VT0 = {"ctx": 0, "own": 4, "oth": 12}

    def wslot():
        s = rot("ws", NSLOT)
        return s

    def rope_evac(p1, bp1, rows, scale, ridx, tok0, out_ap, out_bufs, rope):
        if not rope:
            S.op("act", (lambda e: e.activation(out=out_ap, in_=p1[0:rows, :], func=AF.Copy, scale=scale)), reads=[bp1], writes=out_bufs)
            return
        tab, btab, Rt, bR = (ropA, b_ropA, RAt, b_RA) if rows == 128 else (ropB, b_ropB, RBt, b_RB)
        sa = rot("sc", NSC)
        S.op("act", (lambda e: e.activation(out=sc32[sa][0:rows, :], in_=p1[0:rows, :], func=AF.Copy, scale=scale)), reads=[bp1], writes=[b_sc[sa]])
        p2, bp2 = ps()
        S.op("pe", (lambda e: e.matmul(p2[0:rows, :], lhsT=Rt[0:rows, 0:rows], rhs=sc32[sa][0:rows, :], start=True, stop=True)), reads=[b_sc[sa], bR], writes=[bp2])
        sb_ = rot("sc", NSC)
        S.op("dve", (lambda e: e.tensor_tensor(out=sc32[sb_][0:rows, :], in0=p2[0:rows, :], in1=tab[0:rows, 1, tok0:tok0 + 512], op=ALU.mult)),
             reads=[bp2, btab], writes=[b_sc[sb_]])
        S.op("dve", (lambda e: e.tensor_tensor(out=sc32[sa][0:rows, :], in0=sc32[sa][0:rows, :], in1=tab[0:rows, 0, tok0:tok0 + 512], op=ALU.mult)),
             reads=[b_sc[sa], btab], writes=[b_sc[sa]])
        S.op("dve", (lambda e: e.tensor_tensor(out=out_ap, in0=sc32[sa][0:rows, :], in1=sc32[sb_][0:rows, :], op=ALU.add)),
             reads=[b_sc[sa], b_sc[sb_]], writes=out_bufs)

    def fm_proj(s, bslot, lhs_cols, M, tt):
        p1, bp1 = ps()
        c0 = lhs_cols

        def mm(e):
            for kc in range(8):
                ins = e.matmul(p1[0:M, :], lhsT=wsl[s][:, kc, c0:c0 + M], rhs=hT[:, kc, tt * 512:(tt + 1) * 512], start=(kc == 0), stop=(kc == 7))
            return ins
        S.op("pe", mm, reads=[bslot] + BH(tt * 4, (tt + 1) * 4), writes=[bp1])
        return p1, bp1

    def tm_proj(s, bslot, c0, N, t):
        p1, bp1 = ps()

        def mm(e):
            for kc in range(8):
                ins = e.matmul(p1[:, 0:N], lhsT=hT[:, kc, t * 128:(t + 1) * 128], rhs=wsl[s][:, kc, c0:c0 + N], start=(kc == 0), stop=(kc == 7))
            return ins
        S.op("pe", mm, reads=[bslot] + BH(t, t + 1), writes=[bp1])
        return p1, bp1

    def load_w_cols(src, c0, n, dcol=0, s=None):
        if s is None:
            s = wslot()
        S.dma("pool", (lambda e: e.dma_start(out=wsl[s][:, :, dcol:dcol + n], in_=src[:, c0:c0 + n].rearrange("(kc p) n -> p kc n", p=128))),
              key="ws%d" % s, writes=[b_ws[s]])
        return s

    def lat_norm_T(p1, bp1, c0, lncol0, t, out_fn, out_bufs, tm_out=None):
        ci = rot("ss2", 16)
        ssc, rsc, bssc, brsc = ss2[:, ci:ci + 1], rs2[:, ci:ci + 1], b_ss2c[ci], b_rs2c[ci]
        S.op("act", (lambda e: e.activation(out=junk[:, 0:256], in_=p1[:, c0:c0 + 256], func=AF.Square, accum_out=ssc)),
             reads=[bp1], writes=[b_junk, bssc])
        S.op("act", lambda e: e.activation(out=rsc, in_=ssc, func=AF.Sqrt, scale=1.0 / 256, bias=epsc[:]), reads=[bssc, b_eps], writes=[brsc])
        S.op("dve", lambda e: e.reciprocal(out=rsc, in_=rsc), reads=[brsc], writes=[brsc])
        xi = rot("xn", 2)
        S.op("act", (lambda e: e.activation(out=xn[xi][:, 0:256], in_=p1[:, c0:c0 + 256], func=AF.Copy, scale=rsc)), reads=[bp1, brsc], writes=[b_xn[xi]])
        if tm_out is not None:
            tm_out(rsc, brsc, b_xn[xi])

        def stage_b():
            p2, bp2 = ps()
            p2b = p2[:].bitcast(BF16)

            def tr(e):
                for k2 in range(2):
                    ins = e.transpose(out=p2b[:, k2 * 128:(k2 + 1) * 128], in_=xn[xi][:, k2 * 128:(k2 + 1) * 128], identity=identb[:])
                return ins
            S.op("pe", tr, reads=[b_xn[xi], b_idb], writes=[bp2])

            def ev(e):
                for k2 in range(2):
                    ins = e.tensor_scalar(out=out_fn(k2), in0=p2b[:, k2 * 128:(k2 + 1) * 128], scalar1=lncols[:, lncol0 + k2:lncol0 + k2 + 1], scalar2=None, op0=ALU.mult)
                return ins
            S.op("dve", ev, reads=[bp2, b_lncols], writes=out_bufs)
        return stage_b

    def mix_proj(kind, reg, ridx, rope, full, prompt_out):
        k0 = KOFF[reg]
        vt0 = VT0[reg]
        if full:
            for half in range(2):
                s = wslot()
                for gl in range(2):
                    g = 2 * half + gl
                    for kv in range(2):
                        hc = (kv * 4 + g) * 64
                        S.dma("pool", (lambda e, s=s, gl=gl, kv=kv, hc=hc: e.dma_start(out=wsl[s][:, :, gl * 128 + kv * 64:gl * 128 + (kv + 1) * 64],
                                                                                   in_=w_in[:, hc:hc + 64].rearrange("(kc p) d -> p kc d", p=128))),
                              key="ws%d" % s, writes=[b_ws[s]])
                for gl in range(2):
                    g = 2 * half + gl
                    for tt in range(2):
                        p1, bp1 = fm_proj(s, b_ws[s], gl * 128, 128, tt)
                        rope_evac(p1, bp1, 128, A_SCALE, ridx, tt * 512, QaT[:, g, tt * 512:(tt + 1) * 512], [b_qa[g]], rope)
        s = load_w_cols(w_in, 512, 256)
        for tt in range(2):
            p1, bp1 = fm_proj(s, b_ws[s], 0, 128, tt)
            rope_evac(p1, bp1, 128, 1.0, ridx, tt * 512, KaT[:, k0 + tt * 512:k0 + (tt + 1) * 512], [b_ka[reg]], rope)
        for t in range(8):
            p1, bp1 = tm_proj(s, b_ws[s], 0, 256, t)
            S.op("act", (lambda e, p1=p1, t=t: e.activation(out=Va[:, vt0 + t, :, 0:64], in_=p1[:, 128:256].rearrange("p (k d) -> p k d", k=2), func=AF.Copy)),
                 reads=[bp1], writes=[b_va[reg]])
            if prompt_out and (PO & 1):
                si = rot("sc", NSC)
                S.op("dve", (lambda e, p1=p1, si=si: e.tensor_copy(out=sc32[si][:, 0:256], in_=p1[:, 0:256])), reads=[bp1, b_va[reg]], writes=[b_sc[si]])
                if PO & 8:
                    continue
                if PO & 16:
                    o = S.dma("sp", (lambda e, si=si, t=t: e.dma_start(out=y[1, t * 128:(t + 1) * 128, 0:256], in_=sc32[si][:, 0:256])), key="osc%d" % si, reads=[b_sc[si]])
                    out_ops.append(o)
                    continue
                o = S.dma("sp", (lambda e, si=si, t=t: e.dma_start(out=nk[t * 128:(t + 1) * 128, :], in_=sc32[si][:, 0:128])), key="osc%d" % si, reads=[b_sc[si]])
                out_ops.append(o)
                o = S.dma("sp", (lambda e, si=si, t=t: e.dma_start(out=nv[t * 128:(t + 1) * 128, :], in_=sc32[si][:, 128:256])), key="osc%d" % si, reads=[b_sc[si]])
                out_ops.append(o)
        if full:
            s = load_w_cols(w_in, 768, 256)
            pend = None
            for t in range(8):
                p1, bp1 = tm_proj(s, b_ws[s], 0, 256, t)
                stb = lat_norm_T(p1, bp1, 0, 0, t, (lambda k2, t=t: qlT[:, k2, t * 128:(t + 1) * 128]), [b_ql[t]])
                if pend is not None:
                    pend()
                pend = stb
            pend()
        s = load_w_cols(w_in, 1024, 256)
        pend = None
        for t in range(8):
            p1, bp1 = tm_proj(s, b_ws[s], 0, 256, t)
            tm_out = None
            if prompt_out and (PO & 2):
                def tm_out(rs_ap, brs, bxn, p1=p1, bp1=bp1, t=t):
                    si = rot("sc", NSC)
                    S.op("dve", (lambda e: e.scalar_tensor_tensor(out=sc32[si][:, 0:256], in0=p1[:, 0:256], scalar=rs_ap, in1=kvlnbc[:], op0=ALU.mult, op1=ALU.mult)),
                         reads=[bp1, brs, b_kvln, bxn], writes=[b_sc[si]])
                    o = S.dma("sp", (lambda e: e.dma_start(out=nckv[t * 128:(t + 1) * 128, :], in_=sc32[si][:, 0:256])), key="osc%d" % si, reads=[b_sc[si]])
                    out_ops.append(o)
            stb = lat_norm_T(p1, bp1, 0, 2, t, (lambda k2, t=t: CkvT[:, k2, k0 + t * 128:k0 + (t + 1) * 128]), [b_ckv[reg]], tm_out=tm_out)
            if pend is not None:
                pend()
            pend = stb
        pend()
        s = load_w_cols(w_in, 1216, 96)
        for tt in range(2):
            p1, bp1 = fm_proj(s, b_ws[s], 0, 96, tt)
            if rope:
                sa = rot("sc", NSC)
                rope_evac(p1, bp1, 96, 1.0, ridx, tt * 512, sc32[sa][0:96, :], [b_sc[sa]], True)
                S.op("act", (lambda e, sa=sa, tt=tt: e.activation(out=KrT[64:96, k0 + tt * 512:k0 + (tt + 1) * 512], in_=sc32[sa][64:96, :], func=AF.Copy)),
                     reads=[b_sc[sa]], writes=[b_kr[reg]])
            else:
                S.op("act", (lambda e, p1=p1, tt=tt: e.activation(out=KrT[64:96, k0 + tt * 512:k0 + (tt + 1) * 512], in_=p1[64:96, :], func=AF.Copy)),
                     reads=[bp1], writes=[b_kr[reg]])
        if prompt_out and (PO & 4):
            for t in range(8):
                p1, bp1 = tm_proj(s, b_ws[s], 64, 32, t)
                si = rot("sc", NSC)
                S.op("dve", (lambda e, p1=p1, si=si: e.tensor_copy(out=sc32[si][:, 0:32], in_=p1[:, 0:32])), reads=[bp1], writes=[b_sc[si]])
                o = S.dma("sp", (lambda e, si=si, t=t: e.dma_start(out=nkr[t * 128:(t + 1) * 128, :], in_=sc32[si][:, 0:32])), key="osc%d" % si, reads=[b_sc[si]])
                out_ops.append(o)

    def load_ctx():
        S.op("pool", lambda e: e.memset(ctm[:], 0.0), writes=[b_ctm])
        S.dma("pool", lambda e: e.dma_start(out=ctm[:, :, 0:128], in_=ck.rearrange("(n p) c -> p n c", p=128)), key="ctm", writes=[b_ctm])
        S.dma("pool", lambda e: e.dma_start(out=ctm[:, :, 320:352], in_=ckr.rearrange("(n p) c -> p n c", p=128)), key="ctm", writes=[b_ctm])
        for kv in range(2):
            S.dma("pool", (lambda e, kv=kv: e.dma_start(out=Va[:, 0:4, kv, 0:64], in_=cv.rearrange("(n p) (k d) -> p n k d", p=128, k=2)[:, :, kv, :])), key="vactx", writes=[b_va["ctx"]])
        for n in range(4):
            p1, bp1 = ps()
            p1b = p1[:].bitcast(BF16)
            S.op("pe", (lambda e, n=n, p1b=p1b: e.transpose(out=p1b[:, 0:128], in_=ctm[:, n, 0:128], identity=identb[:])), reads=[b_ctm, b_idb], writes=[bp1])
            S.op("act", (lambda e, n=n, p1b=p1b: e.activation(out=KaT[:, K_CTX + n * 128:K_CTX + (n + 1) * 128], in_=p1b[:, 0:128], func=AF.Copy)), reads=[bp1], writes=[b_ka["ctx"]])
            p2, bp2 = ps()
            p2b = p2[:].bitcast(BF16)
            S.op("pe", (lambda e, n=n, p2b=p2b: e.transpose(out=p2b[0:96, 0:128], in_=ctm[:, n, 256:352], identity=identb[:])), reads=[b_ctm, b_idb], writes=[bp2])
            S.op("act", (lambda e, n=n, p2b=p2b: e.activation(out=KrT[64:96, K_CTX + n * 128:K_CTX + (n + 1) * 128], in_=p2b[64:96, 0:128], func=AF.Copy)), reads=[bp2], writes=[b_kr["ctx"]])
        S.dma("pool", lambda e: e.dma_start(out=ctm[:, :, 0:256], in_=cckv.rearrange("(n p) c -> p n c", p=128)), key="ctm", writes=[b_ctm])
        for n in range(4):
            p1, bp1 = ps()
            p1b = p1[:].bitcast(BF16)

            def tr(e, n=n, p1b=p1b):
                for k2 in range(2):
                    ins = e.transpose(out=p1b[:, k2 * 128:(k2 + 1) * 128], in_=ctm[:, n, k2 * 128:(k2 + 1) * 128], identity=identb[:])
                return ins
            S.op("pe", tr, reads=[b_ctm, b_idb], writes=[bp1])
            S.op("act", (lambda e, n=n, p1b=p1b: e.activation(out=CkvT[:, :, K_CTX + n * 128:K_CTX + (n + 1) * 128], in_=p1b[:, 0:256].rearrange("p (k t) -> p k t", k=2), func=AF.Copy)),
                 reads=[bp1], writes=[b_ckv["ctx"]])

    LOOK = 3

    class Unit:
        pass

    def run_attention_stream(front):
        from collections import deque
        backq = deque()
        normq = []

        def emit_st(u, k):
            k_ap, k_bufs, v_ap, v_bufs, mask, flag = u.tiles[k]
            N = u.N
            pS, bpS = ps()
            S.op("pe", (lambda e: e.matmul(pS[:, 0:N], lhsT=k_ap, rhs=u.q_ap, start=True, stop=True)), reads=list(k_bufs) + list(u.q_bufs), writes=[bpS])
            pi = rot("pt", NPT)
            S.op("act", (lambda e: e.activation(out=PT[pi][:, 0:N], in_=pS[:, 0:N], func=AF.Exp)), reads=[bpS], writes=[b_pt[pi]])
            if mask == "prev":
                S.op("pool", (lambda e: e.affine_select(out=PT[pi][:].rearrange("p (g q) -> p g q", g=4), in_=PT[pi][:].rearrange("p (g q) -> p g q", g=4),
                                                      pattern=[[0, 4], [-1, 128]], compare_op=ALU.is_ge, fill=0.0, base=0, channel_multiplier=1)),
                     reads=[b_pt[pi]], writes=[b_pt[pi]])
            elif mask == "next":
                S.op("pool", (lambda e: e.affine_select(out=PT[pi][:].rearrange("p (g q) -> p g q", g=4), in_=PT[pi][:].rearrange("p (g q) -> p g q", g=4),
                                                      pattern=[[0, 4], [1, 128]], compare_op=ALU.is_ge, fill=0.0, base=0, channel_multiplier=-1)),
                     reads=[b_pt[pi]], writes=[b_pt[pi]])
            if flag is not None:
                S.op("pool", (lambda e: e.tensor_scalar(out=PT[pi][:, 0:N], in0=PT[pi][:, 0:N], scalar1=flags[:, flag:flag + 1], scalar2=None, op0=ALU.mult)),
                     reads=[b_pt[pi], b_flags], writes=[b_pt[pi]])
            return pi

        def emit_pv(u, k, pi):
            k_ap, k_bufs, v_ap, v_bufs, mask, flag = u.tiles[k]
            N = u.N
            nt = len(u.tiles)
            if k == 0:
                u.pO, u.bpO = ps(reserve=True)
            pO = u.pO
            S.op("pe", (lambda e: e.matmul(pO[0:66, 0:N], lhsT=v_ap, rhs=PT[pi][:, 0:N], start=(k == 0), stop=(k == nt - 1))),
                 reads=[b_pt[pi]] + list(v_bufs), writes=[u.bpO])
            if u.use_pd:
                if k == 0:
                    u.pD, u.bpD = ps(reserve=True)
                pD = u.pD
                S.op("pe", (lambda e: e.matmul(pD[0:64, 0:N], lhsT=onesb[:], rhs=PT[pi][:, 0:N], start=(k == 0), stop=(k == nt - 1))),
                     reads=[b_pt[pi], b_onesb], writes=[u.bpD])

        def norm_pd(u):
            pO, bpO, pD, bpD, N = u.pO, u.bpO, u.pD, u.bpD, u.N
            si = rot("sc", NSC)
            if u.sink_cols is None:
                S.op("act", (lambda e: e.activation(out=sc32[si][0:64, 0:N], in_=pD[0:64, 0:N], func=AF.Ln)), reads=[bpD], writes=[b_sc[si]])
            else:
                def lnsink(e):
                    for g in range(4):
                        ins = e.activation(out=sc32[si][0:64, g * 128:(g + 1) * 128], in_=pD[0:64, g * 128:(g + 1) * 128], func=AF.Ln,
                                           bias=esinkb[:, u.sink_cols + g:u.sink_cols + g + 1])
                    return ins
                S.op("act", lnsink, reads=[bpD, b_esinkb], writes=[b_sc[si]])
            S.op("act", (lambda e: e.activation(out=sc32[si][0:64, 0:N], in_=sc32[si][0:64, 0:N], func=AF.Exp, scale=-1.0)), reads=[b_sc[si]], writes=[b_sc[si]])
            S.op("dve", (lambda e: e.tensor_tensor(out=u.out_ap, in0=u.view(pO[0:64, 0:N]), in1=u.view(sc32[si][0:64, 0:N]), op=ALU.mult)),
                 reads=[bpO, b_sc[si]], writes=u.out_bufs)
            ps_release(pO)
            ps_release(pD)

        def norm_a(u):
            pO, bpO, N = u.pO, u.bpO, u.N
            ri = rot("rden", 2)
            u.ri = ri
            rden, b_rden = rden2[:, ri, :], b_rden2[ri]
            if u.sink_cols is None:
                S.op("dve", (lambda e: e.tensor_copy(out=rden[64:65, 0:N], in_=pO[64:65, 0:N])), reads=[bpO], writes=[b_rden])
            else:
                def addsink(e):
                    for g in range(4):
                        ins = e.tensor_scalar(out=rden[64:65, g * 128:(g + 1) * 128], in0=pO[64:65, g * 128:(g + 1) * 128],
                                              scalar1=esink[64:65, u.sink_cols + g:u.sink_cols + g + 1], scalar2=None, op0=ALU.add)
                    return ins
                S.op("dve", addsink, reads=[bpO, b_esink], writes=[b_rden])

        def norm_b(u):
            pO, bpO, N = u.pO, u.bpO, u.N
            rden, b_rden = rden2[:, u.ri, :], b_rden2[u.ri]
            pB, bpB = ps()
            S.op("pe", (lambda e: e.matmul(pB[0:64, 0:N], lhsT=ones32[64:65, 0:64], rhs=rden[64:65, 0:N], start=True, stop=True)), reads=[b_ones, b_rden], writes=[bpB])
            si = rot("sc", NSC)
            S.op("act", (lambda e: e.activation(out=sc32[si][0:64, 0:N], in_=pB[0:64, 0:N], func=AF.Ln)), reads=[bpB], writes=[b_sc[si]])
            S.op("act", (lambda e: e.activation(out=sc32[si][0:64, 0:N], in_=sc32[si][0:64, 0:N], func=AF.Exp, scale=-1.0)), reads=[b_sc[si]], writes=[b_sc[si]])
            S.op("dve", (lambda e: e.tensor_tensor(out=u.out_ap, in0=u.view(pO[0:64, 0:N]), in1=u.view(sc32[si][0:64, 0:N]), op=ALU.mult)),
                 reads=[bpO, b_sc[si]], writes=u.out_bufs)
            ps_release(pO)

        def do_back():
            u, k, pi = backq.popleft()
            emit_pv(u, k, pi)
            for ent in list(normq):
                ent[1] -= 1
                if ent[1] <= 0:
                    norm_b(ent[0])
                    normq.remove(ent)
            if k == len(u.tiles) - 1 and u.use_pd:
                norm_pd(u)
            elif k == len(u.tiles) - 1:
                while len(normq) > 1:
                    norm_b(normq[0][0])
                    normq.pop(0)
                norm_a(u)
                normq.append([u, 6 if len(u.tiles) >= 16 else 3])

        for ent in front:
            if ent[0] == "call":
                ent[1]()
                continue
            _, u, k = ent
            pi = emit_st(u, k)
            backq.append((u, k, pi))
            if len(backq) > LOOK:
                do_back()
        while backq:
            do_back()
        for ent in normq:
            norm_b(ent[0])

    def attention(kind):
        sample = (kind == "S")
        v4 = lambda ap: ap.rearrange("p (g q) -> p g q", g=4)
        ident_v = lambda ap: ap
        front = []

        def add_unit(q_ap, q_bufs, N, tiles, sink_cols, out_ap, out_bufs, view):
            u = Unit()
            u.q_ap, u.q_bufs, u.N, u.tiles, u.sink_cols, u.out_ap, u.out_bufs, u.view = q_ap, q_bufs, N, tiles, sink_cols, out_ap, out_bufs, view
            u.use_pd = not sample
            for k in range(len(tiles)):
                front.append(("st", u, k))
            return u

        s_uq = wslot()
        uq = wsl[s_uq][:].rearrange("p k c -> p (k c)")[:, 0:1536].rearrange("p (k c) -> p k c", k=2)
        S.dma("pool", lambda e: e.dma_start(out=uq, in_=w_uq.rearrange("(k p) c -> p k c", p=128)), key="ws%d" % s_uq, writes=[b_ws[s_uq]])
        s_ukv = wslot()
        ukv = wsl[s_ukv][:].rearrange("p k c -> p (k c)").rearrange("p (k c) -> p k c", k=2)
        S.dma("pool", lambda e: e.dma_start(out=ukv, in_=w_ukv.rearrange("(k p) c -> p k c", p=128)), key="ws%d" % s_ukv, writes=[b_ws[s_ukv]])
        if sample:
            regs = [("ctx", 4), ("own", 8), ("oth", 8)]
            nkt = 20
            kbase = 0
        else:
            regs = [("own", 8)]
            nkt = 8
            kbase = K_OWN
        all_kr = [b_kr[r] for r, _ in regs]
        all_ckv = [b_ckv[r] for r, _ in regs]
        vt_base = 0 if sample else 4

        def prep_rope_rows():
            for i in range(2):
                S.op("act", (lambda e, i=i: e.activation(out=KbT[i][64:96, kbase:kbase + nkt * 128], in_=KrT[64:96, kbase:kbase + nkt * 128], func=AF.Copy)),
                     reads=all_kr, writes=[b_kbr[i]])

        def prep(h):
            kb = h % 2
            for c in range(nkt // 4):
                p1, bp1 = ps()
                col0 = kbase + c * 512

                def mmk(e, p1=p1, col0=col0):
                    for k2 in range(2):
                        ins = e.matmul(p1[0:64, :], lhsT=ukv[:, k2, h * 128:h * 128 + 64], rhs=CkvT[:, k2, col0:col0 + 512], start=(k2 == 0), stop=(k2 == 1))
                    return ins
                S.op("pe", mmk, reads=[b_ws[s_ukv]] + all_ckv, writes=[bp1])
                if c % 2 == 0:
                    S.op("dve", (lambda e, p1=p1, col0=col0: e.tensor_copy(out=KbT[kb][0:64, col0:col0 + 512], in_=p1[0:64, :])), reads=[bp1], writes=[b_kb[kb]])
                else:
                    S.op("pool" if False else "dve", (lambda e, p1=p1, col0=col0: e.tensor_copy(out=KbT[kb][0:64, col0:col0 + 512], in_=p1[0:64, :])), reads=[bp1], writes=[b_kb[kb]])
            for c0 in range(0, nkt, 8):
                nn = min(8, nkt - c0)
                p1, bp1 = ps()

                def mmv(e, p1=p1, c0=c0, nn=nn):
                    for i in range(nn):
                        col0 = kbase + (c0 + i) * 128
                        for k2 in range(2):
                            ins = e.matmul(p1[:, i * 64:(i + 1) * 64], lhsT=CkvT[:, k2, col0:col0 + 128], rhs=ukv[:, k2, h * 128 + 64:h * 128 + 128], start=(k2 == 0), stop=(k2 == 1))
                    return ins
                S.op("pe", mmv, reads=[b_ws[s_ukv]] + all_ckv, writes=[bp1])
                S.op("dve", (lambda e, p1=p1, c0=c0, nn=nn: e.tensor_copy(out=Vb[kb][:, vt_base + c0:vt_base + c0 + nn, 0:64], in_=p1[:, 0:nn * 64].rearrange("p (n d) -> p n d", d=64))),
                     reads=[bp1], writes=[b_vb[kb]])
            for tt in range(2):
                p1, bp1 = ps()

                def mmq(e, p1=p1, tt=tt):
                    for k2 in range(2):
                        ins = e.matmul(p1[0:96, :], lhsT=uq[:, k2, h * 96:(h + 1) * 96], rhs=qlT[:, k2, tt * 512:(tt + 1) * 512], start=(k2 == 0), stop=(k2 == 1))
                    return ins
                S.op("pe", mmq, reads=[b_ws[s_uq]] + b_ql[tt * 4:(tt + 1) * 4], writes=[bp1])
                rope_evac(p1, bp1, 96, B_SCALE, 0, tt * 512, QbT[kb][0:96, tt * 512:(tt + 1) * 512], [b_qb[kb]], sample)

        a_units = 0
        if sample:
            for j in (1, 2, 3, 4, 5, 6, 0, 7):
                for kv in range(2):
                    ks = slice(kv * 64, (kv + 1) * 64)
                    tiles = []
                    for n in range(4):
                        tiles.append((KaT[ks, K_CTX + n * 128:K_CTX + (n + 1) * 128], [b_ka["ctx"]], Va[:, n, kv, 0:66], [b_va["ctx"], b_va1], None, None))
                    tiles.append((KaT[ks, K_OWN + j * 128:K_OWN + (j + 1) * 128], [b_ka["own"]], Va[:, 4 + j, kv, 0:66], [b_va["own"], b_va1], None, None))
                    if j > 0:
                        tiles.append((KaT[ks, K_OWN + (j - 1) * 128:K_OWN + j * 128], [b_ka["own"]], Va[:, 4 + j - 1, kv, 0:66], [b_va["own"], b_va1], "prev", None))
                    else:
                        tiles.append((KaT[ks, K_OTH + 7 * 128:K_OTH + 8 * 128], [b_ka["oth"]], Va[:, 12 + 7, kv, 0:66], [b_va["oth"], b_va1], "prev", 0))
                    if j < 7:
                        tiles.append((KaT[ks, K_OWN + (j + 1) * 128:K_OWN + (j + 2) * 128], [b_ka["own"]], Va[:, 4 + j + 1, kv, 0:66], [b_va["own"], b_va1], "next", None))
                    else:
                        tiles.append((KaT[ks, K_OTH:K_OTH + 128], [b_ka["oth"]], Va[:, 12, kv, 0:66], [b_va["oth"], b_va1], "next", 1))
                    add_unit(QaT[ks, :, j * 128:(j + 1) * 128], b_qa, 512, tiles, kv * 4,
                             oaT[0:64, kv * 4:(kv + 1) * 4, j * 128:(j + 1) * 128], b_oa[kv * 4:(kv + 1) * 4], v4)
                    a_units += 1
                    if a_units == 8:
                        front.append(("call", prep_rope_rows))
                        front.append(("call", (lambda: prep(0))))
        else:
            for sq in range(4):
                for qh in range(2):
                    for kv in range(2):
                        ks = slice(kv * 64, (kv + 1) * 64)
                        q0 = sq * 256 + qh * 128
                        tiles = []
                        for kt in range(2):
                            kk = sq * 2 + kt
                            tiles.append((KaT[ks, K_OWN + kk * 128:K_OWN + (kk + 1) * 128], [b_ka["own"]], Va[:, 4 + kk, kv, 0:66], [b_va["own"], b_va1], None, None))
                        add_unit(QaT[ks, :, q0:q0 + 128], b_qa, 512, tiles, kv * 4,
                                 oaT[0:64, kv * 4:(kv + 1) * 4, q0:q0 + 128], b_oa[kv * 4:(kv + 1) * 4], v4)
                        a_units += 1
                        if a_units == DBG.get("pcall", 8):
                            front.append(("call", prep_rope_rows))
                            front.append(("call", (lambda: prep(0))))
        for h in range(8):
            kb = h % 2
            start_idx = len(front)
            if sample:
                for tt in range(2):
                    tiles = []
                    for n in range(20):
                        tiles.append((KbT[kb][0:96, n * 128:(n + 1) * 128], [b_kb[kb], b_kbr[kb]], Vb[kb][:, n, 0:66], [b_vb[kb], b_vb1[kb]], None, None))
                    add_unit(QbT[kb][0:96, tt * 512:(tt + 1) * 512], [b_qb[kb]], 512, tiles, None, obT[0:64, h, tt * 512:(tt + 1) * 512], [b_ob[h]], ident_v)
            else:
                for sq in range(4):
                    tiles = []
                    for kt in range(2):
                        kk = sq * 2 + kt
                        tiles.append((KbT[kb][0:96, K_OWN + kk * 128:K_OWN + (kk + 1) * 128], [b_kb[kb], b_kbr[kb]], Vb[kb][:, 4 + kk, 0:66], [b_vb[kb], b_vb1[kb]], None, None))
                    add_unit(QbT[kb][0:96, sq * 256:(sq + 1) * 256], [b_qb[kb]], 256, tiles, None, obT[0:64, h, sq * 256:(sq + 1) * 256], [b_ob[h]], ident_v)
            if h + 1 < 8:
                front.insert(start_idx + LOOK + 2, ("call", (lambda h=h: prep(h + 1))))
        run_attention_stream(front)

    def merge(m):
        for i in range(4):
            S.dma("sp", (lambda e, i=i: e.dma_start(out=oaT2[64:128, 2 * i, :], in_=oaT2[0:64, 2 * i + 1, :])), key="pa%d" % i, reads=[b_oa[2 * i + 1]], writes=[b_oa2[i]])
            S.dma("sp", (lambda e, i=i: e.dma_start(out=obT2[64:128, 2 * i, :], in_=obT2[0:64, 2 * i + 1, :])), key="pb%d" % i, reads=[b_ob[2 * i + 1]], writes=[b_ob2[i]])

        def load_c(c):
            sg = wslot()
            S.dma("pool", (lambda e: e.dma_start(out=wsl[sg][:, :, 0:128], in_=w_in[:, 1312 + c * 128:1312 + (c + 1) * 128].rearrange("(kc p) n -> p kc n", p=128))),
                  key="ws%d" % sg, writes=[b_ws[sg]])
            S.dma("pool", (lambda e: e.dma_start(out=wsl[sg][:, :, 128:256], in_=w_in[:, 2336 + c * 128:2336 + (c + 1) * 128].rearrange("(kc p) n -> p kc n", p=128))),
                  key="ws%d" % sg, writes=[b_ws[sg]])
            so = wslot()
            S.dma("pool", (lambda e: e.dma_start(out=wsl[so][:, 0:4, 0:128], in_=w_oa.rearrange("(hp p) n -> p hp n", p=128)[:, :, c * 128:(c + 1) * 128])),
                  key="ws%d" % so, writes=[b_ws[so]])
            S.dma("pool", (lambda e: e.dma_start(out=wsl[so][:, 0:4, 128:256], in_=w_ob.rearrange("(hp p) n -> p hp n", p=128)[:, :, c * 128:(c + 1) * 128])),
                  key="ws%d" % so, writes=[b_ws[so]])
            return sg, so
        nxt = load_c(0)
        for c in range(8):
            sg, so = nxt
            if c + 1 < 8:
                nxt = load_c(c + 1)
            for tt in range(2):
                tsl = slice(tt * 512, (tt + 1) * 512)
                res = []
                for br in range(2):
                    pg, bpg = ps()

                    def mmg(e, pg=pg, br=br, sg=sg, tsl=tsl):
                        for kc in range(8):
                            ins = e.matmul(pg[:], lhsT=wsl[sg][:, kc, br * 128:(br + 1) * 128], rhs=hT[:, kc, tsl], start=(kc == 0), stop=(kc == 7))
                        return ins
                    S.op("pe", mmg, reads=[b_ws[sg]] + BH(tt * 4, (tt + 1) * 4), writes=[bpg])
                    pp, bpp = ps()
                    oT, bo = (oaT2, b_oa + b_oa2) if br == 0 else (obT2, b_ob + b_ob2)

                    def mmo(e, pp=pp, br=br, so=so, oT=oT, tsl=tsl):
                        for hp in range(4):
                            ins = e.matmul(pp[:], lhsT=wsl[so][:, hp, br * 128:(br + 1) * 128], rhs=oT[:, 2 * hp, tsl], start=(hp == 0), stop=(hp == 3))
                        return ins
                    S.op("pe", mmo, reads=[b_ws[so]] + bo, writes=[bpp])
                    si = rot("sc", NSC)
                    S.op("act", (lambda e, pg=pg, si=si: e.activation(out=sc32[si][:], in_=pg[:], func=AF.Sigmoid)), reads=[bpg], writes=[b_sc[si]])
                    S.op("dve", (lambda e, pp=pp, si=si: e.tensor_tensor(out=sc32[si][:], in0=sc32[si][:], in1=pp[:], op=ALU.mult)), reads=[b_sc[si], bpp], writes=[b_sc[si]])
                    res.append(si)
                S.op("dve", (lambda e, res=res, c=c, tsl=tsl: e.tensor_tensor(out=mT[:, c, tsl], in0=sc32[res[0]][:], in1=sc32[res[1]][:], op=ALU.add)),
                     reads=[b_sc[res[0]], b_sc[res[1]]], writes=[b_m[c][tt]])
        nxt = load_w_cols(w_out, 0, 256)
        for cq in range(4):
            s = nxt
            if cq + 1 < 4:
                nxt = load_w_cols(w_out, (cq + 1) * 256, 256)
            for t in range(8):
                pd, bpd = ps()

                def mmw(e, pd=pd, s=s, t=t):
                    for kc in range(8):
                        ins = e.matmul(pd[:, 0:256], lhsT=mT[:, kc, t * 128:(t + 1) * 128], rhs=wsl[s][:, kc, 0:256], start=(kc == 0), stop=(kc == 7))
                    return ins
                S.op("pe", mmw, reads=[b_ws[s]] + [b_m[kc][t // 4] for kc in range(8)], writes=[bpd])
                resid_add(t, cq * 256, 256, pd, bpd)

    def exchange_kv():
        snda = snd.ap()
        rcva = rcv.ap()
        b_snd = [Buf("snd%d" % i) for i in range(5)]
        b_rcv = Buf("rcv")
        S.dma("sp", lambda e: e.dma_start(out=snda[:, 0:1024], in_=KaT[:, K_OWN:K_OWN + 1024]), key="xs0", reads=[b_ka["own"]], writes=[b_snd[0]])
        for kv in range(2):
            S.dma("sp", (lambda e, kv=kv: e.dma_start(out=snda[:, 1024:2048].rearrange("p (n k d) -> p n k d", k=2, d=64)[:, :, kv, :], in_=Va[:, 4:12, kv, 0:64])),
                  key="xs%d" % (1 + kv), reads=[b_va["own"]], writes=[b_snd[1 + kv]])
        S.dma("sp", lambda e: e.dma_start(out=snda[:, 2048:4096].rearrange("p (k t) -> p k t", k=2), in_=CkvT[:, :, K_OWN:K_OWN + 1024]), key="xs3", reads=[b_ckv["own"]], writes=[b_snd[3]])
        S.dma("sp", lambda e: e.dma_start(out=kr_snd.ap(), in_=KrT[64:96, K_OWN:K_OWN + 1024]), key="xs4", reads=[b_kr["own"]], writes=[b_snd[4]])
        b_krr = Buf("kr_rcv")
        S.dma("pool", lambda e: e.collective_compute("AllGather", ALU.bypass, replica_groups=[[0, 1], [2, 3], [4, 5], [6, 7]],
                                                     ins=[kr_snd.ap().opt()], outs=[kr_rcv.ap().opt()]),
              key="ag2", reads=[b_snd[4]], writes=[b_krr], inc=1)
        S.dma("pool", lambda e: e.collective_compute("AllGather", ALU.bypass, replica_groups=[[0, 1], [2, 3], [4, 5], [6, 7]],
                                                     ins=[snd.ap().opt()], outs=[rcv.ap().opt()]),
              key="ag", reads=b_snd[0:4], writes=[b_rcv], inc=1)
        S.dma("sp", lambda e: e.dma_start(out=KaT[:, K_OTH + 896:K_OTH + 1024], in_=rcva[0:128, 896:1024]), key="xl0", reads=[b_rcv], writes=[b_ka["oth"]])
        S.dma("sp", lambda e: e.dma_start(out=KaT[:, K_OTH:K_OTH + 128], in_=rcva[128:256, 0:128]), key="xl0", reads=[b_rcv], writes=[b_ka["oth"]])
        S.dma("sp", lambda e: e.dma_start(out=Va[:, 19, :, 0:64], in_=rcva[0:128, 1920:2048].rearrange("p (k d) -> p k d", k=2)), key="xl1", reads=[b_rcv], writes=[b_va["oth"]])
        S.dma("sp", lambda e: e.dma_start(out=Va[:, 12, :, 0:64], in_=rcva[128:256, 1024:1152].rearrange("p (k d) -> p k d", k=2)), key="xl1", reads=[b_rcv], writes=[b_va["oth"]])
        S.dma("sp", lambda e: e.dma_start(out=CkvT[:, :, K_OWN:K_OWN + 1024], in_=rcva[0:128, 2048:4096].rearrange("p (k t) -> p k t", k=2)), key="xl2", reads=[b_rcv], writes=[b_ckv["own"]])
        S.dma("sp", lambda e: e.dma_start(out=CkvT[:, :, K_OTH:K_OTH + 1024], in_=rcva[128:256, 2048:4096].rearrange("p (k t) -> p k t", k=2)), key="xl3", reads=[b_rcv], writes=[b_ckv["oth"]])
        S.dma("sp", lambda e: e.dma_start(out=KrT[64:96, K_OWN:K_OWN + 1024], in_=kr_rcv.ap()[0:32, :]), key="xl4", reads=[b_krr], writes=[b_kr["own"]])
        S.dma("sp", lambda e: e.dma_start(out=KrT[64:96, K_OTH:K_OTH + 1024], in_=kr_rcv.ap()[32:64, :]), key="xl5", reads=[b_krr], writes=[b_kr["oth"]])

    ffn_bufs = [b for row in b_act for b in row] + b_wd
    att1_bufs = b_oa + b_ob + b_kb + b_kbr + b_oa2 + b_ob2
    att2_bufs = b_qa + list(b_ka.values()) + list(b_va.values()) + [b_va1]
    m_bufs = [b for row in b_m for b in row]

    def set_va_ones(t0, t1):
        S.op("pool", (lambda e: e.memset(Va[:, t0:t1, :, 64:66], 1.0)), writes=[b_va1])

    def program(stage):
        for i in range(2):
            S.op("pool", (lambda e, i=i: e.memset(Vb[i][:, :, 64:66], 1.0)), writes=[b_vb1[i]])
        load_x(1)
        S.dma("sp", lambda e: e.dma_start(out=ropA[:], in_=ropeA[0].rearrange("c p t -> p c t")), key="ropA", writes=[b_ropA])
        S.dma("sp", lambda e: e.dma_start(out=ropB[:], in_=ropeB[0].rearrange("c p t -> p c t")), key="ropB", writes=[b_ropB])
        ffn(0, 0, 0, hook=(lambda: (ada_chunk(), ada_chunk())))
        while ada_state["cc"] < 36:
            ada_chunk()
        set_va_ones(0, 20)
        load_ctx()
        norm_to_hT(1, 0, gate=(1, 0, False))
        mix_proj("S", "own", 0, True, True, False)
        exchange_kv()
        if stage == 2:
            final_out(0)
            return
        fence(ffn_bufs + att1_bufs)
        attention("S")
        if stage == 3:
            final_out(0)
            return
        fence(att2_bufs + m_bufs)
        merge(0)
        if stage == 4:
            final_out(0)
            return
        fence(att1_bufs + ffn_bufs)
        ffn(1, 2, 0)
        final_out(0)
        if stage == 5:
            return
        load_x(2)
        ffn(0, 0, 1)
        norm_to_hT(1, 1, gate=(1, 1, False))
        fence(m_bufs + att2_bufs)
        set_va_ones(4, 12)
        mix_proj("P", "own", 0, False, True, True)
        if stage == 6:
            final_out(1)
            return
        fence(ffn_bufs + att1_bufs)
        attention("P")
        if stage == 7:
            final_out(1)
            return
        fence(att2_bufs + m_bufs)
        merge(1)
        fence(att1_bufs + ffn_bufs)
        ffn(1, 2, 1)
        final_out(1)


    program(DBG.get("stage", 99))

    S.wait_all("sp", out_ops)
    S.emit(nc, st)
    st.close()
    return nc


_NC = {}


def _col(v, n):
    return np.ascontiguousarray(np.asarray(v, np.float32).reshape(n, 128).T)


def make_in_maps(inp):
    c = _consts()
    f = lambda a: np.ascontiguousarray(np.asarray(a, dtype=np.float32))
    x_prompt, x_sample = f(inp["x_prompt"]), f(inp["x_sample"])
    shared = dict(
        ada_w=f(inp["ada_w"][0]), ada_bc=_col(inp["ada_b"][0], 72),
        ncol=np.concatenate([_col(inp["ffn1_norm"][0], 8), _col(inp["mix_norm"][0], 8), _col(inp["ffn2_norm"][0], 8)], axis=1),
        fnorm=f(inp["final_norm"]).reshape(1, D),
        w_gu1=f(inp["ffn1_w_gu"][0]), w_d1=f(inp["ffn1_w_down"][0]), w_gu2=f(inp["ffn2_w_gu"][0]), w_d2=f(inp["ffn2_w_down"][0]),
        w_in=f(inp["w_in"][0]),
        lncol=np.concatenate([_col(inp["q_lat_norm"][0], 2), _col(inp["kv_lat_norm"][0], 2)], axis=1),
        kvln=f(inp["kv_lat_norm"][0]).reshape(1, 256),
        w_uq=f(inp["w_uq"][0]), w_ukv=f(inp["w_ukv"][0]), w_oa=f(inp["w_o_a"][0]), w_ob=f(inp["w_o_b"][0]), w_out=f(inp["w_out"][0]),
        ident=c["ident"], RA=c["RA"], RB=c["RB"],
    )
    sink65 = np.zeros((65, 8), np.float32)
    sink65[64] = f(inp["attn_sink"][0])
    shared["sink65"] = sink65
    maps = []
    for core in range(8):
        b, h = core // 2, core % 2
        xg = np.stack([x_sample[b, (1 - h) * 1024:(2 - h) * 1024], x_sample[b, h * 1024:(h + 1) * 1024],
                       x_prompt[4 * core:4 * core + 4].reshape(1024, D)], axis=0)
        cvec = np.stack([f(inp["c"])[b], f(inp["c_ctx"])], axis=0)
        cvfm = np.ascontiguousarray(cvec.reshape(2, 8, 128).transpose(2, 1, 0).reshape(128, 16))
        flags = np.zeros((128, 2), np.float32)
        flags[:, 0] = float(h)
        flags[:, 1] = float(1 - h)
        d = dict(shared)
        d.update(
            xg=np.ascontiguousarray(xg), cvfm=cvfm,
            ck=f(inp["cache_attn_k"][b, 0]).reshape(512, 128), cv=f(inp["cache_attn_v"][b, 0]).reshape(512, 128),
            cckv=f(inp["cache_mla_ckv"][b, 0]), ckr=f(inp["cache_mla_krope"][b, 0]),
            ropeA=np.ascontiguousarray(c["ropeA"][[h, 1 - h]]), ropeB=np.ascontiguousarray(c["ropeB"][[h, 1 - h]]),
            flags=flags,
        )
        maps.append(d)
    return maps


def kernel(**inputs):
    dbg = tuple(DBG.get("dbg", ()))
    key = (dbg, DBG.get("stage", 99), DBG.get("po", 7), DBG.get("pcall", 8))
    if key not in _NC:
        _NC[key] = build_nc(dbg)
    nc = _NC[key]
    maps = make_in_maps(inputs)
    res = run_bass_kernel_spmd(nc, maps, core_ids=list(range(8)))
    R = res.results
    DBG["results"] = R
    y_prompt = np.concatenate([R[c]["y"][1].reshape(4, 256, D) for c in range(8)], axis=0)
    y_sample = np.stack([np.concatenate([R[2 * b]["y"][0], R[2 * b + 1]["y"][0]], axis=0) for b in range(4)], axis=0)
    nk = np.concatenate([R[c]["nk"].reshape(4, 1, 256, 2, 64) for c in range(8)], axis=0)
    nv = np.concatenate([R[c]["nv"].reshape(4, 1, 256, 2, 64) for c in range(8)], axis=0)
    nckv = np.concatenate([R[c]["nckv"].reshape(4, 1, 256, 256) for c in range(8)], axis=0)
    nkr = np.concatenate([R[c]["nkr"].reshape(4, 1, 256, 32) for c in range(8)], axis=0)
    return (y_prompt.astype(np.float32), y_sample.astype(np.float32), nk.astype(np.float32), nv.astype(np.float32),
            nckv.astype(np.float32), nkr.astype(np.float32))
```

```python
import numpy as np
from contextlib import ExitStack
import concourse.bass as bass
import concourse.mybir as mybir
from concourse.bass_utils import run_bass_kernel_spmd

F32 = mybir.dt.float32
BF16 = mybir.dt.bfloat16
AF = mybir.ActivationFunctionType
ALU = mybir.AluOpType

D = 1024
DFF = 2816
NCH = 22
NBLK = 2
CPB = NCH // NBLK
INW = 3360
EPS = 1e-6
A_SCALE = 64 ** -0.5
B_SCALE = 96 ** -0.5
NKEY = 2560
K_CTX, K_OWN, K_OTH = 0, 512, 1536

ENGS = ("pe", "act", "dve", "pool", "sp")


class Buf:
    __slots__ = ("name", "w", "r", "rd")

    def __init__(self, name=""):
        self.name = name
        self.w = None
        self.r = {}
        self.rd = []


class Op:
    __slots__ = ("eng", "fn", "deps", "needs_inc", "val", "sem", "is_dma", "key", "inc")

    def __init__(self, eng, fn, is_dma=False):
        self.eng = eng
        self.fn = fn
        self.deps = []
        self.needs_inc = False
        self.val = None
        self.sem = None
        self.is_dma = is_dma
        self.key = None


class Sched:
    def __init__(self):
        self.ops = {e: [] for e in ENGS}
        self.dma_cnt = {}

    def _add(self, o, reads, writes):
        deps = {}

        def add(d):
            if d is None:
                return
            if (not d.is_dma) and (not o.is_dma) and d.eng == "pe" and o.eng == "pe":
                return
            deps[id(d)] = d
        for b in reads:
            add(b.w)
        for b in writes:
            add(b.w)
            for d in b.r.values():
                add(d)
            for d in b.rd:
                add(d)
        o.deps = list(deps.values())
        for d in o.deps:
            d.needs_inc = True
        for b in reads:
            if o.is_dma:
                b.rd.append(o)
            else:
                b.r[o.eng] = o
        for b in writes:
            b.w = o
            b.r = {}
            b.rd = []
        self.ops[o.eng].append(o)
        return o

    def op(self, eng, fn, reads=(), writes=()):
        return self._add(Op(eng, fn), reads, writes)

    def dma(self, eng, fn, key, reads=(), writes=(), inc=16):
        o = Op(eng, fn, is_dma=True)
        o.key = key
        o.inc = inc
        c = self.dma_cnt.get(key, 0) + 1
        self.dma_cnt[key] = c
        o.val = inc * c
        o.needs_inc = True
        return self._add(o, reads, writes)

    def wait_all(self, eng, ops):
        o = Op(eng, None)
        o.deps = list(ops)
        for d in o.deps:
            d.needs_inc = True
        self.ops[eng].append(o)
        return o

    def emit(self, nc, stack):
        esem = {e: stack.enter_context(nc.semaphore("s_" + e)) for e in ENGS}
        dsem = {}
        for i, k in enumerate(self.dma_cnt):
            dsem[k] = stack.enter_context(nc.semaphore("d%d" % i))
        for e in ENGS:
            c = 0
            for o in self.ops[e]:
                if o.is_dma:
                    o.sem = dsem[o.key]
                elif o.needs_inc:
                    c += 1
                    o.val = c
                    o.sem = esem[e]

        def run(e, eng):
            known = {}
            for o in self.ops[e]:
                need = {}
                for d in o.deps:
                    k = id(d.sem)
                    if k not in need or need[k][1] < d.val:
                        need[k] = (d.sem, d.val)
                for k, (sem, val) in need.items():
                    if known.get(k, 0) >= val:
                        continue
                    eng.wait_ge(sem, val)
                    known[k] = val
                if o.fn is None:
                    continue
                ins = o.fn(eng)
                if o.is_dma:
                    if o.inc == 16:
                        ins.then_inc(o.sem, 16)
                    else:
                        ins.then_inc(o.sem)
                elif o.needs_inc:
                    ins.then_inc(o.sem, 1)

        with nc.Block() as block:
            @block.tensor
            def _(eng):
                run("pe", eng)

            @block.scalar
            def _(eng):
                run("act", eng)

            @block.vector
            def _(eng):
                run("dve", eng)

            @block.gpsimd
            def _(eng):
                run("pool", eng)

            @block.sync
            def _(eng):
                run("sp", eng)


def _rope_tables(n, d_rot):
    rows = n // 64
    t_row = np.repeat(np.arange(rows, dtype=np.float32), 64)
    t_col = np.tile(np.arange(64, dtype=np.float32), rows)
    d_half = d_rot // 2
    inv = (1.0 / (np.float32(10000.0) ** (np.arange(0, d_half, 2, dtype=np.float32) / np.float32(d_half)))).astype(np.float32)
    ar = t_row[:, None] * inv[None, :]
    ac = t_col[:, None] * inv[None, :]
    ang = np.concatenate([ar, ar, ac, ac], axis=-1).astype(np.float32)
    return np.cos(ang).astype(np.float32), np.sin(ang).astype(np.float32)


def _rot_T(d_rot):
    s = d_rot // 4
    R = np.zeros((d_rot, d_rot), np.float32)
    for i in range(s):
        R[i, i + s] = -1.0
        R[i + s, i] = 1.0
        R[i + 2 * s, i + 3 * s] = -1.0
        R[i + 3 * s, i + 2 * s] = 1.0
    return np.ascontiguousarray(R.T)


_CONST = {}


def _consts():
    if _CONST:
        return _CONST
    cosA, sinA = _rope_tables(2048, 64)
    cosB, sinB = _rope_tables(2048, 32)
    ropeA = np.zeros((2, 2, 128, 1024), np.float32)
    ropeB = np.zeros((2, 2, 96, 1024), np.float32)
    for hh in range(2):
        sl = slice(hh * 1024, (hh + 1) * 1024)
        ropeA[hh, 0] = np.concatenate([cosA[sl].T, cosA[sl].T], 0)
        ropeA[hh, 1] = np.concatenate([sinA[sl].T, sinA[sl].T], 0)
        ropeB[hh, 0, :64] = 1.0
        ropeB[hh, 0, 64:] = cosB[sl].T
        ropeB[hh, 1, 64:] = sinB[sl].T
    RA = np.zeros((128, 128), np.float32)
    r64 = _rot_T(64)
    RA[:64, :64] = r64
    RA[64:, 64:] = r64
    RB = np.zeros((96, 96), np.float32)
    RB[64:, 64:] = _rot_T(32)
    _CONST.update(ropeA=ropeA, ropeB=ropeB, RA=RA, RB=RB, ident=np.eye(128, dtype=np.float32))
    return _CONST


DBG = {}


def build_nc(dbg=()):
    nc = bass.Bass("TRN2", target_bir_lowering=False)
    S = Sched()

    def din(name, shape):
        return nc.dram_tensor(name, list(shape), F32, kind="ExternalInput").ap()

    def dout(name, shape):
        return nc.dram_tensor(name, list(shape), F32, kind="ExternalOutput").ap()

    xg = din("xg", [3, 1024, D])
    cvfm = din("cvfm", [128, 16])
    ck = din("ck", [512, 128])
    cv = din("cv", [512, 128])
    cckv = din("cckv", [512, 256])
    ckr = din("ckr", [512, 32])
    ada_w = din("ada_w", [D, 9 * D])
    ada_bc = din("ada_bc", [128, 72])
    ncol = din("ncol", [128, 24])
    fnorm = din("fnorm", [1, D])
    w_gu = [din("w_gu1", [D, 2 * DFF]), din("w_gu2", [D, 2 * DFF])]
    w_dn = [din("w_d1", [DFF, D]), din("w_d2", [DFF, D])]
    w_in = din("w_in", [D, INW])
    sink65 = din("sink65", [65, 8])
    lncol = din("lncol", [128, 4])
    kvln = din("kvln", [1, 256])
    w_uq = din("w_uq", [256, 768])
    w_ukv = din("w_ukv", [256, 1024])
    w_oa = din("w_oa", [512, D])
    w_ob = din("w_ob", [512, D])
    w_out = din("w_out", [D, D])
    identd = din("ident", [128, 128])
    ropeA = din("ropeA", [2, 2, 128, 1024])
    ropeB = din("ropeB", [2, 2, 96, 1024])
    RAd = din("RA", [128, 128])
    RBd = din("RB", [96, 96])
    flagsd = din("flags", [128, 2])

    y = dout("y", [2, 1024, D])
    nk = dout("nk", [1024, 128])
    nv = dout("nv", [1024, 128])
    nckv = dout("nckv", [1024, 256])
    nkr = dout("nkr", [1024, 32])
    dbg_out = {}
    for name, shape in dbg:
        dbg_out[name] = dout(name, shape)

    XW = 4096
    snd = nc.dram_tensor("kv_snd", [128, XW], BF16)
    rcv = nc.dram_tensor("kv_rcv", [256, XW], BF16)
    kr_snd = nc.dram_tensor("kr_snd", [32, 1024], BF16)
    kr_rcv = nc.dram_tensor("kr_rcv", [64, 1024], BF16)
    st = ExitStack()
    out_ops = []

    def sb(name, shape, dt=F32):
        return st.enter_context(nc.sbuf_tensor(name, list(shape), dt))

    x = sb("x", [128, 8, D]);                 bx = [Buf("x%d" % t) for t in range(8)]
    hT = sb("hT", [128, 8, 1024], BF16);      bhA = [Buf("hTa%d" % t) for t in range(8)]; bhD = [Buf("hTd%d" % t) for t in range(8)]

    def BH(t0, t1):
        return bhA[t0:t1] + bhD[t0:t1]
    ar1 = sb("ar1", [128, 22528], BF16)
    actT = ar1[:, 0:CPB * 1024].rearrange("p (j t) -> p j t", j=CPB)
    wd = ar1[:, CPB * 1024:2 * CPB * 1024].rearrange("p (j d) -> p j d", j=CPB)
    oaT = ar1[0:65, 0:8192].rearrange("p (h t) -> p h t", h=8)
    obT = ar1[0:65, 8192:16384].rearrange("p (h t) -> p h t", h=8)
    oaT2 = ar1[:, 0:8192].rearrange("p (h t) -> p h t", h=8)
    obT2 = ar1[:, 8192:16384].rearrange("p (h t) -> p h t", h=8)
    b_oa2 = [Buf("oa2_%d" % i) for i in range(4)]
    b_ob2 = [Buf("ob2_%d" % i) for i in range(4)]
    KbT = [ar1[0:96, 16384 + i * NKEY:16384 + (i + 1) * NKEY] for i in range(2)]
    b_act = [[Buf("act%d_%d" % (j, tt)) for tt in range(2)] for j in range(CPB)]
    b_wd = [Buf("wd%d" % j) for j in range(CPB)]
    b_oa = [Buf("oa%d" % h) for h in range(8)]
    b_ob = [Buf("ob%d" % h) for h in range(8)]
    b_kb = [Buf("kb%d" % i) for i in range(2)]
    b_kbr = [Buf("kbr%d" % i) for i in range(2)]
    NSLOT = 4
    wsl = [sb("wsl%d" % i, [128, 8, 256], BF16) for i in range(NSLOT)]
    b_ws = [Buf("ws%d" % i) for i in range(NSLOT)]
    ar2 = sb("ar2", [128, 9472], BF16)
    QaT = ar2[:, 0:4096].rearrange("p (g t) -> p g t", g=4)
    KaT = ar2[:, 4096:4096 + NKEY]
    Va = ar2[:, 6656:6656 + 20 * 132].rearrange("p (n k e) -> p n k e", n=20, k=2)
    mT = ar2[:, 0:8192].rearrange("p (c t) -> p c t", c=8)
    b_qa = [Buf("qa%d" % g) for g in range(4)]
    b_ka = {k: Buf("ka_" + k) for k in ("ctx", "own", "oth")}
    b_va = {k: Buf("va_" + k) for k in ("ctx", "own", "oth")}
    b_va1 = Buf("va_ones")
    b_m = [[Buf("m%d_%d" % (c, tt)) for tt in range(2)] for c in range(8)]
    CkvT = sb("CkvT", [128, 2, NKEY], BF16);  b_ckv = {k: Buf("ckv_" + k) for k in ("ctx", "own", "oth")}
    KrT = sb("KrT", [96, NKEY], BF16);        b_kr = {k: Buf("kr_" + k) for k in ("ctx", "own", "oth")}
    qlT = sb("qlT", [128, 2, 1024], BF16);    b_ql = [Buf("ql%d" % t) for t in range(8)]
    QbT = [sb("QbT%d" % i, [96, 1024], BF16) for i in range(2)]
    b_qb = [Buf("qb%d" % i) for i in range(2)]
    Vb = [sb("Vb%d" % i, [128, 20, 66], BF16) for i in range(2)]
    b_vb = [Buf("vb%d" % i) for i in range(2)]
    b_vb1 = [Buf("vb1_%d" % i) for i in range(2)]
    NPT = 4
    PT = [sb("PT%d" % i, [128, 512], BF16) for i in range(NPT)]
    b_pt = [Buf("pt%d" % i) for i in range(NPT)]
    ropA = sb("ropA", [128, 2, 1024]);        b_ropA = Buf("ropA")
    ropB = sb("ropB", [96, 2, 1024]);         b_ropB = Buf("ropB")
    Gbc = sb("Gbc", [128, D]);                b_G = Buf("Gbc")
    fgbc = sb("fgbc", [128, D]);              b_fg = Buf("fg")
    kvlnbc = sb("kvlnbc", [128, 256]);        b_kvln = Buf("kvlnbc")
    ident32 = sb("ident32", [128, 128]);      b_id32 = Buf("id32")
    identb = sb("identb", [128, 128], BF16);  b_idb = Buf("idb")
    RAt = sb("RAt", [128, 128]);              b_RA = Buf("RA")
    RBt = sb("RBt", [96, 96]);                b_RB = Buf("RB")
    ones32 = sb("ones32", [128, 128]);        b_ones = Buf("ones32")
    onesb = sb("onesb", [128, 64], BF16);     b_onesb = Buf("onesb")
    esinkb = sb("esinkb", [64, 8]);           b_esinkb = Buf("esinkb")
    diag = [sb("diag%d" % i, [128, 128]) for i in range(2)]
    b_diag = [Buf("diag%d" % i) for i in range(2)]
    flags = sb("flags_sb", [128, 2]);         b_flags = Buf("flags")
    esink = sb("esink", [65, 8]);             b_esink = Buf("esink")
    cvs = sb("cvs", [128, 16]);               b_cvs = Buf("cvs")
    cvb = sb("cvb", [128, 8, 2], BF16);       b_cvb = Buf("cvb")
    adab = sb("adab", [128, 72]);             b_adab = Buf("adab")
    ncols = sb("ncols", [128, 24]);           b_ncols = Buf("ncols")
    lncols = sb("lncols", [128, 4]);          b_lncols = Buf("lncols")
    modc = sb("modc", [128, 72, 2]);          b_modc = Buf("modc")
    Acol = sb("Acol", [128, 3, 8, 2]);        b_Acol = Buf("Acol")
    ss = sb("ss", [128, 8]);                  b_ss = Buf("ss")
    rstd = sb("rstd", [128, 8]);              b_rstd = Buf("rstd")
    ss2 = sb("ss2", [128, 16]);               b_ss2 = Buf("ss2")
    b_ss2c = [Buf("ss2_%d" % i) for i in range(16)]
    b_rs2c = [Buf("rs2_%d" % i) for i in range(16)]
    rs2 = sb("rs2", [128, 16]);               b_rs2 = Buf("rs2")
    epsc = sb("epsc", [128, 1]);              b_eps = Buf("eps")
    junk = sb("junk", [128, D], BF16);        b_junk = Buf("junk")
    xn = [sb("xn%d" % i, [128, D], BF16) for i in range(2)]
    b_xn = [Buf("xn%d" % i) for i in range(2)]
    NSC = 3
    sc32 = [sb("sc32_%d" % i, [128, 512]) for i in range(NSC)]
    b_sc = [Buf("sc32_%d" % i) for i in range(NSC)]
    rden2 = sb("rden", [65, 2, 512]);         b_rden2 = [Buf("rden0"), Buf("rden1")]
    ctm = sb("ctm", [128, 4, 352], BF16);     b_ctm = Buf("ctm")
    fdum = sb("fdum", [128, 8]);              b_fd = Buf("fdum")

    PSB = [st.enter_context(nc.psum_tensor("psb%d" % i, [128, 512], F32)) for i in range(8)]
    b_ps = [Buf("ps%d" % i) for i in range(8)]
    ps_ctr = [0]

    ps_res = set()

    def ps(reserve=False):
        while True:
            i = ps_ctr[0] % 8
            ps_ctr[0] += 1
            if i not in ps_res:
                break
        if reserve:
            ps_res.add(i)
        return PSB[i], b_ps[i]

    def ps_release(bank):
        for i, b in enumerate(PSB):
            if b is bank:
                ps_res.discard(i)

    rr = {}

    def rot(name, n):
        i = rr.get(name, 0)
        rr[name] = i + 1
        return i % n

    def fence(bufs):
        S.op("pool", lambda e: e.memset(fdum[:, 0:1], 0.0), writes=list(bufs) + [b_fd])

    def dbg_dump(name, src_ap, bufs):
        if name in dbg_out:
            o = S.dma("sp", lambda e: e.dma_start(out=dbg_out[name], in_=src_ap), key="dbg_" + name, reads=bufs)
            out_ops.append(o)

    def ld(dst, src, buf, key, q="sp"):
        return S.dma(q, lambda e: e.dma_start(out=dst, in_=src), key=key, writes=[buf])

    ld(ident32[:], identd[:, :], b_id32, "c0")
    ld(identb[:], identd[:, :], b_idb, "c1", q="pool")
    ld(RAt[:], RAd[:, :], b_RA, "c2")
    ld(RBt[:], RBd[:, :], b_RB, "c3")
    ld(flags[:], flagsd[:, :], b_flags, "c4")
    ld(esink[:], sink65[:, :], b_esink, "c5")
    ld(cvs[:], cvfm[:, :], b_cvs, "c6")
    ld(adab[:], ada_bc[:, :], b_adab, "c7")
    ld(ncols[:], ncol[:, :], b_ncols, "c8")
    ld(lncols[:], lncol[:, :], b_lncols, "c9")
    ld(fgbc[:], fnorm.partition_broadcast(128), b_fg, "c10")
    ld(kvlnbc[:], kvln.partition_broadcast(128), b_kvln, "c11")
    S.op("pool", lambda e: e.memset(ones32[:], 1.0), writes=[b_ones])
    S.op("pool", lambda e: e.memset(onesb[:], 1.0), writes=[b_onesb])
    ld(esinkb[:], sink65[64:65, :].partition_broadcast(64), b_esinkb, "c12")
    S.op("act", lambda e: e.activation(out=esinkb[:], in_=esinkb[:], func=AF.Exp), reads=[b_esinkb], writes=[b_esinkb])
    S.op("pool", lambda e: e.memset(epsc[:], EPS), writes=[b_eps])
    S.op("act", lambda e: e.activation(out=esink[64:65, :], in_=esink[64:65, :], func=AF.Exp), reads=[b_esink], writes=[b_esink])

    S.op("act", lambda e: e.activation(out=cvb[:].rearrange("p k m -> p (k m)"), in_=cvs[:], func=AF.Silu), reads=[b_cvs], writes=[b_cvb])
    pm, bpm = ps(reserve=True)
    b_modc_n = [Buf("modc%d" % n) for n in range(3)]
    b_Acol_n = [Buf("Acol%d" % n) for n in range(3)]
    ada_state = {"cc": 0}

    def ada_finish(n):
        S.op("dve", lambda e: e.tensor_tensor(out=modc[:, 24 * n:24 * n + 24, :], in0=pm[:, 48 * n:48 * n + 48].rearrange("p (c m) -> p c m", m=2),
                                              in1=adab[:, 24 * n:24 * n + 24].unsqueeze(2).to_broadcast([128, 24, 2]), op=ALU.add),
             reads=[bpm, b_adab], writes=[b_modc_n[n]])
        S.op("dve", (lambda e: e.tensor_scalar(out=Acol[:, n], in0=modc[:, (3 * n + 1) * 8:(3 * n + 2) * 8, :], scalar1=1.0, scalar2=None, op0=ALU.add)),
             reads=[b_modc_n[n]], writes=[b_Acol_n[n]])
        S.op("dve", (lambda e: e.tensor_tensor(out=Acol[:, n], in0=Acol[:, n], in1=ncols[:, n * 8:(n + 1) * 8].unsqueeze(2).to_broadcast([128, 8, 2]), op=ALU.mult)),
             reads=[b_Acol_n[n], b_ncols], writes=[b_Acol_n[n]])
        if n == 2:
            ps_release(pm)

    def ada_chunk():
        cc = ada_state["cc"]
        if cc >= 36:
            return
        ada_state["cc"] += 1
        s = rot("ws", NSLOT)
        S.dma("pool", (lambda e: e.dma_start(out=wsl[s][:], in_=ada_w[:, cc * 256:(cc + 1) * 256].rearrange("(kc p) n -> p kc n", p=128))),
              key="ws%d" % s, writes=[b_ws[s]])

        def mm_ada(e):
            for sub in range(2):
                ch = cc * 2 + sub
                for kc in range(8):
                    ins = e.matmul(pm[:, ch * 2:ch * 2 + 2], lhsT=wsl[s][:, kc, sub * 128:(sub + 1) * 128], rhs=cvb[:, kc, :], start=(kc == 0), stop=(kc == 7))
            return ins
        S.op("pe", mm_ada, reads=[b_ws[s], b_cvb], writes=[bpm])
        if cc % 12 == 11:
            ada_finish(cc // 12)

    for _ in range(12):
        ada_chunk()
    dbg_dump("d_modc", modc[:].rearrange("p c m -> p (c m)"), b_modc_n)

    def A_ap(n, kc, m):
        return Acol[:, n, kc, m:m + 1]

    def B_ap(n, kc, m):
        return modc[:, 3 * n * 8 + kc, m:m + 1]

    def load_x(gi):
        for t in range(8):
            S.dma("sp", (lambda e, t=t: e.dma_start(out=x[:, t, :], in_=xg[gi, t * 128:(t + 1) * 128, :])), key="x%d" % t, writes=[bx[t]])

    def gate_steps(n, m, half_scale):
        sc = 0.5 if half_scale else 1.0
        for kc in range(8):
            di = rot("diag", 2)
            S.op("dve", (lambda e, di=di, kc=kc: e.tensor_scalar(out=diag[di][:], in0=ident32[:], scalar1=modc[:, (3 * n + 2) * 8 + kc, m:m + 1], scalar2=sc, op0=ALU.mult, op1=ALU.mult)),
                 reads=[b_id32, b_modc_n[n]], writes=[b_diag[di]])
            pg, bpg = ps()
            S.op("pe", (lambda e, di=di, pg=pg: e.matmul(pg[:, 0:128], lhsT=ones32[:], rhs=diag[di][:], start=True, stop=True)),
                 reads=[b_ones, b_diag[di]], writes=[bpg])
            S.op("act", (lambda e, pg=pg, kc=kc: e.activation(out=Gbc[:, kc * 128:(kc + 1) * 128], in_=pg[:, 0:128], func=AF.Copy)),
                 reads=[bpg], writes=[b_G])
            yield

    def norm_to_hT(n, m, gate=None):
        gs = gate_steps(*gate) if gate is not None else iter(())
        for t in range(8):
            S.op("act", (lambda e, t=t: e.activation(out=junk[:], in_=x[:, t, :], func=AF.Square, accum_out=ss[:, t:t + 1])),
                 reads=[bx[t]], writes=[b_junk, b_ss])
        S.op("act", lambda e: e.activation(out=rstd[:], in_=ss[:], func=AF.Sqrt, scale=1.0 / D, bias=epsc[:]), reads=[b_ss, b_eps], writes=[b_rstd])
        S.op("dve", lambda e: e.reciprocal(out=rstd[:], in_=rstd[:]), reads=[b_rstd], writes=[b_rstd])

        def cp(t):
            xi = rot("xn", 2)
            S.op("act", (lambda e: e.activation(out=xn[xi][:], in_=x[:, t, :], func=AF.Copy, scale=rstd[:, t:t + 1])),
                 reads=[bx[t], b_rstd], writes=[b_xn[xi]])
            return xi
        nxt = cp(0)
        for t in range(8):
            xi = nxt
            if t + 1 < 8:
                nxt = cp(t + 1)
            pt_, bpt = ps()
            ptb = pt_[:].bitcast(BF16)

            def tr(e, xi=xi, ptb=ptb):
                for kc in range(8):
                    ins = e.transpose(out=ptb[:, kc * 128:(kc + 1) * 128], in_=xn[xi][:, kc * 128:(kc + 1) * 128], identity=identb[:])
                return ins
            S.op("pe", tr, reads=[b_xn[xi], b_idb], writes=[bpt])

            def ev_d(e, t=t, ptb=ptb):
                for kc in range(8):
                    ins = e.tensor_scalar(out=hT[:, kc, t * 128:(t + 1) * 128], in0=ptb[:, kc * 128:(kc + 1) * 128],
                                          scalar1=A_ap(n, kc, m), scalar2=B_ap(n, kc, m), op0=ALU.mult, op1=ALU.add)
                return ins
            S.op("dve", ev_d, reads=[bpt, b_Acol_n[n], b_modc_n[n]], writes=[bhA[t], bhD[t]])
            next(gs, None)
        for _ in gs:
            pass

    def resid_add(t, c0, w, pd, bpd):
        si = rot("sc", NSC)
        S.op("dve", (lambda e: e.tensor_tensor(out=sc32[si][:, 0:w], in0=pd[:, 0:w], in1=Gbc[:, c0:c0 + w], op=ALU.mult)),
             reads=[bpd, b_G], writes=[b_sc[si]])
        S.op("dve", (lambda e: e.tensor_tensor(out=x[:, t, c0:c0 + w], in0=x[:, t, c0:c0 + w], in1=sc32[si][:, 0:w], op=ALU.add)),
             reads=[b_sc[si], bx[t]], writes=[bx[t]])

    def ffn(fi, n, m, hook=None):
        norm_to_hT(n, m, gate=(n, m, True))
        wg, wdn = w_gu[fi], w_dn[fi]
        for blk in range(NBLK):
            pend_wd = []
            for jj in range(CPB):
                j = blk * CPB + jj
                s = rot("ws", NSLOT)
                S.dma("pool", (lambda e, s=s, j=j: e.dma_start(out=wsl[s][:, :, 0:128], in_=wg[:, j * 128:(j + 1) * 128].rearrange("(kc p) n -> p kc n", p=128))),
                      key="ws%d" % s, writes=[b_ws[s]])
                S.dma("pool", (lambda e, s=s, j=j: e.dma_start(out=wsl[s][:, :, 128:256], in_=wg[:, DFF + j * 128:DFF + (j + 1) * 128].rearrange("(kc p) n -> p kc n", p=128))),
                      key="ws%d" % s, writes=[b_ws[s]])
                if jj >= 2:
                    jw = jj - 2
                    S.dma("pool", (lambda e, jw=jw, blk=blk: e.dma_start(out=wd[:, jw, :], in_=wdn[(blk * CPB + jw) * 128:(blk * CPB + jw + 1) * 128, :])),
                          key="wd%d" % jw, writes=[b_wd[jw]])
                for tt in range(2):
                    pa, bpa = ps()
                    pu, bpu = ps()

                    def mm_gu(e, s=s, tt=tt, pa=pa, pu=pu):
                        for kc in range(8):
                            e.matmul(pa[:], lhsT=wsl[s][:, kc, 0:128], rhs=hT[:, kc, tt * 512:(tt + 1) * 512], start=(kc == 0), stop=(kc == 7))
                        for kc in range(8):
                            ins = e.matmul(pu[:], lhsT=wsl[s][:, kc, 128:256], rhs=hT[:, kc, tt * 512:(tt + 1) * 512], start=(kc == 0), stop=(kc == 7))
                        return ins
                    S.op("pe", mm_gu, reads=[b_ws[s]] + BH(tt * 4, (tt + 1) * 4), writes=[bpa, bpu])
                    si = rot("sc", NSC)
                    S.op("act", (lambda e, si=si, pa=pa: e.activation(out=sc32[si][:], in_=pa[:], func=AF.Silu)), reads=[bpa], writes=[b_sc[si]])
                    S.op("dve", (lambda e, si=si, pu=pu, jj=jj, tt=tt: e.tensor_tensor(out=actT[:, jj, tt * 512:(tt + 1) * 512], in0=sc32[si][:], in1=pu[:], op=ALU.mult)),
                         reads=[b_sc[si], bpu], writes=[b_act[jj][tt]])
            for jw in range(CPB - 2, CPB):
                S.dma("pool", (lambda e, jw=jw, blk=blk: e.dma_start(out=wd[:, jw, :], in_=wdn[(blk * CPB + jw) * 128:(blk * CPB + jw + 1) * 128, :])),
                      key="wd%d" % jw, writes=[b_wd[jw]])
            for t in range(8):
                if hook is not None:
                    hook()
                for half in range(2):
                    pd, bpd = ps()

                    def mm_d(e, t=t, half=half, pd=pd):
                        for jj in range(CPB):
                            ins = e.matmul(pd[:], lhsT=actT[:, jj, t * 128:(t + 1) * 128], rhs=wd[:, jj, half * 512:(half + 1) * 512], start=(jj == 0), stop=(jj == CPB - 1))
                        return ins
                    S.op("pe", mm_d, reads=[b_act[jj][t // 4] for jj in range(CPB)] + b_wd, writes=[bpd])
                    resid_add(t, half * 512, 512, pd, bpd)

    def final_out(gi_out):
        for t in range(8):
            S.op("act", (lambda e, t=t: e.activation(out=junk[:], in_=x[:, t, :], func=AF.Square, accum_out=ss[:, t:t + 1])),
                 reads=[bx[t]], writes=[b_junk, b_ss])
        S.op("act", lambda e: e.activation(out=rstd[:], in_=ss[:], func=AF.Sqrt, scale=1.0 / D, bias=epsc[:]), reads=[b_ss, b_eps], writes=[b_rstd])
        S.op("dve", lambda e: e.reciprocal(out=rstd[:], in_=rstd[:]), reads=[b_rstd], writes=[b_rstd])
        for t in range(8):
            S.op("dve", (lambda e, t=t: e.scalar_tensor_tensor(out=x[:, t, :], in0=x[:, t, :], scalar=rstd[:, t:t + 1], in1=fgbc[:], op0=ALU.mult, op1=ALU.mult)),
                 reads=[bx[t], b_rstd, b_fg], writes=[bx[t]])
            o = S.dma("sp", (lambda e, t=t: e.dma_start(out=y[gi_out, t * 128:(t + 1) * 128, :], in_=x[:, t, :])), key="yo%d" % t, reads=[bx[t]])
            out_ops.append(o)

    PO = DBG.get("po", 7)
    KOFF = {"ctx": K_CTX, "own": K_OWN, "oth": K_OTH}
    VT0 = {"ctx": 0, "own": 4, "oth": 12}

    def wslot():
        s = rot("ws", NSLOT)
        return s

    def rope_evac(p1, bp1, rows, scale, ridx, tok0, out_ap, out_bufs, rope):
        if not rope:
            S.op("act", (lambda e: e.activation(out=out_ap, in_=p1[0:rows, :], func=AF.Copy, scale=scale)), reads=[bp1], writes=out_bufs)
            return
        tab, btab, Rt, bR = (ropA, b_ropA, RAt, b_RA) if rows == 128 else (ropB, b_ropB, RBt, b_RB)
        sa = rot("sc", NSC)
        S.op("act", (lambda e: e.activation(out=sc32[sa][0:rows, :], in_=p1[0:rows, :], func=AF.Copy, scale=scale)), reads=[bp1], writes=[b_sc[sa]])
        p2, bp2 = ps()
        S.op("pe", (lambda e: e.matmul(p2[0:rows, :], lhsT=Rt[0:rows, 0:rows], rhs=sc32[sa][0:rows, :], start=True, stop=True)), reads=[b_sc[sa], bR], writes=[bp2])
        sb_ = rot("sc", NSC)
        S.op("dve", (lambda e: e.tensor_tensor(out=sc32[sb_][0:rows, :], in0=p2[0:rows, :], in1=tab[0:rows, 1, tok0:tok0 + 512], op=ALU.mult)),
             reads=[bp2, btab], writes=[b_sc[sb_]])
        S.op("dve", (lambda e: e.tensor_tensor(out=sc32[sa][0:rows, :], in0=sc32[sa][0:rows, :], in1=tab[0:rows, 0, tok0:tok0 + 512], op=ALU.mult)),
             reads=[b_sc[sa], btab], writes=[b_sc[sa]])
        S.op("dve", (lambda e: e.tensor_tensor(out=out_ap, in0=sc32[sa][0:rows, :], in1=sc32[sb_][0:rows, :], op=ALU.add)),
             reads=[b_sc[sa], b_sc[sb_]], writes=out_bufs)

    def fm_proj(s, bslot, lhs_cols, M, tt):
        p1, bp1 = ps()
        c0 = lhs_cols

        def mm(e):
            for kc in range(8):
                ins = e.matmul(p1[0:M, :], lhsT=wsl[s][:, kc, c0:c0 + M], rhs=hT[:, kc, tt * 512:(tt + 1) * 512], start=(kc == 0), stop=(kc == 7))
            return ins
        S.op("pe", mm, reads=[bslot] + BH(tt * 4, (tt + 1) * 4), writes=[bp1])
        return p1, bp1

    def tm_proj(s, bslot, c0, N, t):
        p1, bp1 = ps()

        def mm(e):
            for kc in range(8):
                ins = e.matmul(p1[:, 0:N], lhsT=hT[:, kc, t * 128:(t + 1) * 128], rhs=wsl[s][:, kc, c0:c0 + N], start=(kc == 0), stop=(kc == 7))
            return ins
        S.op("pe", mm, reads=[bslot] + BH(t, t + 1), writes=[bp1])
        return p1, bp1

    def load_w_cols(src, c0, n, dcol=0, s=None):
        if s is None:
            s = wslot()
        S.dma("pool", (lambda e: e.dma_start(out=wsl[s][:, :, dcol:dcol + n], in_=src[:, c0:c0 + n].rearrange("(kc p) n -> p kc n", p=128))),
              key="ws%d" % s, writes=[b_ws[s]])
        return s

    def lat_norm_T(p1, bp1, c0, lncol0, t, out_fn, out_bufs, tm_out=None):
        ci = rot("ss2", 16)
        ssc, rsc, bssc, brsc = ss2[:, ci:ci + 1], rs2[:, ci:ci + 1], b_ss2c[ci], b_rs2c[ci]
        S.op("act", (lambda e: e.activation(out=junk[:, 0:256], in_=p1[:, c0:c0 + 256], func=AF.Square, accum_out=ssc)),
             reads=[bp1], writes=[b_junk, bssc])
        S.op("act", lambda e: e.activation(out=rsc, in_=ssc, func=AF.Sqrt, scale=1.0 / 256, bias=epsc[:]), reads=[bssc, b_eps], writes=[brsc])
        S.op("dve", lambda e: e.reciprocal(out=rsc, in_=rsc), reads=[brsc], writes=[brsc])
        xi = rot("xn", 2)
        S.op("act", (lambda e: e.activation(out=xn[xi][:, 0:256], in_=p1[:, c0:c0 + 256], func=AF.Copy, scale=rsc)), reads=[bp1, brsc], writes=[b_xn[xi]])
        if tm_out is not None:
            tm_out(rsc, brsc, b_xn[xi])

        def stage_b():
            p2, bp2 = ps()
            p2b = p2[:].bitcast(BF16)

            def tr(e):
                for k2 in range(2):
                    ins = e.transpose(out=p2b[:, k2 * 128:(k2 + 1) * 128], in_=xn[xi][:, k2 * 128:(k2 + 1) * 128], identity=identb[:])
                return ins
            S.op("pe", tr, reads=[b_xn[xi], b_idb], writes=[bp2])

            def ev(e):
                for k2 in range(2):
                    ins = e.tensor_scalar(out=out_fn(k2), in0=p2b[:, k2 * 128:(k2 + 1) * 128], scalar1=lncols[:, lncol0 + k2:lncol0 + k2 + 1], scalar2=None, op0=ALU.mult)
                return ins
            S.op("dve", ev, reads=[bp2, b_lncols], writes=out_bufs)
        return stage_b

    def mix_proj(kind, reg, ridx, rope, full, prompt_out):
        k0 = KOFF[reg]
        vt0 = VT0[reg]
        if full:
            for half in range(2):
                s = wslot()
                for gl in range(2):
                    g = 2 * half + gl
                    for kv in range(2):
                        hc = (kv * 4 + g) * 64
                        S.dma("pool", (lambda e, s=s, gl=gl, kv=kv, hc=hc: e.dma_start(out=wsl[s][:, :, gl * 128 + kv * 64:gl * 128 + (kv + 1) * 64],
                                                                                   in_=w_in[:, hc:hc + 64].rearrange("(kc p) d -> p kc d", p=128))),
                              key="ws%d" % s, writes=[b_ws[s]])
                for gl in range(2):
                    g = 2 * half + gl
                    for tt in range(2):
                        p1, bp1 = fm_proj(s, b_ws[s], gl * 128, 128, tt)
                        rope_evac(p1, bp1, 128, A_SCALE, ridx, tt * 512, QaT[:, g, tt * 512:(tt + 1) * 512], [b_qa[g]], rope)
        s = load_w_cols(w_in, 512, 256)
        for tt in range(2):
            p1, bp1 = fm_proj(s, b_ws[s], 0, 128, tt)
            rope_evac(p1, bp1, 128, 1.0, ridx, tt * 512, KaT[:, k0 + tt * 512:k0 + (tt + 1) * 512], [b_ka[reg]], rope)
        for t in range(8):
            p1, bp1 = tm_proj(s, b_ws[s], 0, 256, t)
            S.op("act", (lambda e, p1=p1, t=t: e.activation(out=Va[:, vt0 + t, :, 0:64], in_=p1[:, 128:256].rearrange("p (k d) -> p k d", k=2), func=AF.Copy)),
                 reads=[bp1], writes=[b_va[reg]])
            if prompt_out and (PO & 1):
                si = rot("sc", NSC)
                S.op("dve", (lambda e, p1=p1, si=si: e.tensor_copy(out=sc32[si][:, 0:256], in_=p1[:, 0:256])), reads=[bp1, b_va[reg]], writes=[b_sc[si]])
                if PO & 8:
                    continue
                if PO & 16:
                    o = S.dma("sp", (lambda e, si=si, t=t: e.dma_start(out=y[1, t * 128:(t + 1) * 128, 0:256], in_=sc32[si][:, 0:256])), key="osc%d" % si, reads=[b_sc[si]])
                    out_ops.append(o)
                    continue
                o = S.dma("sp", (lambda e, si=si, t=t: e.dma_start(out=nk[t * 128:(t + 1) * 128, :], in_=sc32[si][:, 0:128])), key="osc%d" % si, reads=[b_sc[si]])
                out_ops.append(o)
                o = S.dma("sp", (lambda e, si=si, t=t: e.dma_start(out=nv[t * 128:(t + 1) * 128, :], in_=sc32[si][:, 128:256])), key="osc%d" % si, reads=[b_sc[si]])
                out_ops.append(o)
        if full:
            s = load_w_cols(w_in, 768, 256)
            pend = None
            for t in range(8):
                p1, bp1 = tm_proj(s, b_ws[s], 0, 256, t)
                stb = lat_norm_T(p1, bp1, 0, 0, t, (lambda k2, t=t: qlT[:, k2, t * 128:(t + 1) * 128]), [b_ql[t]])
                if pend is not None:
                    pend()
                pend = stb
            pend()
        s = load_w_cols(w_in, 1024, 256)
        pend = None
        for t in range(8):
            p1, bp1 = tm_proj(s, b_ws[s], 0, 256, t)
            tm_out = None
            if prompt_out and (PO & 2):
                def tm_out(rs_ap, brs, bxn, p1=p1, bp1=bp1, t=t):
                    si = rot("sc", NSC)
                    S.op("dve", (lambda e: e.scalar_tensor_tensor(out=sc32[si][:, 0:256], in0=p1[:, 0:256], scalar=rs_ap, in1=kvlnbc[:], op0=ALU.mult, op1=ALU.mult)),
                         reads=[bp1, brs, b_kvln, bxn], writes=[b_sc[si]])
                    o = S.dma("sp", (lambda e: e.dma_start(out=nckv[t * 128:(t + 1) * 128, :], in_=sc32[si][:, 0:256])), key="osc%d" % si, reads=[b_sc[si]])
                    out_ops.append(o)
            stb = lat_norm_T(p1, bp1, 0, 2, t, (lambda k2, t=t: CkvT[:, k2, k0 + t * 128:k0 + (t + 1) * 128]), [b_ckv[reg]], tm_out=tm_out)
            if pend is not None:
                pend()
            pend = stb
        pend()
        s = load_w_cols(w_in, 1216, 96)
        for tt in range(2):
            p1, bp1 = fm_proj(s, b_ws[s], 0, 96, tt)
            if rope:
                sa = rot("sc", NSC)
                rope_evac(p1, bp1, 96, 1.0, ridx, tt * 512, sc32[sa][0:96, :], [b_sc[sa]], True)
                S.op("act", (lambda e, sa=sa, tt=tt: e.activation(out=KrT[64:96, k0 + tt * 512:k0 + (tt + 1) * 512], in_=sc32[sa][64:96, :], func=AF.Copy)),
                     reads=[b_sc[sa]], writes=[b_kr[reg]])
            else:
                S.op("act", (lambda e, p1=p1, tt=tt: e.activation(out=KrT[64:96, k0 + tt * 512:k0 + (tt + 1) * 512], in_=p1[64:96, :], func=AF.Copy)),
                     reads=[bp1], writes=[b_kr[reg]])
        if prompt_out and (PO & 4):
            for t in range(8):
                p1, bp1 = tm_proj(s, b_ws[s], 64, 32, t)
                si = rot("sc", NSC)
                S.op("dve", (lambda e, p1=p1, si=si: e.tensor_copy(out=sc32[si][:, 0:32], in_=p1[:, 0:32])), reads=[bp1], writes=[b_sc[si]])
                o = S.dma("sp", (lambda e, si=si, t=t: e.dma_start(out=nkr[t * 128:(t + 1) * 128, :], in_=sc32[si][:, 0:32])), key="osc%d" % si, reads=[b_sc[si]])
                out_ops.append(o)

    def load_ctx():
        S.op("pool", lambda e: e.memset(ctm[:], 0.0), writes=[b_ctm])
        S.dma("pool", lambda e: e.dma_start(out=ctm[:, :, 0:128], in_=ck.rearrange("(n p) c -> p n c", p=128)), key="ctm", writes=[b_ctm])
        S.dma("pool", lambda e: e.dma_start(out=ctm[:, :, 320:352], in_=ckr.rearrange("(n p) c -> p n c", p=128)), key="ctm", writes=[b_ctm])
        for kv in range(2):
            S.dma("pool", (lambda e, kv=kv: e.dma_start(out=Va[:, 0:4, kv, 0:64], in_=cv.rearrange("(n p) (k d) -> p n k d", p=128, k=2)[:, :, kv, :])), key="vactx", writes=[b_va["ctx"]])
        for n in range(4):
            p1, bp1 = ps()
            p1b = p1[:].bitcast(BF16)
            S.op("pe", (lambda e, n=n, p1b=p1b: e.transpose(out=p1b[:, 0:128], in_=ctm[:, n, 0:128], identity=identb[:])), reads=[b_ctm, b_idb], writes=[bp1])
            S.op("act", (lambda e, n=n, p1b=p1b: e.activation(out=KaT[:, K_CTX + n * 128:K_CTX + (n + 1) * 128], in_=p1b[:, 0:128], func=AF.Copy)), reads=[bp1], writes=[b_ka["ctx"]])
            p2, bp2 = ps()
            p2b = p2[:].bitcast(BF16)
            S.op("pe", (lambda e, n=n, p2b=p2b: e.transpose(out=p2b[0:96, 0:128], in_=ctm[:, n, 256:352], identity=identb[:])), reads=[b_ctm, b_idb], writes=[bp2])
            S.op("act", (lambda e, n=n, p2b=p2b: e.activation(out=KrT[64:96, K_CTX + n * 128:K_CTX + (n + 1) * 128], in_=p2b[64:96, 0:128], func=AF.Copy)), reads=[bp2], writes=[b_kr["ctx"]])
        S.dma("pool", lambda e: e.dma_start(out=ctm[:, :, 0:256], in_=cckv.rearrange("(n p) c -> p n c", p=128)), key="ctm", writes=[b_ctm])
        for n in range(4):
            p1, bp1 = ps()
            p1b = p1[:].bitcast(BF16)

            def tr(e, n=n, p1b=p1b):
                for k2 in range(2):
                    ins = e.transpose(out=p1b[:, k2 * 128:(k2 + 1) * 128], in_=ctm[:, n, k2 * 128:(k2 + 1) * 128], identity=identb[:])
                return ins
            S.op("pe", tr, reads=[b_ctm, b_idb], writes=[bp1])
            S.op("act", (lambda e, n=n, p1b=p1b: e.activation(out=CkvT[:, :, K_CTX + n * 128:K_CTX + (n + 1) * 128], in_=p1b[:, 0:256].rearrange("p (k t) -> p k t", k=2), func=AF.Copy)),
                 reads=[bp1], writes=[b_ckv["ctx"]])

    LOOK = 3

    class Unit:
        pass

    def run_attention_stream(front):
        from collections import deque
        backq = deque()
        normq = []

        def emit_st(u, k):
            k_ap, k_bufs, v_ap, v_bufs, mask, flag = u.tiles[k]
            N = u.N
            pS, bpS = ps()
            S.op("pe", (lambda e: e.matmul(pS[:, 0:N], lhsT=k_ap, rhs=u.q_ap, start=True, stop=True)), reads=list(k_bufs) + list(u.q_bufs), writes=[bpS])
            pi = rot("pt", NPT)
            S.op("act", (lambda e: e.activation(out=PT[pi][:, 0:N], in_=pS[:, 0:N], func=AF.Exp)), reads=[bpS], writes=[b_pt[pi]])
            if mask == "prev":
                S.op("pool", (lambda e: e.affine_select(out=PT[pi][:].rearrange("p (g q) -> p g q", g=4), in_=PT[pi][:].rearrange("p (g q) -> p g q", g=4),
                                                      pattern=[[0, 4], [-1, 128]], compare_op=ALU.is_ge, fill=0.0, base=0, channel_multiplier=1)),
                     reads=[b_pt[pi]], writes=[b_pt[pi]])
            elif mask == "next":
                S.op("pool", (lambda e: e.affine_select(out=PT[pi][:].rearrange("p (g q) -> p g q", g=4), in_=PT[pi][:].rearrange("p (g q) -> p g q", g=4),
                                                      pattern=[[0, 4], [1, 128]], compare_op=ALU.is_ge, fill=0.0, base=0, channel_multiplier=-1)),
                     reads=[b_pt[pi]], writes=[b_pt[pi]])
            if flag is not None:
                S.op("pool", (lambda e: e.tensor_scalar(out=PT[pi][:, 0:N], in0=PT[pi][:, 0:N], scalar1=flags[:, flag:flag + 1], scalar2=None, op0=ALU.mult)),
                     reads=[b_pt[pi], b_flags], writes=[b_pt[pi]])
            return pi

        def emit_pv(u, k, pi):
            k_ap, k_bufs, v_ap, v_bufs, mask, flag = u.tiles[k]
            N = u.N
            nt = len(u.tiles)
            if k == 0:
                u.pO, u.bpO = ps(reserve=True)
            pO = u.pO
            S.op("pe", (lambda e: e.matmul(pO[0:66, 0:N], lhsT=v_ap, rhs=PT[pi][:, 0:N], start=(k == 0), stop=(k == nt - 1))),
                 reads=[b_pt[pi]] + list(v_bufs), writes=[u.bpO])
            if u.use_pd:
                if k == 0:
                    u.pD, u.bpD = ps(reserve=True)
                pD = u.pD
                S.op("pe", (lambda e: e.matmul(pD[0:64, 0:N], lhsT=onesb[:], rhs=PT[pi][:, 0:N], start=(k == 0), stop=(k == nt - 1))),
                     reads=[b_pt[pi], b_onesb], writes=[u.bpD])

        def norm_pd(u):
            pO, bpO, pD, bpD, N = u.pO, u.bpO, u.pD, u.bpD, u.N
            si = rot("sc", NSC)
            if u.sink_cols is None:
                S.op("act", (lambda e: e.activation(out=sc32[si][0:64, 0:N], in_=pD[0:64, 0:N], func=AF.Ln)), reads=[bpD], writes=[b_sc[si]])
            else:
                def lnsink(e):
                    for g in range(4):
                        ins = e.activation(out=sc32[si][0:64, g * 128:(g + 1) * 128], in_=pD[0:64, g * 128:(g + 1) * 128], func=AF.Ln,
                                           bias=esinkb[:, u.sink_cols + g:u.sink_cols + g + 1])
                    return ins
                S.op("act", lnsink, reads=[bpD, b_esinkb], writes=[b_sc[si]])
            S.op("act", (lambda e: e.activation(out=sc32[si][0:64, 0:N], in_=sc32[si][0:64, 0:N], func=AF.Exp, scale=-1.0)), reads=[b_sc[si]], writes=[b_sc[si]])
            S.op("dve", (lambda e: e.tensor_tensor(out=u.out_ap, in0=u.view(pO[0:64, 0:N]), in1=u.view(sc32[si][0:64, 0:N]), op=ALU.mult)),
                 reads=[bpO, b_sc[si]], writes=u.out_bufs)
            ps_release(pO)
            ps_release(pD)

        def norm_a(u):
            pO, bpO, N = u.pO, u.bpO, u.N
            ri = rot("rden", 2)
            u.ri = ri
            rden, b_rden = rden2[:, ri, :], b_rden2[ri]
            if u.sink_cols is None:
                S.op("dve", (lambda e: e.tensor_copy(out=rden[64:65, 0:N], in_=pO[64:65, 0:N])), reads=[bpO], writes=[b_rden])
            else:
                def addsink(e):
                    for g in range(4):
                        ins = e.tensor_scalar(out=rden[64:65, g * 128:(g + 1) * 128], in0=pO[64:65, g * 128:(g + 1) * 128],
                                              scalar1=esink[64:65, u.sink_cols + g:u.sink_cols + g + 1], scalar2=None, op0=ALU.add)
                    return ins
                S.op("dve", addsink, reads=[bpO, b_esink], writes=[b_rden])

        def norm_b(u):
            pO, bpO, N = u.pO, u.bpO, u.N
            rden, b_rden = rden2[:, u.ri, :], b_rden2[u.ri]
            pB, bpB = ps()
            S.op("pe", (lambda e: e.matmul(pB[0:64, 0:N], lhsT=ones32[64:65, 0:64], rhs=rden[64:65, 0:N], start=True, stop=True)), reads=[b_ones, b_rden], writes=[bpB])
            si = rot("sc", NSC)
            S.op("act", (lambda e: e.activation(out=sc32[si][0:64, 0:N], in_=pB[0:64, 0:N], func=AF.Ln)), reads=[bpB], writes=[b_sc[si]])
            S.op("act", (lambda e: e.activation(out=sc32[si][0:64, 0:N], in_=sc32[si][0:64, 0:N], func=AF.Exp, scale=-1.0)), reads=[b_sc[si]], writes=[b_sc[si]])
            S.op("dve", (lambda e: e.tensor_tensor(out=u.out_ap, in0=u.view(pO[0:64, 0:N]), in1=u.view(sc32[si][0:64, 0:N]), op=ALU.mult)),
                 reads=[bpO, b_sc[si]], writes=u.out_bufs)
            ps_release(pO)

        def do_back():
            u, k, pi = backq.popleft()
            emit_pv(u, k, pi)
            for ent in list(normq):
                ent[1] -= 1
                if ent[1] <= 0:
                    norm_b(ent[0])
                    normq.remove(ent)
            if k == len(u.tiles) - 1 and u.use_pd:
                norm_pd(u)
            elif k == len(u.tiles) - 1:
                while len(normq) > 1:
                    norm_b(normq[0][0])
                    normq.pop(0)
                norm_a(u)
                normq.append([u, 3])

        for ent in front:
            if ent[0] == "call":
                ent[1]()
                continue
            _, u, k = ent
            pi = emit_st(u, k)
            backq.append((u, k, pi))
            if len(backq) > LOOK:
                do_back()
        while backq:
            do_back()
        for ent in normq:
            norm_b(ent[0])

    def attention(kind):
        sample = (kind == "S")
        v4 = lambda ap: ap.rearrange("p (g q) -> p g q", g=4)
        ident_v = lambda ap: ap
        front = []

        def add_unit(q_ap, q_bufs, N, tiles, sink_cols, out_ap, out_bufs, view):
            u = Unit()
            u.q_ap, u.q_bufs, u.N, u.tiles, u.sink_cols, u.out_ap, u.out_bufs, u.view = q_ap, q_bufs, N, tiles, sink_cols, out_ap, out_bufs, view
            u.use_pd = not sample
            for k in range(len(tiles)):
                front.append(("st", u, k))
            return u

        s_uq = wslot()
        uq = wsl[s_uq][:].rearrange("p k c -> p (k c)")[:, 0:1536].rearrange("p (k c) -> p k c", k=2)
        S.dma("pool", lambda e: e.dma_start(out=uq, in_=w_uq.rearrange("(k p) c -> p k c", p=128)), key="ws%d" % s_uq, writes=[b_ws[s_uq]])
        s_ukv = wslot()
        ukv = wsl[s_ukv][:].rearrange("p k c -> p (k c)").rearrange("p (k c) -> p k c", k=2)
        S.dma("pool", lambda e: e.dma_start(out=ukv, in_=w_ukv.rearrange("(k p) c -> p k c", p=128)), key="ws%d" % s_ukv, writes=[b_ws[s_ukv]])
        if sample:
            regs = [("ctx", 4), ("own", 8), ("oth", 8)]
            nkt = 20
            kbase = 0
        else:
            regs = [("own", 8)]
            nkt = 8
            kbase = K_OWN
        all_kr = [b_kr[r] for r, _ in regs]
        all_ckv = [b_ckv[r] for r, _ in regs]
        vt_base = 0 if sample else 4

        def prep_rope_rows():
            for i in range(2):
                S.op("act", (lambda e, i=i: e.activation(out=KbT[i][64:96, kbase:kbase + nkt * 128], in_=KrT[64:96, kbase:kbase + nkt * 128], func=AF.Copy)),
                     reads=all_kr, writes=[b_kbr[i]])

        def prep(h):
            kb = h % 2
            for c in range(nkt // 4):
                p1, bp1 = ps()
                col0 = kbase + c * 512

                def mmk(e, p1=p1, col0=col0):
                    for k2 in range(2):
                        ins = e.matmul(p1[0:64, :], lhsT=ukv[:, k2, h * 128:h * 128 + 64], rhs=CkvT[:, k2, col0:col0 + 512], start=(k2 == 0), stop=(k2 == 1))
                    return ins
                S.op("pe", mmk, reads=[b_ws[s_ukv]] + all_ckv, writes=[bp1])
                if c % 2 == 0:
                    S.op("dve", (lambda e, p1=p1, col0=col0: e.tensor_copy(out=KbT[kb][0:64, col0:col0 + 512], in_=p1[0:64, :])), reads=[bp1], writes=[b_kb[kb]])
                else:
                    S.op("pool" if False else "dve", (lambda e, p1=p1, col0=col0: e.tensor_copy(out=KbT[kb][0:64, col0:col0 + 512], in_=p1[0:64, :])), reads=[bp1], writes=[b_kb[kb]])
            for c0 in range(0, nkt, 8):
                nn = min(8, nkt - c0)
                p1, bp1 = ps()

                def mmv(e, p1=p1, c0=c0, nn=nn):
                    for i in range(nn):
                        col0 = kbase + (c0 + i) * 128
                        for k2 in range(2):
                            ins = e.matmul(p1[:, i * 64:(i + 1) * 64], lhsT=CkvT[:, k2, col0:col0 + 128], rhs=ukv[:, k2, h * 128 + 64:h * 128 + 128], start=(k2 == 0), stop=(k2 == 1))
                    return ins
                S.op("pe", mmv, reads=[b_ws[s_ukv]] + all_ckv, writes=[bp1])
                S.op("dve", (lambda e, p1=p1, c0=c0, nn=nn: e.tensor_copy(out=Vb[kb][:, vt_base + c0:vt_base + c0 + nn, 0:64], in_=p1[:, 0:nn * 64].rearrange("p (n d) -> p n d", d=64))),
                     reads=[bp1], writes=[b_vb[kb]])
            for tt in range(2):
                p1, bp1 = ps()

                def mmq(e, p1=p1, tt=tt):
                    for k2 in range(2):
                        ins = e.matmul(p1[0:96, :], lhsT=uq[:, k2, h * 96:(h + 1) * 96], rhs=qlT[:, k2, tt * 512:(tt + 1) * 512], start=(k2 == 0), stop=(k2 == 1))
                    return ins
                S.op("pe", mmq, reads=[b_ws[s_uq]] + b_ql[tt * 4:(tt + 1) * 4], writes=[bp1])
                rope_evac(p1, bp1, 96, B_SCALE, 0, tt * 512, QbT[kb][0:96, tt * 512:(tt + 1) * 512], [b_qb[kb]], sample)

        a_units = 0
        if sample:
            for j in (1, 2, 3, 4, 5, 6, 0, 7):
                for kv in range(2):
                    ks = slice(kv * 64, (kv + 1) * 64)
                    tiles = []
                    for n in range(4):
                        tiles.append((KaT[ks, K_CTX + n * 128:K_CTX + (n + 1) * 128], [b_ka["ctx"]], Va[:, n, kv, 0:66], [b_va["ctx"], b_va1], None, None))
                    tiles.append((KaT[ks, K_OWN + j * 128:K_OWN + (j + 1) * 128], [b_ka["own"]], Va[:, 4 + j, kv, 0:66], [b_va["own"], b_va1], None, None))
                    if j > 0:
                        tiles.append((KaT[ks, K_OWN + (j - 1) * 128:K_OWN + j * 128], [b_ka["own"]], Va[:, 4 + j - 1, kv, 0:66], [b_va["own"], b_va1], "prev", None))
                    else:
                        tiles.append((KaT[ks, K_OTH + 7 * 128:K_OTH + 8 * 128], [b_ka["oth"]], Va[:, 12 + 7, kv, 0:66], [b_va["oth"], b_va1], "prev", 0))
                    if j < 7:
                        tiles.append((KaT[ks, K_OWN + (j + 1) * 128:K_OWN + (j + 2) * 128], [b_ka["own"]], Va[:, 4 + j + 1, kv, 0:66], [b_va["own"], b_va1], "next", None))
                    else:
                        tiles.append((KaT[ks, K_OTH:K_OTH + 128], [b_ka["oth"]], Va[:, 12, kv, 0:66], [b_va["oth"], b_va1], "next", 1))
                    add_unit(QaT[ks, :, j * 128:(j + 1) * 128], b_qa, 512, tiles, kv * 4,
                             oaT[0:64, kv * 4:(kv + 1) * 4, j * 128:(j + 1) * 128], b_oa[kv * 4:(kv + 1) * 4], v4)
                    a_units += 1
                    if a_units == 8:
                        front.append(("call", prep_rope_rows))
                        front.append(("call", (lambda: prep(0))))
        else:
            for sq in range(4):
                for qh in range(2):
                    for kv in range(2):
                        ks = slice(kv * 64, (kv + 1) * 64)
                        q0 = sq * 256 + qh * 128
                        tiles = []
                        for kt in range(2):
                            kk = sq * 2 + kt
                            tiles.append((KaT[ks, K_OWN + kk * 128:K_OWN + (kk + 1) * 128], [b_ka["own"]], Va[:, 4 + kk, kv, 0:66], [b_va["own"], b_va1], None, None))
                        add_unit(QaT[ks, :, q0:q0 + 128], b_qa, 512, tiles, kv * 4,
                                 oaT[0:64, kv * 4:(kv + 1) * 4, q0:q0 + 128], b_oa[kv * 4:(kv + 1) * 4], v4)
                        a_units += 1
                        if a_units == DBG.get("pcall", 8):
                            front.append(("call", prep_rope_rows))
                            front.append(("call", (lambda: prep(0))))
        for h in range(8):
            kb = h % 2
            start_idx = len(front)
            if sample:
                for tt in range(2):
                    tiles = []
                    for n in range(20):
                        tiles.append((KbT[kb][0:96, n * 128:(n + 1) * 128], [b_kb[kb], b_kbr[kb]], Vb[kb][:, n, 0:66], [b_vb[kb], b_vb1[kb]], None, None))
                    add_unit(QbT[kb][0:96, tt * 512:(tt + 1) * 512], [b_qb[kb]], 512, tiles, None, obT[0:64, h, tt * 512:(tt + 1) * 512], [b_ob[h]], ident_v)
            else:
                for sq in range(4):
                    tiles = []
                    for kt in range(2):
                        kk = sq * 2 + kt
                        tiles.append((KbT[kb][0:96, K_OWN + kk * 128:K_OWN + (kk + 1) * 128], [b_kb[kb], b_kbr[kb]], Vb[kb][:, 4 + kk, 0:66], [b_vb[kb], b_vb1[kb]], None, None))
                    add_unit(QbT[kb][0:96, sq * 256:(sq + 1) * 256], [b_qb[kb]], 256, tiles, None, obT[0:64, h, sq * 256:(sq + 1) * 256], [b_ob[h]], ident_v)
            if h + 1 < 8:
                front.insert(start_idx + LOOK + 2, ("call", (lambda h=h: prep(h + 1))))
        run_attention_stream(front)

    def merge(m):
        for i in range(4):
            S.dma("sp", (lambda e, i=i: e.dma_start(out=oaT2[64:128, 2 * i, :], in_=oaT2[0:64, 2 * i + 1, :])), key="pa%d" % i, reads=[b_oa[2 * i + 1]], writes=[b_oa2[i]])
            S.dma("sp", (lambda e, i=i: e.dma_start(out=obT2[64:128, 2 * i, :], in_=obT2[0:64, 2 * i + 1, :])), key="pb%d" % i, reads=[b_ob[2 * i + 1]], writes=[b_ob2[i]])

        def load_c(c):
            sg = wslot()
            S.dma("pool", (lambda e: e.dma_start(out=wsl[sg][:, :, 0:128], in_=w_in[:, 1312 + c * 128:1312 + (c + 1) * 128].rearrange("(kc p) n -> p kc n", p=128))),
                  key="ws%d" % sg, writes=[b_ws[sg]])
            S.dma("pool", (lambda e: e.dma_start(out=wsl[sg][:, :, 128:256], in_=w_in[:, 2336 + c * 128:2336 + (c + 1) * 128].rearrange("(kc p) n -> p kc n", p=128))),
                  key="ws%d" % sg, writes=[b_ws[sg]])
            so = wslot()
            S.dma("pool", (lambda e: e.dma_start(out=wsl[so][:, 0:4, 0:128], in_=w_oa.rearrange("(hp p) n -> p hp n", p=128)[:, :, c * 128:(c + 1) * 128])),
                  key="ws%d" % so, writes=[b_ws[so]])
            S.dma("pool", (lambda e: e.dma_start(out=wsl[so][:, 0:4, 128:256], in_=w_ob.rearrange("(hp p) n -> p hp n", p=128)[:, :, c * 128:(c + 1) * 128])),
                  key="ws%d" % so, writes=[b_ws[so]])
            return sg, so
        nxt = load_c(0)
        for c in range(8):
            sg, so = nxt
            if c + 1 < 8:
                nxt = load_c(c + 1)
            for tt in range(2):
                tsl = slice(tt * 512, (tt + 1) * 512)
                res = []
                for br in range(2):
                    pg, bpg = ps()

                    def mmg(e, pg=pg, br=br, sg=sg, tsl=tsl):
                        for kc in range(8):
                            ins = e.matmul(pg[:], lhsT=wsl[sg][:, kc, br * 128:(br + 1) * 128], rhs=hT[:, kc, tsl], start=(kc == 0), stop=(kc == 7))
                        return ins
                    S.op("pe", mmg, reads=[b_ws[sg]] + BH(tt * 4, (tt + 1) * 4), writes=[bpg])
                    pp, bpp = ps()
                    oT, bo = (oaT2, b_oa + b_oa2) if br == 0 else (obT2, b_ob + b_ob2)

                    def mmo(e, pp=pp, br=br, so=so, oT=oT, tsl=tsl):
                        for hp in range(4):
                            ins = e.matmul(pp[:], lhsT=wsl[so][:, hp, br * 128:(br + 1) * 128], rhs=oT[:, 2 * hp, tsl], start=(hp == 0), stop=(hp == 3))
                        return ins
                    S.op("pe", mmo, reads=[b_ws[so]] + bo, writes=[bpp])
                    si = rot("sc", NSC)
                    S.op("act", (lambda e, pg=pg, si=si: e.activation(out=sc32[si][:], in_=pg[:], func=AF.Sigmoid)), reads=[bpg], writes=[b_sc[si]])
                    S.op("dve", (lambda e, pp=pp, si=si: e.tensor_tensor(out=sc32[si][:], in0=sc32[si][:], in1=pp[:], op=ALU.mult)), reads=[b_sc[si], bpp], writes=[b_sc[si]])
                    res.append(si)
                S.op("dve", (lambda e, res=res, c=c, tsl=tsl: e.tensor_tensor(out=mT[:, c, tsl], in0=sc32[res[0]][:], in1=sc32[res[1]][:], op=ALU.add)),
                     reads=[b_sc[res[0]], b_sc[res[1]]], writes=[b_m[c][tt]])
        nxt = load_w_cols(w_out, 0, 256)
        for cq in range(4):
            s = nxt
            if cq + 1 < 4:
                nxt = load_w_cols(w_out, (cq + 1) * 256, 256)
            for t in range(8):
                pd, bpd = ps()

                def mmw(e, pd=pd, s=s, t=t):
                    for kc in range(8):
                        ins = e.matmul(pd[:, 0:256], lhsT=mT[:, kc, t * 128:(t + 1) * 128], rhs=wsl[s][:, kc, 0:256], start=(kc == 0), stop=(kc == 7))
                    return ins
                S.op("pe", mmw, reads=[b_ws[s]] + [b_m[kc][t // 4] for kc in range(8)], writes=[bpd])
                resid_add(t, cq * 256, 256, pd, bpd)

    def exchange_kv():
        snda = snd.ap()
        rcva = rcv.ap()
        b_snd = [Buf("snd%d" % i) for i in range(5)]
        b_rcv = Buf("rcv")
        S.dma("sp", lambda e: e.dma_start(out=snda[:, 0:1024], in_=KaT[:, K_OWN:K_OWN + 1024]), key="xs0", reads=[b_ka["own"]], writes=[b_snd[0]])
        for kv in range(2):
            S.dma("sp", (lambda e, kv=kv: e.dma_start(out=snda[:, 1024:2048].rearrange("p (n k d) -> p n k d", k=2, d=64)[:, :, kv, :], in_=Va[:, 4:12, kv, 0:64])),
                  key="xs%d" % (1 + kv), reads=[b_va["own"]], writes=[b_snd[1 + kv]])
        S.dma("sp", lambda e: e.dma_start(out=snda[:, 2048:4096].rearrange("p (k t) -> p k t", k=2), in_=CkvT[:, :, K_OWN:K_OWN + 1024]), key="xs3", reads=[b_ckv["own"]], writes=[b_snd[3]])
        S.dma("sp", lambda e: e.dma_start(out=kr_snd.ap(), in_=KrT[64:96, K_OWN:K_OWN + 1024]), key="xs4", reads=[b_kr["own"]], writes=[b_snd[4]])
        b_krr = Buf("kr_rcv")
        S.dma("pool", lambda e: e.collective_compute("AllGather", ALU.bypass, replica_groups=[[0, 1], [2, 3], [4, 5], [6, 7]],
                                                     ins=[kr_snd.ap().opt()], outs=[kr_rcv.ap().opt()]),
              key="ag2", reads=[b_snd[4]], writes=[b_krr], inc=1)
        S.dma("pool", lambda e: e.collective_compute("AllGather", ALU.bypass, replica_groups=[[0, 1], [2, 3], [4, 5], [6, 7]],
                                                     ins=[snd.ap().opt()], outs=[rcv.ap().opt()]),
              key="ag", reads=b_snd[0:4], writes=[b_rcv], inc=1)
        S.dma("sp", lambda e: e.dma_start(out=KaT[:, K_OTH + 896:K_OTH + 1024], in_=rcva[0:128, 896:1024]), key="xl0", reads=[b_rcv], writes=[b_ka["oth"]])
        S.dma("sp", lambda e: e.dma_start(out=KaT[:, K_OTH:K_OTH + 128], in_=rcva[128:256, 0:128]), key="xl0", reads=[b_rcv], writes=[b_ka["oth"]])
        S.dma("sp", lambda e: e.dma_start(out=Va[:, 19, :, 0:64], in_=rcva[0:128, 1920:2048].rearrange("p (k d) -> p k d", k=2)), key="xl1", reads=[b_rcv], writes=[b_va["oth"]])
        S.dma("sp", lambda e: e.dma_start(out=Va[:, 12, :, 0:64], in_=rcva[128:256, 1024:1152].rearrange("p (k d) -> p k d", k=2)), key="xl1", reads=[b_rcv], writes=[b_va["oth"]])
        S.dma("sp", lambda e: e.dma_start(out=CkvT[:, :, K_OWN:K_OWN + 1024], in_=rcva[0:128, 2048:4096].rearrange("p (k t) -> p k t", k=2)), key="xl2", reads=[b_rcv], writes=[b_ckv["own"]])
        S.dma("sp", lambda e: e.dma_start(out=CkvT[:, :, K_OTH:K_OTH + 1024], in_=rcva[128:256, 2048:4096].rearrange("p (k t) -> p k t", k=2)), key="xl3", reads=[b_rcv], writes=[b_ckv["oth"]])
        S.dma("sp", lambda e: e.dma_start(out=KrT[64:96, K_OWN:K_OWN + 1024], in_=kr_rcv.ap()[0:32, :]), key="xl4", reads=[b_krr], writes=[b_kr["own"]])
        S.dma("sp", lambda e: e.dma_start(out=KrT[64:96, K_OTH:K_OTH + 1024], in_=kr_rcv.ap()[32:64, :]), key="xl5", reads=[b_krr], writes=[b_kr["oth"]])

    ffn_bufs = [b for row in b_act for b in row] + b_wd
    att1_bufs = b_oa + b_ob + b_kb + b_kbr + b_oa2 + b_ob2
    att2_bufs = b_qa + list(b_ka.values()) + list(b_va.values()) + [b_va1]
    m_bufs = [b for row in b_m for b in row]

    def set_va_ones(t0, t1):
        S.op("pool", (lambda e: e.memset(Va[:, t0:t1, :, 64:66], 1.0)), writes=[b_va1])

    def program(stage):
        for i in range(2):
            S.op("pool", (lambda e, i=i: e.memset(Vb[i][:, :, 64:66], 1.0)), writes=[b_vb1[i]])
        load_x(1)
        S.dma("sp", lambda e: e.dma_start(out=ropA[:], in_=ropeA[0].rearrange("c p t -> p c t")), key="ropA", writes=[b_ropA])
        S.dma("sp", lambda e: e.dma_start(out=ropB[:], in_=ropeB[0].rearrange("c p t -> p c t")), key="ropB", writes=[b_ropB])
        hk = {"n": 0}

        def ada_hook():
            hk["n"] += 1
            ada_chunk()
            if hk["n"] > 8:
                ada_chunk()
        ffn(0, 0, 0, hook=ada_hook)
        while ada_state["cc"] < 36:
            ada_chunk()
        set_va_ones(0, 20)
        load_ctx()
        norm_to_hT(1, 0, gate=(1, 0, False))
        mix_proj("S", "own", 0, True, True, False)
        exchange_kv()
        if stage == 2:
            final_out(0)
            return
        fence(ffn_bufs + att1_bufs)
        attention("S")
        if stage == 3:
            final_out(0)
            return
        fence(att2_bufs + m_bufs)
        merge(0)
        if stage == 4:
            final_out(0)
            return
        fence(att1_bufs + ffn_bufs)
        ffn(1, 2, 0)
        final_out(0)
        if stage == 5:
            return
        load_x(2)
        ffn(0, 0, 1)
        norm_to_hT(1, 1, gate=(1, 1, False))
        fence(m_bufs + att2_bufs)
        set_va_ones(4, 12)
        mix_proj("P", "own", 0, False, True, True)
        if stage == 6:
            final_out(1)
            return
        fence(ffn_bufs + att1_bufs)
        attention("P")
        if stage == 7:
            final_out(1)
            return
        fence(att2_bufs + m_bufs)
        merge(1)
        fence(att1_bufs + ffn_bufs)
        ffn(1, 2, 1)
        final_out(1)


    program(DBG.get("stage", 99))

    S.wait_all("sp", out_ops)
    S.emit(nc, st)
    st.close()
    return nc


_NC = {}


def _col(v, n):
    return np.ascontiguousarray(np.asarray(v, np.float32).reshape(n, 128).T)


def make_in_maps(inp):
    c = _consts()
    f = lambda a: np.ascontiguousarray(np.asarray(a, dtype=np.float32))
    x_prompt, x_sample = f(inp["x_prompt"]), f(inp["x_sample"])
    shared = dict(
        ada_w=f(inp["ada_w"][0]), ada_bc=_col(inp["ada_b"][0], 72),
        ncol=np.concatenate([_col(inp["ffn1_norm"][0], 8), _col(inp["mix_norm"][0], 8), _col(inp["ffn2_norm"][0], 8)], axis=1),
        fnorm=f(inp["final_norm"]).reshape(1, D),
        w_gu1=f(inp["ffn1_w_gu"][0]), w_d1=f(inp["ffn1_w_down"][0]), w_gu2=f(inp["ffn2_w_gu"][0]), w_d2=f(inp["ffn2_w_down"][0]),
        w_in=f(inp["w_in"][0]),
        lncol=np.concatenate([_col(inp["q_lat_norm"][0], 2), _col(inp["kv_lat_norm"][0], 2)], axis=1),
        kvln=f(inp["kv_lat_norm"][0]).reshape(1, 256),
        w_uq=f(inp["w_uq"][0]), w_ukv=f(inp["w_ukv"][0]), w_oa=f(inp["w_o_a"][0]), w_ob=f(inp["w_o_b"][0]), w_out=f(inp["w_out"][0]),
        ident=c["ident"], RA=c["RA"], RB=c["RB"],
    )
    sink65 = np.zeros((65, 8), np.float32)
    sink65[64] = f(inp["attn_sink"][0])
    shared["sink65"] = sink65
    maps = []
    for core in range(8):
        b, h = core // 2, core % 2
        xg = np.stack([x_sample[b, (1 - h) * 1024:(2 - h) * 1024], x_sample[b, h * 1024:(h + 1) * 1024],
                       x_prompt[4 * core:4 * core + 4].reshape(1024, D)], axis=0)
        cvec = np.stack([f(inp["c"])[b], f(inp["c_ctx"])], axis=0)
        cvfm = np.ascontiguousarray(cvec.reshape(2, 8, 128).transpose(2, 1, 0).reshape(128, 16))
        flags = np.zeros((128, 2), np.float32)
        flags[:, 0] = float(h)
        flags[:, 1] = float(1 - h)
        d = dict(shared)
        d.update(
            xg=np.ascontiguousarray(xg), cvfm=cvfm,
            ck=f(inp["cache_attn_k"][b, 0]).reshape(512, 128), cv=f(inp["cache_attn_v"][b, 0]).reshape(512, 128),
            cckv=f(inp["cache_mla_ckv"][b, 0]), ckr=f(inp["cache_mla_krope"][b, 0]),
            ropeA=np.ascontiguousarray(c["ropeA"][[h, 1 - h]]), ropeB=np.ascontiguousarray(c["ropeB"][[h, 1 - h]]),
            flags=flags,
        )
        maps.append(d)
    return maps


def kernel(**inputs):
    dbg = tuple(DBG.get("dbg", ()))
    key = (dbg, DBG.get("stage", 99), DBG.get("po", 7), DBG.get("pcall", 8))
    if key not in _NC:
        _NC[key] = build_nc(dbg)
    nc = _NC[key]
    maps = make_in_maps(inputs)
    res = run_bass_kernel_spmd(nc, maps, core_ids=list(range(8)))
    R = res.results
    DBG["results"] = R
    y_prompt = np.concatenate([R[c]["y"][1].reshape(4, 256, D) for c in range(8)], axis=0)
    y_sample = np.stack([np.concatenate([R[2 * b]["y"][0], R[2 * b + 1]["y"][0]], axis=0) for b in range(4)], axis=0)
    nk = np.concatenate([R[c]["nk"].reshape(4, 1, 256, 2, 64) for c in range(8)], axis=0)
    nv = np.concatenate([R[c]["nv"].reshape(4, 1, 256, 2, 64) for c in range(8)], axis=0)
    nckv = np.concatenate([R[c]["nckv"].reshape(4, 1, 256, 256) for c in range(8)], axis=0)
    nkr = np.concatenate([R[c]["nkr"].reshape(4, 1, 256, 32) for c in range(8)], axis=0)
    return (y_prompt.astype(np.float32), y_sample.astype(np.float32), nk.astype(np.float32), nv.astype(np.float32),
            nckv.astype(np.float32), nkr.astype(np.float32))
```
